# Optimizing a Trainium2 kernel written in Bass

```python
import math
import jax, jax.numpy as jnp
from jax import lax
import numpy as np

D_MODEL = 1024
BATCH = 32
SEQ = 256
DEPTH = 4
DEC_BATCH = 2
DEC_SEQ = 1024
PAST_LEN = 512

GRID_W = 64
N_MIXERS = 3
N_A = (DEPTH + 2) // 3
N_B = (DEPTH + 1) // 3
N_C = DEPTH // 3
N_MOD = 6
EPS = 1e-6
D_RNN = 1280
LRU_BLOCKS = 10
LRU_BLOCK = D_RNN // LRU_BLOCKS
LRU_CONV_W = 4
LRU_CONV_LEFT = 2
LRU_C = 8.0
LRU_A_MIN = 0.9
LRU_A_MAX = 0.999
D_B = 2 * D_MODEL
CHUNK = 128
G_B = 8
N_HEADS_C = 8
HEAD_DIM_C = D_MODEL // (2 * N_HEADS_C)
N_FREQ_AXIS = HEAD_DIM_C // 4
ROPE_BASE = 10000.0
Q_BLOCK = 128
D_FF = 2816
FFN_CONV_W = 3
FFN_CONV_LEFT = 1

kernel_name = 'hybrid_diffusion_rglru_chunkmlp_diffattn_step'


def rms_norm(x, g):
    xf = x.astype(jnp.float32)
    y = xf * lax.rsqrt(jnp.mean(xf * xf, axis=-1, keepdims=True) + EPS)
    return (y * g.astype(jnp.float32)).astype(x.dtype)


def adaln(cond, w, b):
    m = jax.nn.silu(cond) @ w + b
    return jnp.split(m[:, None, :], N_MOD, axis=-1)


def modulate(x, g, shift, scale):
    return rms_norm(x, g) * (1 + scale) + shift


def dw_conv(x, w, b, left):
    K = w.shape[0]
    T = x.shape[1]
    xp = jnp.pad(x, ((0, 0), (left, K - 1 - left), (0, 0)))
    y = b
    for k in range(K):
        y = y + xp[:, k:k + T] * w[k]
    return y


def linear_scan(a, b, h0):
    b = b.at[:, 0].add(a[:, 0] * h0)
    def combine(l, r):
        al, bl = l
        ar, br = r
        return al * ar, ar * bl + br
    _, h = lax.associative_scan(combine, (a, b), axis=1)
    return h


def rglru_mixer(h, h0, w_in, conv_w, conv_b, w_gate, b_gate, lam, w_out):
    gate_br, rec = jnp.split(h @ w_in, 2, axis=-1)
    xc = dw_conv(rec, conv_w, conv_b, LRU_CONV_LEFT)
    B, T, _ = xc.shape
    xb = xc.reshape(B, T, LRU_BLOCKS, LRU_BLOCK)
    gates = jax.nn.sigmoid(jnp.einsum('btnk,dgnkj->dgbtnj', xb, w_gate) + b_gate[:, :, None, None])
    gates = gates.astype(jnp.float32).reshape(2, 2, B, T, D_RNN)
    r, i = gates[:, 0], gates[:, 1]
    log_a = -LRU_C * r * jax.nn.softplus(-lam.astype(jnp.float32))[:, None, None, :]
    a = jnp.exp(log_a)
    bx = jnp.sqrt(-jnp.expm1(2.0 * log_a)) * (i * xc.astype(jnp.float32)[None])
    h0f = h0.astype(jnp.float32)
    hf = linear_scan(a[0], bx[0], h0f[:, 0])
    hb = jnp.flip(linear_scan(jnp.flip(a[1], 1), jnp.flip(bx[1], 1), h0f[:, 1]), 1)
    y = (jax.nn.gelu(gate_br) * (hf + hb).astype(h.dtype)) @ w_out
    final = jnp.stack([hf[:, -1], hb[:, 0]], axis=1).astype(h.dtype)
    return y, final


def chunk_mlp_mixer(h, w_in, b_in, norm_g, w_s, b_s, w_out):
    u, v = jnp.split(jax.nn.gelu(h @ w_in + b_in), 2, axis=-1)
    v = rms_norm(v, norm_g)
    B, T, _ = v.shape
    vr = v.reshape(B, T // CHUNK, CHUNK, G_B, D_B // G_B)
    sv = jnp.einsum('gpq,bcqgk->bcpgk', w_s, vr) + b_s.T[:, :, None]
    return (u * sv.reshape(B, T, D_B)) @ w_out


def axial_rope(T):
    rows = T // GRID_W
    row = jnp.repeat(jnp.arange(rows), GRID_W)
    col = jnp.tile(jnp.arange(GRID_W), rows)
    inv = ROPE_BASE ** (-jnp.arange(N_FREQ_AXIS, dtype=jnp.float32) / N_FREQ_AXIS)
    ang = jnp.concatenate([row[:, None] * inv, col[:, None] * inv], axis=-1)
    return jnp.cos(ang), jnp.sin(ang)


def apply_rope(x, cos, sin):
    half = HEAD_DIM_C // 2
    x1, x2 = x[..., :half], x[..., half:]
    cs = cos[None, :, None, None, :].astype(x.dtype)
    sn = sin[None, :, None, None, :].astype(x.dtype)
    return jnp.concatenate([x1 * cs - x2 * sn, x2 * cs + x1 * sn], axis=-1)


def diff_qkv(h, w_qkv, qk_g, rope):
    B, T, _ = h.shape
    q, k, v = jnp.split(h @ w_qkv, 3, axis=-1)
    q = rms_norm(q.reshape(B, T, N_HEADS_C, 2, HEAD_DIM_C), qk_g[0])
    k = rms_norm(k.reshape(B, T, N_HEADS_C, 2, HEAD_DIM_C), qk_g[1])
    v = v.reshape(B, T, N_HEADS_C, 2 * HEAD_DIM_C)
    if rope is not None:
        q = apply_rope(q, rope[0], rope[1])
        k = apply_rope(k, rope[0], rope[1])
    return q, k, v


def diff_lambda(lam_vec, lam_init):
    lv = lam_vec.astype(jnp.float32)
    return jnp.exp(jnp.sum(lv[0] * lv[1])) - jnp.exp(jnp.sum(lv[2] * lv[3])) + lam_init


def diff_attention(q, k, v, lam, lam_init, subln_g):
    B, Tq = q.shape[0], q.shape[1]
    nb = Tq // Q_BLOCK
    qb = q.reshape(B, nb, Q_BLOCK, N_HEADS_C, 2, HEAD_DIM_C).transpose(1, 0, 2, 3, 4, 5)
    scale = HEAD_DIM_C ** -0.5
    def one_block(qblk):
        s = jnp.einsum('bqhcd,bkhcd->bchqk', qblk, k).astype(jnp.float32) * scale
        p = jax.nn.softmax(s, axis=-1)
        w = p[:, 0] - lam * p[:, 1]
        return jnp.einsum('bhqk,bkhe->bqhe', w.astype(v.dtype), v)
    o = lax.map(one_block, qb)
    o = o.transpose(1, 0, 2, 3, 4).reshape(B, Tq, N_HEADS_C, 2 * HEAD_DIM_C)
    o = rms_norm(o, subln_g) * (1.0 - lam_init)
    return o.reshape(B, Tq, N_HEADS_C * 2 * HEAD_DIM_C)


def conv_ffn(h, w_up, conv_w, conv_b, w_down):
    z = dw_conv(h @ w_up, conv_w, conv_b, FFN_CONV_LEFT)
    g, u = jnp.split(z, 2, axis=-1)
    return (jax.nn.silu(g) * u) @ w_down


def setup_inputs(seed: int = 0) -> dict:
    key = jax.random.key(seed)
    ks = iter(jax.random.split(key, 40))
    f32 = jnp.float32
    def nrm(shape, scale):
        return jax.random.normal(next(ks), shape, f32) * scale
    def gain(shape):
        return 1.0 + nrm(shape, 0.02)
    a0 = jax.random.uniform(next(ks), (N_A, 2, D_RNN), f32, minval=LRU_A_MIN, maxval=LRU_A_MAX)
    a_root = a0 ** (1.0 / LRU_C)
    lru_lambda = jnp.log(a_root) - jnp.log1p(-a_root)
    return {
        'x_prompt': nrm((BATCH, SEQ, D_MODEL), 1.0),
        'x_sample': nrm((DEC_BATCH, DEC_SEQ, D_MODEL), 1.0),
        'state_lru': nrm((DEC_BATCH, N_A, 2, D_RNN), 0.5),
        'cache_k': nrm((DEC_BATCH, N_C, PAST_LEN, N_HEADS_C, 2, HEAD_DIM_C), 1.0),
        'cache_v': nrm((DEC_BATCH, N_C, PAST_LEN, N_HEADS_C, 2 * HEAD_DIM_C), 1.0),
        'c': nrm((DEC_BATCH, D_MODEL), 1.0),
        'c_ctx': nrm((D_MODEL,), 1.0),
        'w_mod': nrm((DEPTH, D_MODEL, N_MOD * D_MODEL), 0.5 * D_MODEL ** -0.5),
        'b_mod': nrm((DEPTH, N_MOD * D_MODEL), 0.02),
        'norm_g': gain((DEPTH, 2, D_MODEL)),
        'lru_w_in': nrm((N_A, D_MODEL, 2 * D_RNN), D_MODEL ** -0.5),
        'lru_conv_w': nrm((N_A, LRU_CONV_W, D_RNN), LRU_CONV_W ** -0.5),
        'lru_conv_b': nrm((N_A, D_RNN), 0.02),
        'lru_w_gate': nrm((N_A, 2, 2, LRU_BLOCKS, LRU_BLOCK, LRU_BLOCK), LRU_BLOCK ** -0.5),
        'lru_b_gate': nrm((N_A, 2, 2, LRU_BLOCKS, LRU_BLOCK), 0.02),
        'lru_lambda': lru_lambda,
        'lru_w_out': nrm((N_A, D_RNN, D_MODEL), D_RNN ** -0.5),
        'cmlp_w_in': nrm((N_B, D_MODEL, 2 * D_B), D_MODEL ** -0.5),
        'cmlp_b_in': nrm((N_B, 2 * D_B), 0.02),
        'cmlp_norm_g': gain((N_B, D_B)),
        'cmlp_w_s': nrm((N_B, G_B, CHUNK, CHUNK), CHUNK ** -0.5),
        'cmlp_b_s': nrm((N_B, G_B, CHUNK), 0.02),
        'cmlp_w_out': nrm((N_B, D_B, D_MODEL), D_B ** -0.5),
        'attn_w_qkv': nrm((N_C, D_MODEL, 3 * N_HEADS_C * 2 * HEAD_DIM_C), D_MODEL ** -0.5),
        'attn_qk_g': gain((N_C, 2, HEAD_DIM_C)),
        'attn_lambda': nrm((N_C, 4, HEAD_DIM_C), 0.1),
        'attn_subln_g': gain((N_C, 2 * HEAD_DIM_C)),
        'attn_w_out': nrm((N_C, N_HEADS_C * 2 * HEAD_DIM_C, D_MODEL), (N_HEADS_C * 2 * HEAD_DIM_C) ** -0.5),
        'ffn_w_up': nrm((DEPTH, D_MODEL, 2 * D_FF), D_MODEL ** -0.5),
        'ffn_conv_w': nrm((DEPTH, FFN_CONV_W, 2 * D_FF), FFN_CONV_W ** -0.5),
        'ffn_conv_b': nrm((DEPTH, 2 * D_FF), 0.02),
        'ffn_w_down': nrm((DEPTH, D_FF, D_MODEL), D_FF ** -0.5),
    }


def reference(x_prompt, x_sample, state_lru, cache_k, cache_v, c, c_ctx, w_mod, b_mod, norm_g,
              lru_w_in, lru_conv_w, lru_conv_b, lru_w_gate, lru_b_gate, lru_lambda, lru_w_out,
              cmlp_w_in, cmlp_b_in, cmlp_norm_g, cmlp_w_s, cmlp_b_s, cmlp_w_out,
              attn_w_qkv, attn_qk_g, attn_lambda, attn_subln_g, attn_w_out,
              ffn_w_up, ffn_conv_w, ffn_conv_b, ffn_w_down):
    xp, xs = x_prompt, x_sample
    rope = axial_rope(xs.shape[1])
    cond_ctx = c_ctx[None, :]
    new_lru, new_k, new_v = [], [], []
    for l in range(DEPTH):
        kind = l % N_MIXERS
        j = l // N_MIXERS
        sh1p, sc1p, g1p, sh2p, sc2p, g2p = adaln(cond_ctx, w_mod[l], b_mod[l])
        sh1s, sc1s, g1s, sh2s, sc2s, g2s = adaln(c, w_mod[l], b_mod[l])
        hp = modulate(xp, norm_g[l, 0], sh1p, sc1p)
        hs = modulate(xs, norm_g[l, 0], sh1s, sc1s)
        if kind == 0:
            lru_args = (lru_w_in[j], lru_conv_w[j], lru_conv_b[j], lru_w_gate[j], lru_b_gate[j],
                        lru_lambda[j], lru_w_out[j])
            h0 = jnp.zeros((xp.shape[0], 2, D_RNN), xp.dtype)
            yp, st = rglru_mixer(hp, h0, *lru_args)
            ys, _ = rglru_mixer(hs, state_lru[:, j], *lru_args)
            new_lru.append(st)
        elif kind == 1:
            cm_args = (cmlp_w_in[j], cmlp_b_in[j], cmlp_norm_g[j], cmlp_w_s[j], cmlp_b_s[j], cmlp_w_out[j])
            yp = chunk_mlp_mixer(hp, *cm_args)
            ys = chunk_mlp_mixer(hs, *cm_args)
        else:
            lam_init = 0.8 - 0.6 * math.exp(-0.3 * l)
            lam = diff_lambda(attn_lambda[j], lam_init)
            qp, kp, vp = diff_qkv(hp, attn_w_qkv[j], attn_qk_g[j], None)
            qs, ks_lat, vs_lat = diff_qkv(hs, attn_w_qkv[j], attn_qk_g[j], rope)
            yp = diff_attention(qp, kp, vp, lam, lam_init, attn_subln_g[j]) @ attn_w_out[j]
            k_all = jnp.concatenate([cache_k[:, j], ks_lat], axis=1)
            v_all = jnp.concatenate([cache_v[:, j], vs_lat], axis=1)
            ys = diff_attention(qs, k_all, v_all, lam, lam_init, attn_subln_g[j]) @ attn_w_out[j]
            new_k.append(kp)
            new_v.append(vp)
        xp = xp + g1p * yp
        xs = xs + g1s * ys
        hp = modulate(xp, norm_g[l, 1], sh2p, sc2p)
        hs = modulate(xs, norm_g[l, 1], sh2s, sc2s)
        xp = xp + g2p * conv_ffn(hp, ffn_w_up[l], ffn_conv_w[l], ffn_conv_b[l], ffn_w_down[l])
        xs = xs + g2s * conv_ffn(hs, ffn_w_up[l], ffn_conv_w[l], ffn_conv_b[l], ffn_w_down[l])
    new_state_lru = jnp.stack(new_lru, axis=1)
    new_cache_k = jnp.stack(new_k, axis=1)
    new_cache_v = jnp.stack(new_v, axis=1)
    return (xp, xs, new_state_lru, new_cache_k, new_cache_v)
```

```python
import contextlib
import math
import numpy as np
import concourse.bass as bass
import concourse.mybir as mybir
from concourse.bass_utils import run_bass_kernel_spmd

F32 = mybir.dt.float32
BF16 = mybir.dt.bfloat16
AF = mybir.ActivationFunctionType
ALU = mybir.AluOpType

N_CORES = 8
D = 1024
NT = 1280
NS = 5
SL = 256
TT = [(0, 512), (512, 512), (1024, 256)]
D_RNN = 1280
D_B = 2048
D_FF = 2816
NI = 22
EPS = 1e-6
N_LAYERS = 4
SAME_ENGINE_SYNC = True
DEBUG_MODE = None
DBG = {}
NEG = -30000.0


class Reg:
    __slots__ = ("last_write", "readers")

    def __init__(self, inherit=()):
        self.last_write = None
        self.readers = list(inherit)


class Instr:
    __slots__ = ("eng", "fn", "deps", "is_dma", "dma_sem", "dma_val", "need_inc", "inc_val")

    def __init__(self, eng, fn, is_dma=False):
        self.eng = eng
        self.fn = fn
        self.deps = set()
        self.is_dma = is_dma
        self.dma_sem = None
        self.dma_val = 0
        self.need_inc = False
        self.inc_val = 0


ENGINES = ["pe", "act", "dve", "pool", "sp"]


def _flat(regs, out):
    for r in regs:
        if r is None:
            continue
        if isinstance(r, Reg):
            out.append(r)
        else:
            _flat(r, out)
    return out


class Prog:
    def __init__(self, nc):
        self.nc = nc
        self.instrs = []
        self.streams = {}

    def _track(self, ins, reads, writes):
        reads = _flat(reads, [])
        writes = _flat(writes, [])
        for r in reads:
            if r.last_write is not None:
                ins.deps.add(r.last_write)
        for r in writes:
            if r.last_write is not None:
                ins.deps.add(r.last_write)
            for rd in r.readers:
                ins.deps.add(rd)
        for r in reads:
            r.readers.append(ins)
        for r in writes:
            r.last_write = ins
            r.readers = []
        ins.deps.discard(ins)

    def op(self, eng, fn, reads=(), writes=()):
        ins = Instr(eng, fn)
        self._track(ins, reads, writes)
        self.instrs.append(ins)
        return ins

    def dma(self, eng, stream, fn, reads=(), writes=()):
        ins = Instr(eng, fn, is_dma=True)
        st = self.streams.setdefault(stream, [0, None])
        st[0] += 16
        ins.dma_sem = stream
        ins.dma_val = st[0]
        if st[1] is not None:
            ins.deps.add(st[1])
        st[1] = ins
        self._track(ins, reads, writes)
        self.instrs.append(ins)
        return ins

    def emit(self, final_wait_eng="sp"):
        nc = self.nc
        for ins in self.instrs:
            for d in ins.deps:
                if not d.is_dma:
                    if d.eng != ins.eng or (SAME_ENGINE_SYNC and d.eng != "pe"):
                        d.need_inc = True
        counts = {e: 0 for e in ENGINES}
        per_eng = {e: [] for e in ENGINES}
        for ins in self.instrs:
            per_eng[ins.eng].append(ins)
            if not ins.is_dma and ins.need_inc:
                counts[ins.eng] += 1
                ins.inc_val = counts[ins.eng]
        with contextlib.ExitStack() as es:
            esem = {e: es.enter_context(nc.semaphore("s_" + e)) for e in ENGINES}
            ssem = {s: es.enter_context(nc.semaphore("d_" + s)) for s in self.streams}
            block = es.enter_context(nc.Block())
            handles = {"pe": block.tensor, "act": block.scalar, "dve": block.vector,
                       "pool": block.gpsimd, "sp": block.sync}

            def make_body(e):
                def body(eng):
                    waited = {}
                    for ins in per_eng[e]:
                        need = {}
                        for d in ins.deps:
                            if d.is_dma:
                                key = ("d", d.dma_sem)
                                val = d.dma_val
                            else:
                                if d.eng == e and (e == "pe" or not SAME_ENGINE_SYNC):
                                    continue
                                key = ("e", d.eng)
                                val = d.inc_val
                            if val > need.get(key, 0):
                                need[key] = val
                        for key, val in need.items():
                            if waited.get(key, 0) >= val:
                                continue
                            waited[key] = val
                            sem = ssem[key[1]] if key[0] == "d" else esem[key[1]]
                            eng.wait_ge(sem, val)
                        bi = ins.fn(eng)
                        if ins.is_dma:
                            bi.then_inc(ssem[ins.dma_sem], 16)
                        elif ins.need_inc:
                            bi.then_inc(esem[e], 1)
                    if e == final_wait_eng:
                        for s, (cnt, _) in self.streams.items():
                            if waited.get(("d", s), 0) < cnt:
                                eng.wait_ge(ssem[s], cnt)
                return body

            for e in ENGINES:
                if per_eng[e] or e == final_wait_eng:
                    handles[e](make_body(e))
        return {e: len(per_eng[e]) for e in ENGINES}


class Buf:
    def __init__(self, t, off, nbytes, inherit):
        self.t = t
        self.off = off
        self.nbytes = nbytes
        self.inherit = inherit
        self._regs = {}
        self.closed = False

    def reg(self, *key):
        r = self._regs.get(key)
        if r is None:
            r = Reg(self.inherit)
            self._regs[key] = r
        return r

    def all_instrs(self):
        out = list(self.inherit)
        for r in self._regs.values():
            if r.last_write is not None:
                out.append(r.last_write)
            out.extend(r.readers)
        return out


_DT_SIZE = {F32: 4, BF16: 2}


class Arena:
    def __init__(self, nc, base, size, name):
        self.nc, self.base, self.size, self.name = nc, base, size, name
        self.cur = 0
        self.frames = []
        self.live = []
        self.dead = []
        self.n = 0
        self.hi = 0

    def push(self):
        self.frames.append((self.cur, len(self.live)))

    def pop(self):
        cur, nlive = self.frames.pop()
        for b in self.live[nlive:]:
            self.dead.append((b.off, b.off + b.nbytes, b.all_instrs()))
        del self.live[nlive:]
        self.cur = cur

    def alloc(self, shape, dtype):
        n = 1
        for s in shape[1:]:
            n *= s
        nbytes = (n * _DT_SIZE[dtype] + 31) // 32 * 32
        off = self.cur
        assert off + nbytes <= self.size, (self.name, off, nbytes, self.size)
        self.cur += nbytes
        self.hi = max(self.hi, self.cur)
        inherit = []
        keep = []
        for (a, b, ins) in self.dead:
            if a < off + nbytes and off < b:
                inherit.extend(ins)
                if not (off <= a and b <= off + nbytes):
                    keep.append((a, b, ins))
            else:
                keep.append((a, b, ins))
        self.dead = keep
        self.n += 1
        t = self.nc.alloc_sbuf_tensor_at("%s%d" % (self.name, self.n), list(shape), dtype,
                                         offset=self.base + off)
        buf = Buf(t, off, nbytes, list(dict.fromkeys(inherit)))
        buf.abs_off = self.base + off
        self.live.append(buf)
        return buf


def buf_view(nc, buf, shape, dtype, name):
    return nc.alloc_sbuf_tensor_at(name, list(shape), dtype, offset=buf.abs_off)


class Ring:
    def __init__(self, nc, base, size):
        self.nc, self.base, self.size = nc, base, size
        self.cur = 0
        self.bufs = []
        self.n = 0

    def reset(self):
        self.cur = 0

    def alloc(self, shape, dtype):
        n = 1
        for s in shape[1:]:
            n *= s
        nbytes = (n * _DT_SIZE[dtype] + 31) // 32 * 32
        assert nbytes <= self.size
        if self.cur + nbytes > self.size:
            self.cur = 0
        off = self.cur
        self.cur += nbytes
        inherit = []
        keep = []
        for b in self.bufs:
            if b.off < off + nbytes and off < b.off + b.nbytes:
                assert b.closed, "ring overwrite of a live weight buffer"
                inherit.extend(b.all_instrs())
            else:
                keep.append(b)
        self.bufs = keep
        self.n += 1
        t = self.nc.alloc_sbuf_tensor_at("wr%d" % self.n, list(shape), dtype, offset=self.base + off)
        buf = Buf(t, off, nbytes, list(dict.fromkeys(inherit)))
        self.bufs.append(buf)
        return buf


def slots_of(ti):
    return (ti * 2, 2) if ti < 2 else (4, 1)


def hregs(H, c, ti):
    s0, ns = slots_of(ti)
    return [H.reg(c, s0 + i) for i in range(ns)]


class Builder:
    def __init__(self, n_layers):
        self.n_layers = n_layers
        nc = bass.Bass("TRN2", target_bir_lowering=False)
        self.nc = nc
        self.P = Prog(nc)
        self.din = {}
        self.dout = {}
        self.nstream = 0
        total = 207 * 1024
        B0 = 16928
        self.pers = Arena(nc, B0, 76 * 1024, "ps")
        self.ring = Ring(nc, B0 + 76 * 1024, 32 * 1024)
        self.arena = Arena(nc, B0 + 108 * 1024, total - 108 * 1024, "ar")
        self.banks = []
        for i in range(8):
            t = nc.alloc_psum_tensor("bank%d" % i, [128, 512], F32)
            self.banks.append((t, Reg()))
        self.bank_i = 0
        self.pool_i = {}

    def inp(self, name, shape):
        ap = self.nc.dram_tensor(name, list(shape), F32, kind="ExternalInput").ap()
        self.din[name] = ap
        return ap

    def outp(self, name, shape):
        ap = self.nc.dram_tensor(name, list(shape), F32, kind="ExternalOutput").ap()
        self.dout[name] = ap
        return ap

    def ps(self):
        b = self.banks[self.bank_i]
        self.bank_i = (self.bank_i + 1) % 7
        return b

    def ps_pool(self, key, idxs):
        i = self.pool_i.get(key, 0)
        self.pool_i[key] = i + 1
        return self.banks[idxs[i % len(idxs)]]

    def mm(self, out, lhsT, rhs, start, stop, rd, wr):
        self.P.op("pe", lambda e: e.matmul(out, lhsT=lhsT, rhs=rhs, start=start, stop=stop), rd, wr)

    def act(self, out, in_, func, rd, wr, bias=None, scale=None):
        kw = {}
        if bias is not None:
            kw["bias"] = bias
        if scale is not None:
            kw["scale"] = scale
        self.P.op("act", lambda e: e.activation(out=out, in_=in_, func=func, **kw), rd, wr)

    def tt(self, eng, out, in0, in1, op, rd, wr):
        self.P.op(eng, lambda e: e.tensor_tensor(out=out, in0=in0, in1=in1, op=op), rd, wr)

    def ts(self, eng, out, in0, s1, s2, op0, op1, rd, wr):
        if op1 is None:
            self.P.op(eng, lambda e: e.tensor_scalar(out=out, in0=in0, scalar1=s1, scalar2=None, op0=op0), rd, wr)
        else:
            self.P.op(eng, lambda e: e.tensor_scalar(out=out, in0=in0, scalar1=s1, scalar2=s2, op0=op0, op1=op1), rd, wr)

    def stt(self, out, in0, scalar, in1, op0, op1, rd, wr):
        self.P.op("dve", lambda e: e.scalar_tensor_tensor(out=out, in0=in0, scalar=scalar, in1=in1,
                                                          op0=op0, op1=op1), rd, wr)

    def copy(self, eng, out, in_, rd, wr):
        self.P.op(eng, lambda e: e.tensor_copy(out=out, in_=in_), rd, wr)

    def memset(self, eng, ap, val, wr):
        self.P.op(eng, lambda e: e.memset(ap, val), (), wr)

    def scan(self, out, d0, d1, rd, wr):
        self.P.op("dve", lambda e: e.tensor_tensor_scan(out=out, data0=d0, data1=d1, initial=0.0,
                                                        op0=ALU.mult, op1=ALU.add), rd, wr)

    def load(self, out, in_, wr, eng="sp"):
        self.nstream += 1
        self.P.dma(eng, "l%d" % (self.nstream % 8), lambda e: e.dma_start(out=out, in_=in_), (), wr)

    def store(self, out, in_, rd, stream):
        self.P.dma("sp", stream, lambda e: e.dma_start(out=out, in_=in_), rd, ())

    def wload(self, src, shape):
        buf = self.ring.alloc(shape, BF16)
        t = buf.t
        if len(shape) == 3:
            o = t[:, :, :]
        else:
            o = t[:, :]
        self.P.dma("pool", "wr%d" % (self.ring.n % 8), lambda e: e.dma_start(out=o, in_=src), (), [buf.reg()])
        return buf

    def build(self):
        nc = self.nc
        nl = self.n_layers
        pers = self.pers
        xT = self.inp("xT", [128, 8, NT])
        condT = self.inp("condT", [128, 8 * NS])
        mk = self.inp("mk", [128, 32])
        cst = self.inp("cst", [128, 8])
        h0 = self.inp("h0", [128, 2 * 10 * 2 * NS])
        wmod = self.inp("wmod", [4, 12, 128, 8, 512])
        bmodF = self.inp("bmodF", [128, 4 * 240])
        ngr = self.inp("ngr", [128, 4 * 2 * 40])
        lwin = self.inp("lwin", [2, 10, 128, 8, 256])
        lcw = self.inp("lcw", [128, 2 * 10 * 4])
        lcb = self.inp("lcb", [128, 2 * 10])
        lwg = self.inp("lwg", [2, 128, 40, 128])
        lbg = self.inp("lbg", [128, 2 * 10 * 4])
        llam = self.inp("llam", [128, 2 * 2 * 10])
        lwout = self.inp("lwout", [2, 2, 128, 10, 512])
        fwu = self.inp("fwu", [4, NI, 128, 8, 256])
        fcw = self.inp("fcw", [128, 4 * 44 * 3])
        fcb = self.inp("fcb", [128, 4 * 44])
        fwd = self.inp("fwd", [4, 4, 128, NI, 256])
        self.wmod, self.lwin, self.lwg, self.lwout, self.fwu, self.fwd = wmod, lwin, lwg, lwout, fwu, fwd
        if nl >= 2:
            self.cwu = self.inp("cwu", [8, 128, 8, 256])
            self.cbu = self.inp("cbu", [128, 16])
            self.cwv = self.inp("cwv", [4, 128, 8, 512])
            self.cbv = self.inp("cbv", [1, 2048])
            self.cng = self.inp("cng", [128, 2048])
            self.cwsT = self.inp("cwsT", [128, 8, 128])
            self.cbs = self.inp("cbs", [1, 1024])
            self.cwo = self.inp("cwo", [2, 128, 16, 512])
        if nl >= 3:
            self.awqk = self.inp("awqk", [8, 128, 8, 256])
            self.awv = self.inp("awv", [2, 128, 8, 512])
            self.aqg = self.inp("aqg", [128, 2])
            self.alam = self.inp("alam", [128, 256])
            self.asg = self.inp("asg", [128, 1])
            self.awo = self.inp("awo", [2, 128, 8, 512])
            self.ckT = self.inp("ckT", [128, 8, 512])
            self.cv = self.inp("cv", [128, 4, 1024])
            self.cosT = self.inp("cosT", [128, NT])
            self.sinT = self.inp("sinT", [128, NT])
            self.rm = self.inp("rm", [128, 128])
            self.abias = self.inp("abias", [128, 24])
            self.kTo = self.outp("kT", [128, 8, NT])
            self.vOo = self.outp("vO", [128, 10, 1024])
        yT = self.outp("yT", [128, 8, NT])
        stO = self.outp("st", [2, 128, 10 * 2 * NS])
        self.stO = stO

        X = pers.alloc([128, 8, NT], F32)
        H = pers.alloc([128, 8, NT], BF16)
        self.X, self.H = X, H
        SCT = pers.alloc([128, 8 * NS], BF16)
        CND = pers.alloc([128, 8 * NS], F32)
        MK = pers.alloc([128, 32], F32)
        CST = pers.alloc([128, 8], F32)
        H0 = pers.alloc([128, 2, 10, 2, NS], F32)
        BMOD = pers.alloc([128, 4, 240], F32)
        NGR = pers.alloc([128, 4, 2, 40], F32)
        MOD = [pers.alloc([128, 48, NS], F32) for _ in range(2)]
        GS = [pers.alloc([128, 2, 40], F32) for _ in range(2)]
        LCW = pers.alloc([128, 2, 10, 4], F32)
        LCB = pers.alloc([128, 2, 10], F32)
        LBG = pers.alloc([128, 2, 10, 4], F32)
        LLAM = pers.alloc([128, 40], F32)
        NSP8 = pers.alloc([128, 2, 2, 10], F32)
        NSP16 = pers.alloc([128, 2, 2, 10], F32)
        self.NSP16 = NSP16
        LT = [pers.alloc([128, 40], F32) for _ in range(3)]
        FCW = pers.alloc([128, 4, 44, 3], F32)
        FCB = pers.alloc([128, 4, 44], F32)
        ONES = pers.alloc([128, 128], BF16)
        self.MK, self.CST, self.H0, self.MOD, self.GS = MK, CST, H0, MOD, GS
        self.LCW, self.LCB, self.LBG, self.NSP8, self.FCW, self.FCB = LCW, LCB, LBG, NSP8, FCW, FCB
        self.ONES = ONES
        self.SCT, self.BMOD, self.NGR = SCT, BMOD, NGR

        def flat2(buf):
            t = buf.t
            nd = len(t.shape)
            if nd == 2:
                return t[:, :]
            names = " ".join("a%d" % i for i in range(nd - 1))
            return t[tuple([slice(None)] * nd)].rearrange("p %s -> p (%s)" % (names, names))
        self.flat2 = flat2

        for buf, src in [(CND, condT), (MK, mk), (CST, cst), (H0, h0), (BMOD, bmodF), (NGR, ngr), (LCW, lcw),
                         (LCB, lcb), (LBG, lbg), (LLAM, llam), (FCW, fcw), (FCB, fcb)]:
            self.load(flat2(buf), src, [buf.reg()])
        for c in range(8):
            self.load(X.t[:, c, :], xT[:, c, :], [X.reg(c, 0), X.reg(c, 1), X.reg(c, 2)])
        self.memset("dve", ONES.t[:, :], 1.0, [ONES.reg()])
        self.act(SCT.t[:, :], CND.t[:, :], AF.Silu, [CND.reg()], [SCT.reg()])
        a0, a1, a2 = [b_.t[:, :] for b_ in LT]
        r0, r1, r2 = [b_.reg() for b_ in LT]
        self.act(a0, LLAM.t[:, :], AF.Abs, [LLAM.reg()], [r0])
        self.act(a1, a0, AF.Exp, [r0], [r1], scale=-1.0)
        self.act(a0, a1, AF.Ln, [r1, CST.reg()], [r0], bias=CST.t[:, 1:2])
        self.ts("dve", a2, LLAM.t[:, :], -1.0, 0.0, ALU.mult, ALU.max, [LLAM.reg()], [r2])
        self.tt("dve", a1, a0, a2, ALU.add, [r0, r2], [r1])
        self.ts("dve", flat2(NSP8), a1, -8.0, None, ALU.mult, None, [r1], [NSP8.reg()])
        self.ts("dve", flat2(NSP16), a1, -16.0, None, ALU.mult, None, [r1], [NSP16.reg()])

        self.ad = None
        if DEBUG_MODE == "attn":
            self.adaln_begin(2)
            self.adaln_flush()
            self.modulate(2, 0)
            self.attn(2, 0)
            nl = 0
        else:
            self.adaln_begin(0)
            for _ in range(6):
                self.bg()
        for l in range(nl):
            kind = l % 3
            j = l // 3
            self.modulate(l, 0)
            if kind == 0:
                self.lru(l, j)
            elif kind == 1:
                self.cmlp(l, j)
            else:
                self.attn(l, j)
            self.adaln_flush()
            self.modulate(l, 1)
            if l + 1 < nl:
                self.adaln_begin(l + 1)
            self.ffn(l)
            self.adaln_flush()

        for c in range(8):
            self.store(yT[:, c, :], X.t[:, c, :], [X.reg(c, 0), X.reg(c, 1), X.reg(c, 2)], "oy%d" % c)
        counts = self.P.emit()
        return counts

    def adaln_begin(self, l):
        assert self.ad is None
        self.ad = dict(l=l, step=0, W=self.wload(self.wmod[l, 0], [128, 8, 512]))

    def bg(self):
        ad = self.ad
        if ad is None:
            return
        l, n4 = ad["l"], ad["step"]
        MOD, GS = self.MOD[l % 2], self.GS[l % 2]
        pt, pr = self.banks[7]
        W = ad["W"]
        if n4 + 1 < 12:
            ad["W"] = self.wload(self.wmod[l, n4 + 1], [128, 8, 512])
        for nn in range(4):
            n = n4 * 4 + nn
            for kc in range(8):
                self.mm(pt[:, n * NS:(n + 1) * NS], W.t[:, kc, nn * 128:(nn + 1) * 128],
                        self.SCT.t[:, kc * NS:(kc + 1) * NS], kc == 0, kc == 7,
                        [W.reg(), self.SCT.reg()], [pr])
        W.closed = True
        ad["step"] += 1
        if ad["step"] in (6, 12):
            hf = ad["step"] // 6 - 1
            modf = self.flat2(MOD)
            cs = slice(hf * 120, (hf + 1) * 120)
            self.tt("dve", modf[:, cs], pt[:, cs], self.BMOD.t[:, l, cs], ALU.add, [pr, self.BMOD.reg()], [MOD.reg(hf)])
            m = 1 + 3 * hf
            self.stt(GS.t[:, hf, :], modf[:, m * 40:(m + 1) * 40], 1.0, self.NGR.t[:, l, hf, :],
                     ALU.add, ALU.mult, [MOD.reg(hf), self.NGR.reg()], [GS.reg(hf)])
        if ad["step"] == 12:
            self.ad = None

    def adaln_flush(self):
        while self.ad is not None:
            self.bg()

    def modulate(self, l, which):
        X, H, MOD, GS = self.X, self.H, self.MOD[l % 2], self.GS[l % 2]
        msh = 3 * which
        CST = self.CST
        ar = self.arena
        ar.push()
        RSTD = ar.alloc([128, NT], F32)
        SQ = ar.alloc([128, 4, 512], BF16)
        XN = ar.alloc([128, 2, NT], F32)
        k = 0
        for ti, (t0, tn) in enumerate(TT):
            pt, pr = self.ps()
            for c in range(8):
                sq = SQ.t[:, k % 4, 0:tn]
                sqr = SQ.reg(k % 4)
                k += 1
                if c % 3 != 2:
                    self.act(sq, X.t[:, c, t0:t0 + tn], AF.Square, [X.reg(c, ti)], [sqr])
                else:
                    self.tt("dve", sq, X.t[:, c, t0:t0 + tn], X.t[:, c, t0:t0 + tn], ALU.mult, [X.reg(c, ti)], [sqr])
                self.mm(pt[:, 0:tn], self.ONES.t[:, :], sq, c == 0, c == 7, [sqr, self.ONES.reg()], [pr])
            rs = RSTD.t[:, t0:t0 + tn]
            rr = RSTD.reg(ti)
            self.act(rs, pt[:, 0:tn], AF.Ln, [pr, CST.reg()], [rr], bias=CST.t[:, 0:1], scale=1.0 / D)
            self.act(rs, rs, AF.Exp, [rr], [rr], scale=-0.5)
        n = 0
        for c in range(8):
            xn = XN.t[:, c % 2, :]
            xnr = XN.reg(c % 2)
            self.tt("dve", xn, X.t[:, c, :], RSTD.t[:, :], ALU.mult,
                    [X.reg(c, 0), X.reg(c, 1), X.reg(c, 2), RSTD.reg(0), RSTD.reg(1), RSTD.reg(2)], [xnr])
            for s_ in range(NS):
                ti = s_ // 2 if s_ < 4 else 2
                gs = GS.t[:, which, c * NS + s_:c * NS + s_ + 1]
                sh = MOD.t[:, msh * 8 + c, s_:s_ + 1]
                o = H.t[:, c, s_ * SL:(s_ + 1) * SL]
                i_ = XN.t[:, c % 2, s_ * SL:(s_ + 1) * SL]
                if n % 5 < 3:
                    self.act(o, i_, AF.Identity, [xnr, GS.reg(which), MOD.reg(which)], [H.reg(c, s_)],
                             bias=sh, scale=gs)
                else:
                    self.ts("dve", o, i_, gs, sh, ALU.mult, ALU.add, [xnr, GS.reg(which), MOD.reg(which)],
                            [H.reg(c, s_)])
                n += 1
        ar.pop()

    def resid_add(self, pt, pr, c, ti, gate_m, MOD):
        X = self.X
        s0, ns = slots_of(ti)
        for si in range(ns):
            s = s0 + si
            xs = X.t[:, c, s * SL:(s + 1) * SL]
            self.stt(xs, pt[:, si * SL:(si + 1) * SL], MOD.t[:, gate_m * 8 + c, s:s + 1], xs,
                     ALU.mult, ALU.add, [pr, MOD.reg(gate_m // 3), X.reg(c, ti)], [X.reg(c, ti)])

    def lru(self, l, a):
        ar = self.arena
        H, MOD, MK, CST = self.H, self.MOD[l % 2], self.MK, self.CST
        ar.push()
        M = ar.alloc([128, 10, NT], BF16)
        GB = ar.alloc([128, NT], BF16)
        RECP = ar.alloc([128, NS, 259], F32)
        HD1t = buf_view(self.nc, RECP, [128, NT], F32, "hd1v%d" % l)
        XC = ar.alloc([128, NT], F32)
        XCB = ar.alloc([128, NT], BF16)
        RA = [[ar.alloc([128, NT], F32) for _ in range(2)] for _ in range(2)]
        IB = [[ar.alloc([128, NT], F32) for _ in range(2)] for _ in range(2)]
        SQb = [ar.alloc([128, NT], F32) for _ in range(2)]
        HD0 = ar.alloc([128, NT], F32)
        ST = ar.alloc([128, 10, 2, NS], F32)
        TB = [ar.alloc([128, NS], F32) for _ in range(2)]
        self.memset("dve", self.flat2(RECP), 0.0, [RECP.reg()])
        xc3 = XC.t[:, :].rearrange("p (s t) -> p s t", t=SL)
        Ws = {}

        def part_a(j):
            W = Ws[j]
            banks = [self.ps() for _ in TT]
            for kc in range(8):
                for ti, (t0, tn) in enumerate(TT):
                    pt, pr = banks[ti]
                    self.mm(pt[:, 0:tn], W.t[:, kc, 0:128], H.t[:, kc, t0:t0 + tn],
                            kc == 0, kc == 7, [W.reg(), hregs(H, kc, ti)], [pr])
            W.closed = True
            for ti, (t0, tn) in enumerate(TT):
                pt, pr = banks[ti]
                self.act(GB.t[:, t0:t0 + tn], pt[:, 0:tn], AF.Gelu_apprx_tanh, [pr], [GB.reg()])

        def part_b(j):
            W = self.wload(self.lwin[a, j], [128, 8, 256])
            Ws[j] = W
            WGj = self.wload(self.lwg[a][:, j * 4:(j + 1) * 4, :], [128, 4, 128])
            banks = [self.ps() for _ in TT]
            for kc in range(8):
                for ti, (t0, tn) in enumerate(TT):
                    pt, pr = banks[ti]
                    self.mm(pt[:, 0:tn], W.t[:, kc, 128:256], H.t[:, kc, t0:t0 + tn],
                            kc == 0, kc == 7, [W.reg(), hregs(H, kc, ti)], [pr])
            for ti, (t0, tn) in enumerate(TT):
                pt, pr = banks[ti]
                s0, ns = slots_of(ti)
                self.act(RECP.t[:, s0:s0 + ns, 2:258], pt[:, 0:tn].rearrange("p (s t) -> p s t", t=SL),
                         AF.Copy, [pr], [RECP.reg()])
            self.memset("dve", RECP.t[:, 0, 0:2], 0.0, [RECP.reg()])
            self.tt("dve", RECP.t[:, 1:5, 0:2], RECP.t[:, 0:4, 256:258],
                    MK.t[:, 14:22].rearrange("p (s t) -> p s t", t=2), ALU.mult, [RECP.reg(), MK.reg()], [RECP.reg()])
            self.tt("dve", RECP.t[:, 0:4, 258:259], RECP.t[:, 1:5, 2:3],
                    MK.t[:, 10:14].rearrange("p (s t) -> p s t", t=1), ALU.mult, [RECP.reg(), MK.reg()], [RECP.reg()])
            cw = lambda k: self.LCW.t[:, a, j, k:k + 1]
            self.act(xc3, RECP.t[:, :, 0:256], AF.Identity, [RECP.reg(), self.LCW.reg(), self.LCB.reg()], [XC.reg()],
                     bias=self.LCB.t[:, a, j:j + 1], scale=cw(0))
            if j >= 1:
                part_s(j - 1)
            for k in range(1, 4):
                self.stt(xc3, RECP.t[:, :, k:k + 256], cw(k), xc3, ALU.mult, ALU.add,
                         [RECP.reg(), self.LCW.reg(), XC.reg()], [XC.reg()])
            self.copy("dve", XCB.t[:, :], XC.t[:, :], [XC.reg()], [XCB.reg()])
            for d in range(2):
                for g in range(2):
                    dst = RA[j % 2][d] if g == 0 else IB[j % 2][d]
                    for ti, (t0, tn) in enumerate(TT):
                        pt, pr = self.ps()
                        self.mm(pt[:, 0:tn], WGj.t[:, d * 2 + g, :], XCB.t[:, t0:t0 + tn], True, True,
                                [WGj.reg(), XCB.reg()], [pr])
                        self.act(dst.t[:, t0:t0 + tn], pt[:, 0:tn], AF.Sigmoid, [pr, self.LBG.reg()], [dst.reg()],
                                 bias=self.LBG.t[:, a, j, d * 2 + g:d * 2 + g + 1])
            WGj.closed = True
            for d in range(2):
                ra = RA[j % 2][d]
                self.act(ra.t[:, :], ra.t[:, :], AF.Exp, [ra.reg(), self.NSP8.reg()], [ra.reg()],
                         scale=self.NSP8.t[:, a, d, j:j + 1])

        def part_d(j):
            for d in range(2):
                ib = IB[j % 2][d]
                self.tt("dve", ib.t[:, :], ib.t[:, :], XC.t[:, :], ALU.mult, [ib.reg(), XC.reg()], [ib.reg()])
            for d in range(2):
                ra = RA[j % 2][d]
                self.tt("dve", SQb[d].t[:, :], ra.t[:, :], ra.t[:, :], ALU.mult, [ra.reg()], [SQb[d].reg()])

        def part_s(j):
            for d in range(2):
                self.act(SQb[d].t[:, :], SQb[d].t[:, :], AF.Sqrt, [SQb[d].reg(), CST.reg()], [SQb[d].reg()],
                         bias=CST.t[:, 1:2], scale=-1.0)

        def part_x(j):
            for d in range(2):
                ib = IB[j % 2][d]
                self.tt("dve", ib.t[:, :], ib.t[:, :], SQb[d].t[:, :], ALU.mult, [ib.reg(), SQb[d].reg()], [ib.reg()])

        def part_c(j):
            for d in range(2):
                ra, ib, tb = RA[j % 2][d], IB[j % 2][d], TB[d]
                e0 = 0 if d == 0 else SL - 1
                af = ra.t[:, e0:NT:SL]
                bf = ib.t[:, e0:NT:SL]
                mcol = MK.t[:, 0:5] if d == 0 else MK.t[:, 5:10]
                self.tt("dve", tb.t[:, :], af, self.H0.t[:, a, j, d, :], ALU.mult, [ra.reg(), self.H0.reg()], [tb.reg()])
                self.tt("dve", bf, bf, tb.t[:, :], ALU.add, [ib.reg(), tb.reg()], [ib.reg()])
                self.tt("dve", af, af, mcol, ALU.mult, [ra.reg(), MK.reg()], [ra.reg()])
                if d == 0:
                    self.scan(HD0.t[:, :], ra.t[:, :], ib.t[:, :], [ra.reg(), ib.reg()], [HD0.reg()])
                    self.copy("dve", ST.t[:, j, 0, :], HD0.t[:, SL - 1:NT:SL], [HD0.reg()], [ST.reg(j)])
                else:
                    self.scan(HD1t[:, ::-1], ra.t[:, ::-1], ib.t[:, ::-1], [ra.reg(), ib.reg()], [RECP.reg()])
                    self.copy("dve", ST.t[:, j, 1, :], HD1t[:, 0:NT:SL], [RECP.reg()], [ST.reg(j)])
            self.tt("dve", HD0.t[:, :], HD0.t[:, :], HD1t[:, :], ALU.add, [HD0.reg(), RECP.reg()], [HD0.reg()])
            self.tt("dve", M.t[:, j, :], HD0.t[:, :], GB.t[:, :], ALU.mult, [HD0.reg(), GB.reg()], [M.reg(j)])

        for it in range(11):
            self.bg()
            if it >= 1:
                part_a(it - 1)
            if it < 10:
                part_b(it)
            if it == 10:
                part_s(9)
            if it >= 1:
                part_x(it - 1)
                part_c(it - 1)
            if it < 10:
                part_d(it)
        self.store(self.stO[a], self.flat2(ST), [ST.reg(j) for j in range(10)], "ost%d" % a)
        for half in range(2):
            W = self.wload(self.lwout[a, half], [128, 10, 512])
            for cc in range(4):
                c = half * 4 + cc
                for ti, (t0, tn) in enumerate(TT):
                    pt, pr = self.ps()
                    for jc in range(10):
                        self.mm(pt[:, 0:tn], W.t[:, jc, cc * 128:(cc + 1) * 128], M.t[:, jc, t0:t0 + tn],
                                jc == 0, jc == 9, [W.reg(), M.reg(jc)], [pr])
                    self.resid_add(pt, pr, c, ti, 2, MOD)
            W.closed = True
        ar.pop()

    def ffn(self, l):
        ar = self.arena
        H, MOD, MK = self.H, self.MOD[l % 2], self.MK
        ar.push()
        A = ar.alloc([128, NI, NT], BF16)
        Z = [[ar.alloc([128, NS, 258], F32) for _ in range(2)] for _ in range(2)]
        C2 = [[ar.alloc([128, NT], F32) for _ in range(2)] for _ in range(2)]
        for zz in Z:
            for z in zz:
                self.memset("dve", self.flat2(z), 0.0, [z.reg()])
        mk1 = MK.t[:, 10:14].rearrange("p (s t) -> p s t", t=1)

        def ffn_tail(i_):
            C_ = C2[i_ % 2]
            self.act(C_[0].t[:, :], C_[0].t[:, :], AF.Silu, [C_[0].reg()], [C_[0].reg()])
            self.tt("dve", A.t[:, i_, :], C_[0].t[:, :], C_[1].t[:, :], ALU.mult, [C_[0].reg(), C_[1].reg()], [A.reg(i_)])

        for i in range(NI):
            W = self.wload(self.fwu[l, i], [128, 8, 256])
            if i >= 2:
                self.bg()
            banks = [[self.ps() for _ in TT] for _ in range(2)]
            for half in range(2):
                for kc in range(8):
                    for ti, (t0, tn) in enumerate(TT):
                        pt, pr = banks[half][ti]
                        self.mm(pt[:, 0:tn], W.t[:, kc, half * 128:(half + 1) * 128], H.t[:, kc, t0:t0 + tn],
                                kc == 0, kc == 7, [W.reg(), hregs(H, kc, ti)], [pr])
            W.closed = True
            C = C2[i % 2]
            for half in range(2):
                z = Z[i % 2][half]
                for ti, (t0, tn) in enumerate(TT):
                    pt, pr = banks[half][ti]
                    s0, ns = slots_of(ti)
                    self.act(z.t[:, s0:s0 + ns, 1:257], pt[:, 0:tn].rearrange("p (s t) -> p s t", t=SL),
                             AF.Copy, [pr], [z.reg()])
            if i >= 1:
                ffn_tail(i - 1)
            for half in range(2):
                z = Z[i % 2][half]
                self.tt("dve", z.t[:, 1:5, 0:1], z.t[:, 0:4, 256:257], mk1, ALU.mult, [z.reg(), MK.reg()], [z.reg()])
                self.tt("dve", z.t[:, 0:4, 257:258], z.t[:, 1:5, 1:2], mk1, ALU.mult, [z.reg(), MK.reg()], [z.reg()])
            for half in range(2):
                z = Z[i % 2][half]
                cb = C[half]
                n = half * NI + i
                c3 = cb.t[:, :].rearrange("p (s t) -> p s t", t=SL)
                self.act(c3, z.t[:, :, 0:256], AF.Identity, [z.reg(), self.FCW.reg(), self.FCB.reg()], [cb.reg()],
                         bias=self.FCB.t[:, l, n:n + 1], scale=self.FCW.t[:, l, n, 0:1])
            for half in range(2):
                z = Z[i % 2][half]
                cb = C[half]
                n = half * NI + i
                c3 = cb.t[:, :].rearrange("p (s t) -> p s t", t=SL)
                for k in range(1, 3):
                    self.stt(c3, z.t[:, :, k:k + 256], self.FCW.t[:, l, n, k:k + 1], c3, ALU.mult, ALU.add,
                             [z.reg(), self.FCW.reg(), cb.reg()], [cb.reg()])
        ffn_tail(NI - 1)
        for q in range(4):
            W = self.wload(self.fwd[l, q], [128, NI, 256])
            for cc in range(2):
                c = q * 2 + cc
                for ti, (t0, tn) in enumerate(TT):
                    pt, pr = self.ps()
                    for ic in range(NI):
                        self.mm(pt[:, 0:tn], W.t[:, ic, cc * 128:(cc + 1) * 128], A.t[:, ic, t0:t0 + tn],
                                ic == 0, ic == NI - 1, [W.reg(), A.reg(ic)], [pr])
                    self.resid_add(pt, pr, c, ti, 5, MOD)
            W.closed = True
        ar.pop()

    def cmlp(self, l, j):
        ar = self.arena
        H, MOD, CST = self.H, self.MOD[l % 2], self.CST
        ar.push()
        U = ar.alloc([128, 16, NT], BF16)
        VGs = [ar.alloc([128, 2048], F32) for _ in range(2)]
        VN = [ar.alloc([128, 2048], BF16) for _ in range(2)]
        CNG = ar.alloc([128, 2048], F32)
        WST = ar.alloc([128, 8, 128], BF16)
        CBU = ar.alloc([128, 16], F32)
        CBVH = ar.alloc([1, 2048], BF16)
        CBVL = ar.alloc([1, 2048], BF16)
        CBSH = ar.alloc([1, 1024], BF16)
        CBSL = ar.alloc([1, 1024], BF16)
        JUNK = ar.alloc([128, 512], BF16)
        SS = ar.alloc([128, 16], F32)
        self.load(CNG.t[:, :], self.cng, [CNG.reg()])
        self.load(CBU.t[:, :], self.cbu, [CBU.reg()])
        self.P.dma("pool", "wg", lambda e: e.dma_start(out=WST.t[:, :, :], in_=self.cwsT), (), [WST.reg()])
        stg = [(VGs[0].t[0:1, 0:2048], [VGs[0].reg(c_) for c_ in range(4)], self.cbv, CBVH, CBVL),
               (VGs[1].t[0:1, 0:1024], [VGs[1].reg(c_) for c_ in range(4)], self.cbs, CBSH, CBSL)]
        for sap, sregs, src_, hi_, lo_ in stg:
            self.load(sap, src_, sregs)
            self.act(hi_.t[:, :], sap, AF.Copy, sregs, [hi_.reg()])
            self.tt("dve", lo_.t[:, :], sap, hi_.t[:, :], ALU.subtract, sregs + [hi_.reg()], [lo_.reg()])
        for i in range(8):
            W = self.wload(self.cwu[i], [128, 8, 256])
            banks = [[self.ps() for _ in TT] for _ in range(2)]
            for half in range(2):
                for kc in range(8):
                    for ti, (t0, tn) in enumerate(TT):
                        pt, pr = banks[half][ti]
                        self.mm(pt[:, 0:tn], W.t[:, kc, half * 128:(half + 1) * 128], H.t[:, kc, t0:t0 + tn],
                                kc == 0, kc == 7, [W.reg(), hregs(H, kc, ti)], [pr])
            W.closed = True
            for half in range(2):
                cc = 2 * i + half
                for ti, (t0, tn) in enumerate(TT):
                    pt, pr = banks[half][ti]
                    self.act(U.t[:, cc, t0:t0 + tn], pt[:, 0:tn], AF.Gelu_apprx_tanh, [pr, CBU.reg()],
                             [U.reg(cc, ti)], bias=CBU.t[:, cc:cc + 1])
        self.ring.reset()
        WV = [self.wload(self.cwv[ct], [128, 8, 512]) for ct in range(4)]
        vbanks = {}

        def v_mm(tb):
            ti = min(tb // 4, 2)
            tsl = slice(tb * 128, (tb + 1) * 128)
            vbanks[tb] = []
            for ct in range(4):
                pt, pr = self.ps()
                vbanks[tb].append((pt, pr))
                for kc in range(8):
                    self.mm(pt[:, 0:512], H.t[:, kc, tsl], WV[ct].t[:, kc, :], kc == 0, False,
                            [hregs(H, kc, ti), WV[ct].reg()], [pr])
                self.mm(pt[:, 0:512], self.ONES.t[0:1, :], CBVH.t[0:1, ct * 512:(ct + 1) * 512], False, False,
                        [self.ONES.reg(), CBVH.reg()], [pr])
                self.mm(pt[:, 0:512], self.ONES.t[0:1, :], CBVL.t[0:1, ct * 512:(ct + 1) * 512], False, True,
                        [self.ONES.reg(), CBVL.reg()], [pr])

        def v_act(tb):
            vg = VGs[tb % 2]
            so = 8 * (tb % 2)
            for ct in range(4):
                pt, pr = vbanks[tb][ct]
                self.act(vg.t[:, ct * 512:(ct + 1) * 512], pt[:, 0:512], AF.Gelu_apprx_tanh, [pr], [vg.reg(ct)])
                self.P.op("act", (lambda o, i_, acc: (lambda e: e.activation(out=o, in_=i_, func=AF.Square, accum_out=acc)))(
                    JUNK.t[:, :], vg.t[:, ct * 512:(ct + 1) * 512], SS.t[:, so + ct:so + ct + 1]),
                    [vg.reg(ct)], [JUNK.reg(), SS.reg(so + ct)])

        def n_part(tb):
            vg = VGs[tb % 2]
            so = 8 * (tb % 2)
            self.P.op("dve", (lambda so_: lambda e: e.tensor_reduce(out=SS.t[:, so_ + 4:so_ + 5], in_=SS.t[:, so_:so_ + 4],
                                                                    axis=mybir.AxisListType.X, op=ALU.add))(so),
                      [SS.reg(so + c_) for c_ in range(4)], [SS.reg(so + 4)])
            self.act(SS.t[:, so + 5:so + 6], SS.t[:, so + 4:so + 5], AF.Ln, [SS.reg(so + 4), CST.reg()], [SS.reg(so + 5)],
                     bias=CST.t[:, 0:1], scale=1.0 / D_B)
            self.act(SS.t[:, so + 6:so + 7], SS.t[:, so + 5:so + 6], AF.Exp, [SS.reg(so + 5)], [SS.reg(so + 6)], scale=-0.5)
            vn = VN[tb % 2]
            self.stt(vn.t[:, :], vg.t[:, :], SS.t[:, so + 6:so + 7], CNG.t[:, :], ALU.mult, ALU.mult,
                     [vg.reg(c_) for c_ in range(4)] + [SS.reg(so + 6), CNG.reg()], [vn.reg()])

        def s_part(tb):
            ti = min(tb // 4, 2)
            tsl = slice(tb * 128, (tb + 1) * 128)
            vn = VN[tb % 2]
            for cq in range(4):
                pt, pr = self.ps()
                for c4 in range(4):
                    cc = cq * 4 + c4
                    g = cc // 2
                    o = pt[:, c4 * 128:(c4 + 1) * 128]
                    self.mm(o, vn.t[:, cc * 128:(cc + 1) * 128], WST.t[:, g, :], True, False,
                            [vn.reg(), WST.reg()], [pr])
                    self.mm(o, self.ONES.t[0:1, :], CBSH.t[0:1, g * 128:(g + 1) * 128], False, False,
                            [self.ONES.reg(), CBSH.reg()], [pr])
                    self.mm(o, self.ONES.t[0:1, :], CBSL.t[0:1, g * 128:(g + 1) * 128], False, True,
                            [self.ONES.reg(), CBSL.reg()], [pr])
                uv = U.t[:, cq * 4:(cq + 1) * 4, tsl]
                self.tt("dve", uv, pt[:, 0:512].rearrange("p (c t) -> p c t", t=128), uv, ALU.mult,
                        [pr] + [U.reg(cq * 4 + c_, ti) for c_ in range(4)],
                        [U.reg(cq * 4 + c_, ti) for c_ in range(4)])

        v_mm(0)
        v_act(0)
        for tb in range(10):
            if tb + 1 < 10:
                v_mm(tb + 1)
            n_part(tb)
            if tb + 1 < 10:
                v_act(tb + 1)
            s_part(tb)
        for w in WV:
            w.closed = True
        for half in range(2):
            W = self.wload(self.cwo[half], [128, 16, 512])
            for cc4 in range(4):
                c = half * 4 + cc4
                for ti, (t0, tn) in enumerate(TT):
                    pt, pr = self.ps()
                    for kc in range(16):
                        self.mm(pt[:, 0:tn], W.t[:, kc, cc4 * 128:(cc4 + 1) * 128], U.t[:, kc, t0:t0 + tn],
                                kc == 0, kc == 15, [W.reg(), U.reg(kc, ti)], [pr])
                    self.resid_add(pt, pr, c, ti, 2, MOD)
            W.closed = True
        ar.pop()

    def attn(self, l, j):
        ar = self.arena
        H, MOD, CST = self.H, self.MOD[l % 2], self.CST
        lam_init = 0.8 - 0.6 * math.exp(-0.3 * l)
        ar.push()
        SM = ar.alloc([128, 16], F32)
        ar.push()
        ALAM = ar.alloc([128, 256], F32)
        LP = ar.alloc([128, 128], F32)
        self.load(ALAM.t[:, :], self.alam, [ALAM.reg()])
        self.load(SM.t[:, 0:1], self.asg, [SM.reg(0)])
        self.tt("dve", LP.t[:, 0:64], ALAM.t[:, 0:64], ALAM.t[:, 64:128], ALU.mult, [ALAM.reg()], [LP.reg()])
        self.tt("dve", LP.t[:, 64:128], ALAM.t[:, 128:192], ALAM.t[:, 192:256], ALU.mult, [ALAM.reg()], [LP.reg()])
        self.P.op("dve", lambda e: e.tensor_reduce(out=SM.t[:, 1:3], in_=LP.t[:, :].rearrange("p (a b) -> p a b", b=64),
                                                   axis=mybir.AxisListType.X, op=ALU.add), [LP.reg()], [SM.reg(1)])
        self.act(SM.t[:, 3:5], SM.t[:, 1:3], AF.Exp, [SM.reg(1)], [SM.reg(3)])
        self.tt("dve", SM.t[:, 5:6], SM.t[:, 4:5], SM.t[:, 3:4], ALU.subtract, [SM.reg(3)], [SM.reg(5)])
        self.ts("dve", SM.t[:, 6:7], SM.t[:, 5:6], -lam_init, None, ALU.add, None, [SM.reg(5)], [SM.reg(6)])
        self.ts("dve", SM.t[:, 7:8], SM.t[:, 0:1], 1.0 - lam_init, None, ALU.mult, None, [SM.reg(0)], [SM.reg(7)])
        if not DBG.get("nonest"):
            ar.pop()
        else:
            ar.frames.pop()
        NLAM = SM.t[:, 6:7]
        SG1 = SM.t[:, 7:8]
        QZ = [ar.alloc([128, 4, NT], BF16) for _ in range(2)]
        KT = ar.alloc([128, 4, 512 + NT], BF16)
        VB = ar.alloc([128, 14, 512], BF16)
        AO0 = ar.alloc([128, 4, NT], BF16)
        COS = ar.alloc([128, 512], F32)
        SIN = ar.alloc([128, 512], F32)
        RM = ar.alloc([128, 128], F32)
        BD = ar.alloc([128, 128], BF16)
        SQ2 = [ar.alloc([128, 512], BF16) for _ in range(2)]
        RS2 = [ar.alloc([128, 512], F32) for _ in range(2)]
        QN = [ar.alloc([128, 512], F32) for _ in range(2)]
        QNB = [ar.alloc([128, 512], BF16) for _ in range(2)]
        RMB = ar.alloc([128, 128], BF16)
        T1r = ar.alloc([128, 512], F32)
        T2r = ar.alloc([128, 512], F32)
        VF = ar.alloc([128, 512], F32)
        PT = [ar.alloc([128, 512], BF16) for _ in range(4)]
        RZ = ar.alloc([128, 512], F32)
        T1 = ar.alloc([128, 256], F32)
        T2 = ar.alloc([128, 256], F32)
        OO = [ar.alloc([128, 256], F32) for _ in range(2)]
        OSQ = [ar.alloc([128, 256], BF16) for _ in range(2)]
        RSO = ar.alloc([128, 256], F32)
        AQG = ar.alloc([128, 2], F32)
        ABI = ar.alloc([128, 24], F32)
        for buf, src in [(RM, self.rm), (AQG, self.aqg), (ABI, self.abias)]:
            self.load(buf.t[:, :], src, [buf.reg()])
        self.copy("dve", RMB.t[:, :], RM.t[:, :], [RM.reg()], [RMB.reg()])
        for qz in QZ:
            self.memset("dve", self.flat2(qz), 0.0, [qz.reg("z")])
        bde = "pool" if DBG.get("bdpool") else "dve"
        self.memset(bde, BD.t[:, :], 0.0, [BD.reg()])
        self.memset(bde, BD.t[0:64, 0:64], 1.0, [BD.reg()])
        self.memset(bde, BD.t[64:128, 64:128], 1.0, [BD.reg()])
        sc_banks = [0, 1, 2]
        acc_banks = [4, 5, 6, 7]
        for hg in range(2):
            self.ring.reset()
            Wq = [self.wload(self.awqk[2 * hg + i], [128, 8, 256]) for i in range(2)]
            Wk = [self.wload(self.awqk[4 + 2 * hg + i], [128, 8, 256]) for i in range(2)]
            WVh = self.wload(self.awv[hg], [128, 8, 512])
            self.P.dma("pool", "wg", (lambda hg_: lambda e: e.dma_start(out=KT.t[:, :, 0:512], in_=self.ckT[:, 4 * hg_:4 * hg_ + 4, :]))(hg),
                       (), [KT.reg("c")])
            self.P.dma("pool", "wg", (lambda hg_: lambda e: e.dma_start(out=VB.t[:, 0:4, :], in_=self.cv[:, :, 512 * hg_:512 * hg_ + 512]))(hg),
                       (), [VB.reg(kb) for kb in range(4)])
            items = [(ti, isk, hh) for ti in range(3) for isk in range(2) for hh in range(4)]
            st = {}

            def stage_a(n):
                ti, isk, hh = items[n]
                t0, tn = TT[ti]
                if isk == 0 and hh == 0:
                    self.load(COS.t[:, 0:tn], self.cosT[:, t0:t0 + tn], [COS.reg()])
                    self.load(SIN.t[:, 0:tn], self.sinT[:, t0:t0 + tn], [SIN.reg()])
                W = (Wk if isk else Wq)[hh // 2]
                wc = (hh % 2) * 128
                pt, pr = self.ps()
                for kc in range(8):
                    self.mm(pt[:, 0:tn], W.t[:, kc, wc:wc + 128], H.t[:, kc, t0:t0 + tn], kc == 0, kc == 7,
                            [W.reg(), hregs(H, kc, ti)], [pr])
                sq = SQ2[n % 2]
                self.act(sq.t[:, 0:tn], pt[:, 0:tn], AF.Square, [pr], [sq.reg()])
                st[n] = (pt, pr)

            def stage_b(n):
                ti, isk, hh = items[n]
                t0, tn = TT[ti]
                h = 4 * hg + hh
                pt, pr = st[n]
                sq, rs, qn = SQ2[n % 2], RS2[n % 2], QN[n % 2]
                pm, pmr = self.ps()
                self.mm(pm[:, 0:tn], BD.t[:, :], sq.t[:, 0:tn], True, True, [BD.reg(), sq.reg()], [pmr])
                self.act(rs.t[:, 0:tn], pm[:, 0:tn], AF.Ln, [pmr, CST.reg()], [rs.reg()], bias=CST.t[:, 0:1],
                         scale=1.0 / 64)
                self.act(rs.t[:, 0:tn], rs.t[:, 0:tn], AF.Exp, [rs.reg()], [rs.reg()], scale=-0.5)
                self.stt(qn.t[:, 0:tn], pt[:, 0:tn], AQG.t[:, isk:isk + 1], rs.t[:, 0:tn], ALU.mult, ALU.mult,
                         [pr, AQG.reg(), rs.reg()], [qn.reg()])
                if isk:
                    self.store(self.kTo[:, h, t0:t0 + tn], qn.t[:, 0:tn], [qn.reg()], "ok%d" % (n % 4))

            def stage_c(n):
                ti, isk, hh = items[n]
                t0, tn = TT[ti]
                qn = QN[n % 2]
                qb = QNB[n % 2]
                self.act(qb.t[:, 0:tn], qn.t[:, 0:tn], AF.Copy, [qn.reg()], [qb.reg()])
                pq, pqr = self.ps()
                self.mm(pq[:, 0:tn], RMB.t[:, :], qb.t[:, 0:tn], True, True, [RMB.reg(), qb.reg()], [pqr])
                self.tt("pool" if DBG.get("t1pool") else "dve", T1r.t[:, 0:tn], qn.t[:, 0:tn], COS.t[:, 0:tn], ALU.mult, [qn.reg(), COS.reg()], [T1r.reg()])
                self.tt("dve", T2r.t[:, 0:tn], pq[:, 0:tn], SIN.t[:, 0:tn], ALU.mult, [pqr, SIN.reg()], [T2r.reg()])
                if isk:
                    dst, dreg = KT.t[:, hh, 512 + t0:512 + t0 + tn], KT.reg(hh, ti)
                    self.tt("dve", dst, T1r.t[:, 0:tn], T2r.t[:, 0:tn], ALU.add, [T1r.reg(), T2r.reg()], [dreg])
                else:
                    for c in range(2):
                        ps_ = slice(c * 64, (c + 1) * 64)
                        self.tt("dve", QZ[c].t[ps_, hh, t0:t0 + tn], T1r.t[ps_, 0:tn], T2r.t[ps_, 0:tn], ALU.add,
                                [T1r.reg(), T2r.reg(), QZ[c].reg("z")], [QZ[c].reg(hh, ti)])

            nit = len(items)
            for ti in range(3):
                idx = [n for n in range(nit) if items[n][0] == ti]
                lo, hi = idx[0], idx[-1] + 1
                if DBG.get("seqproj"):
                    for k in range(lo, hi):
                        stage_a(k)
                        stage_b(k)
                        stage_c(k)
                    continue
                for k in range(lo - 1, hi + 1):
                    if lo <= k + 1 < hi:
                        stage_a(k + 1)
                    if lo <= k < hi:
                        stage_b(k)
                    if lo <= k - 1 < hi:
                        stage_c(k - 1)
            for tb in range(10):
                ti = min(tb // 4, 2)
                pt, pr = self.ps()
                for kc in range(8):
                    self.mm(pt[:, 0:512], H.t[:, kc, tb * 128:(tb + 1) * 128], WVh.t[:, kc, :], kc == 0, kc == 7,
                            [hregs(H, kc, ti), WVh.reg()], [pr])
                self.act(VF.t[:, :], pt[:, 0:512], AF.Copy, [pr], [VF.reg()])
                self.copy("dve", VB.t[:, 4 + tb, :], VF.t[:, :], [VF.reg()], [VB.reg(4 + tb)])
                self.store(self.vOo[:, tb, 512 * hg:512 * hg + 512], VF.t[:, :], [VF.reg()], "ov%d" % (tb % 2))
            for w in Wq + Wk + [WVh]:
                w.closed = True
            jobs = []
            for s in range(NS):
                kbs = list(range(12)) if s < 4 else [12, 13]
                for hh in range(4):
                    for ki, kb in enumerate(kbs):
                        jobs.append((s, hh, kb, ki == 0, ki == len(kbs) - 1))
            jst = {}
            acc = {}

            def job_scores(n):
                s, hh, kb, first, last = jobs[n]
                tis = s // 2 if s < 4 else 2
                k0 = kb * 128
                if kb < 4:
                    kreg = KT.reg("c")
                else:
                    kreg = KT.reg(hh, min((kb - 4) // 4, 2))
                pc, pcr = self.ps_pool("sc", sc_banks)
                for c in range(2):
                    self.mm(pc[:, c * 256:(c + 1) * 256], KT.t[:, hh, k0:k0 + 128],
                            QZ[c].t[:, hh, s * SL:(s + 1) * SL], True, True,
                            [kreg, QZ[c].reg(hh, tis), QZ[c].reg("z")], [pcr])
                jst[n] = (pc, pcr)

            def job_rest(n):
                s, hh, kb, first, last = jobs[n]
                pc, pcr = jst.pop(n)
                ptb = PT[n % len(PT)]
                if s < 4:
                    bcol = s * 6 + kb // 2
                    self.act(ptb.t[:, :], pc[:, 0:512], AF.Exp, [pcr, ABI.reg()], [ptb.reg()],
                             bias=ABI.t[:, bcol:bcol + 1], scale=0.125)
                else:
                    self.act(ptb.t[:, :], pc[:, 0:512], AF.Exp, [pcr], [ptb.reg()], scale=0.125)
                if first:
                    acc[(s, hh)] = (self.ps_pool("acc", acc_banks), self.ps_pool("acc", acc_banks))
                (pz, pzr), (po, por) = acc[(s, hh)]
                self.mm(pz[:, 0:512], self.ONES.t[:, :], ptb.t[:, :], first, last, [self.ONES.reg(), ptb.reg()], [pzr])
                self.mm(po[:, 0:512], VB.t[:, kb, hh * 128:(hh + 1) * 128], ptb.t[:, :], first, last,
                        [VB.reg(kb), ptb.reg()], [por])

            fin_i = [0]

            def fin_part1(s, hh):
                (pz, pzr), (po, por) = acc.pop((s, hh))
                k = fin_i[0] % 2
                fin_i[0] += 1
                if DBG.get("recip_act"):
                    self.act(RZ.t[:, :], pz[:, 0:512], AF.Ln, [pzr], [RZ.reg()])
                    self.act(RZ.t[:, :], RZ.t[:, :], AF.Exp, [RZ.reg()], [RZ.reg()], scale=-1.0)
                else:
                    self.P.op("dve", lambda e: e.reciprocal(out=RZ.t[:, :], in_=pz[:, 0:512]), [pzr], [RZ.reg()])
                self.tt("dve", T1.t[:, :], po[:, 0:256], RZ.t[:, 0:256], ALU.mult, [por, RZ.reg()], [T1.reg()])
                self.stt(T2.t[:, :], po[:, 256:512], NLAM, RZ.t[:, 256:512], ALU.mult, ALU.mult,
                         [por, RZ.reg(), SM.reg(6)], [T2.reg()])
                self.tt("dve", OO[k].t[:, :], T1.t[:, :], T2.t[:, :], ALU.add, [T1.reg(), T2.reg()], [OO[k].reg()])
                self.tt("dve", OSQ[k].t[:, :], OO[k].t[:, :], OO[k].t[:, :], ALU.mult, [OO[k].reg()], [OSQ[k].reg()])
                return (s, hh, k)

            def fin_part2(f):
                s, hh, k = f
                tis = s // 2 if s < 4 else 2
                h = 4 * hg + hh
                pm, pmr = self.banks[3]
                self.mm(pm[:, 0:256], self.ONES.t[:, :], OSQ[k].t[:, :], True, True, [self.ONES.reg(), OSQ[k].reg()], [pmr])
                self.act(RSO.t[:, :], pm[:, 0:256], AF.Ln, [pmr, CST.reg()], [RSO.reg()], bias=CST.t[:, 0:1], scale=1.0 / 128)
                self.act(RSO.t[:, :], RSO.t[:, :], AF.Exp, [RSO.reg()], [RSO.reg()], scale=-0.5)
                if hg == 0:
                    dst, dreg = AO0.t[:, hh, s * SL:(s + 1) * SL], AO0.reg(hh, tis)
                else:
                    dst, dreg = H.t[:, h, s * SL:(s + 1) * SL], H.reg(h, s)
                self.stt(dst, OO[k].t[:, :], SG1, RSO.t[:, :], ALU.mult, ALU.mult, [OO[k].reg(), SM.reg(7), RSO.reg()], [dreg])

            nj = len(jobs)
            LA = DBG.get("la", 2)
            if DBG.get("noscore"):
                nj = 0
            pending = []
            for n in range(min(LA, nj)):
                job_scores(n)
            for n in range(nj):
                if n + LA < nj:
                    job_scores(n + LA)
                job_rest(n)
                while pending and pending[0][1] <= n:
                    fin_part2(pending.pop(0)[0])
                if jobs[n][4]:
                    while len(pending) >= 1:
                        fin_part2(pending.pop(0)[0])
                    pending.append((fin_part1(jobs[n][0], jobs[n][1]), n + 9))
            while pending:
                fin_part2(pending.pop(0)[0])
        for half in range(2):
            W = self.wload(self.awo[half], [128, 8, 512])
            for cc4 in range(4):
                c = half * 4 + cc4
                for ti, (t0, tn) in enumerate(TT):
                    pt, pr = self.ps()
                    for kc in range(8):
                        if kc < 4:
                            rhs, rreg = AO0.t[:, kc, t0:t0 + tn], AO0.reg(kc, ti)
                        else:
                            rhs, rreg = H.t[:, kc, t0:t0 + tn], hregs(H, kc, ti)
                        self.mm(pt[:, 0:tn], W.t[:, kc, cc4 * 128:(cc4 + 1) * 128], rhs, kc == 0, kc == 7,
                                [W.reg(), rreg], [pr])
                    self.resid_add(pt, pr, c, ti, 2, MOD)
            W.closed = True
        ar.pop()


def _fm(x_tok):
    T, F = x_tok.shape
    return np.ascontiguousarray(x_tok.reshape(T, F // 128, 128).transpose(2, 1, 0))


def core_slots(core):
    if core < 2:
        return [("s", core, q) for q in range(4)] + [("p", 30 + core, 0)]
    return [("p", 5 * (core - 2) + s, 0) for s in range(NS)]


def prep_shared(inp):
    f = np.float32
    sh = {}
    w_mod = inp["w_mod"]
    sh["wmod"] = np.ascontiguousarray(
        w_mod.reshape(4, 8, 128, 12, 512).transpose(0, 3, 2, 1, 4)).astype(f)
    bm = inp["b_mod"].reshape(4, 48, 128).transpose(2, 0, 1)
    sh["bmodF"] = np.ascontiguousarray(np.repeat(bm[:, :, :, None], NS, axis=3).reshape(128, 4 * 240)).astype(f)
    ng = inp["norm_g"].reshape(4, 2, 8, 128).transpose(3, 0, 1, 2)
    sh["ngr"] = np.ascontiguousarray(np.repeat(ng[:, :, :, :, None], NS, axis=4).reshape(128, 4 * 2 * 40)).astype(f)
    lw = inp["lru_w_in"]
    gbw = lw[:, :, :D_RNN].reshape(2, 8, 128, 10, 128)
    rcw = lw[:, :, D_RNN:].reshape(2, 8, 128, 10, 128)
    both = np.stack([gbw, rcw], axis=4)
    sh["lwin"] = np.ascontiguousarray(both.transpose(0, 3, 2, 1, 4, 5).reshape(2, 10, 128, 8, 256)).astype(f)
    sh["lcw"] = np.ascontiguousarray(inp["lru_conv_w"].reshape(2, 4, 10, 128).transpose(3, 0, 2, 1).reshape(128, 80)).astype(f)
    sh["lcb"] = np.ascontiguousarray(inp["lru_conv_b"].reshape(2, 10, 128).transpose(2, 0, 1).reshape(128, 20)).astype(f)
    sh["lwg"] = np.ascontiguousarray(inp["lru_w_gate"].transpose(0, 4, 3, 1, 2, 5).reshape(2, 128, 40, 128)).astype(f)
    sh["lbg"] = np.ascontiguousarray(inp["lru_b_gate"].transpose(4, 0, 3, 1, 2).reshape(128, 80)).astype(f)
    sh["llam"] = np.ascontiguousarray(inp["lru_lambda"].reshape(2, 2, 10, 128).transpose(3, 0, 1, 2).reshape(128, 40)).astype(f)
    sh["lwout"] = np.ascontiguousarray(inp["lru_w_out"].reshape(2, 10, 128, 2, 512).transpose(0, 3, 2, 1, 4)).astype(f)
    fu = inp["ffn_w_up"]
    g = fu[:, :, :D_FF].reshape(4, 8, 128, NI, 128)
    u = fu[:, :, D_FF:].reshape(4, 8, 128, NI, 128)
    both = np.stack([g, u], axis=4)
    sh["fwu"] = np.ascontiguousarray(both.transpose(0, 3, 2, 1, 4, 5).reshape(4, NI, 128, 8, 256)).astype(f)
    sh["fcw"] = np.ascontiguousarray(inp["ffn_conv_w"].reshape(4, 3, 44, 128).transpose(3, 0, 2, 1).reshape(128, 4 * 44 * 3)).astype(f)
    sh["fcb"] = np.ascontiguousarray(inp["ffn_conv_b"].reshape(4, 44, 128).transpose(2, 0, 1).reshape(128, 4 * 44)).astype(f)
    sh["fwd"] = np.ascontiguousarray(inp["ffn_w_down"].reshape(4, NI, 128, 4, 256).transpose(0, 3, 2, 1, 4)).astype(f)
    cw = inp["cmlp_w_in"][0]
    sh["cwu"] = np.ascontiguousarray(cw[:, :D_B].reshape(8, 128, 8, 256).transpose(2, 1, 0, 3)).astype(f)
    sh["cbu"] = np.ascontiguousarray(inp["cmlp_b_in"][0, :D_B].reshape(16, 128).T).astype(f)
    sh["cwv"] = np.ascontiguousarray(cw[:, D_B:].reshape(8, 128, 4, 512).transpose(2, 1, 0, 3)).astype(f)
    sh["cbv"] = np.ascontiguousarray(inp["cmlp_b_in"][0, D_B:].reshape(1, D_B)).astype(f)
    sh["cng"] = np.ascontiguousarray(np.broadcast_to(inp["cmlp_norm_g"][0][None, :], (128, D_B))).astype(f)
    sh["cwsT"] = np.ascontiguousarray(inp["cmlp_w_s"][0].transpose(2, 0, 1)).astype(f)
    sh["cbs"] = np.ascontiguousarray(inp["cmlp_b_s"][0].reshape(1, 1024)).astype(f)
    sh["cwo"] = np.ascontiguousarray(inp["cmlp_w_out"][0].reshape(16, 128, 2, 512).transpose(2, 1, 0, 3)).astype(f)
    aw = inp["attn_w_qkv"][0]
    sh["awqk"] = np.ascontiguousarray(aw[:, :2048].reshape(8, 128, 8, 256).transpose(2, 1, 0, 3)).astype(f)
    sh["awv"] = np.ascontiguousarray(aw[:, 2048:].reshape(8, 128, 2, 512).transpose(2, 1, 0, 3)).astype(f)
    qg = inp["attn_qk_g"][0]
    sh["aqg"] = np.ascontiguousarray(np.concatenate([qg, qg], axis=1).T).astype(f)
    sh["alam"] = np.ascontiguousarray(np.broadcast_to(inp["attn_lambda"][0].reshape(1, 256), (128, 256))).astype(f)
    sh["asg"] = np.ascontiguousarray(inp["attn_subln_g"][0].reshape(128, 1)).astype(f)
    sh["awo"] = np.ascontiguousarray(inp["attn_w_out"][0].reshape(8, 128, 2, 512).transpose(2, 1, 0, 3)).astype(f)
    rm = np.zeros((128, 128), f)
    for m_ in range(128):
        d_ = m_ % 64
        rm[m_ + 32 if d_ < 32 else m_ - 32, m_] = 1.0
    sh["rm"] = rm
    cst = np.zeros((128, 8), f)
    cst[:, 0] = EPS
    cst[:, 1] = 1.0
    sh["cst"] = cst
    return sh


def prep_core(inp, core):
    f = np.float32
    slots = core_slots(core)
    toks = []
    cond = []
    for (kind, b, q) in slots:
        if kind == "s":
            toks.append(inp["x_sample"][b, q * SL:(q + 1) * SL])
            cond.append(inp["c"][b])
        else:
            toks.append(inp["x_prompt"][b])
            cond.append(inp["c_ctx"])
    x_tok = np.concatenate(toks, axis=0)
    m = {}
    m["xT"] = _fm(x_tok).astype(f)
    cnd = np.stack(cond, axis=0)
    m["condT"] = np.ascontiguousarray(cnd.reshape(NS, 8, 128).transpose(2, 1, 0).reshape(128, 8 * NS)).astype(f)
    mj = np.array([1.0 if (slots[s][0] == "s" and slots[s + 1][0] == "s") else 0.0 for s in range(4)], f)
    mk = np.zeros((128, 32), f)
    mk[:, 1:5] = mj
    mk[:, 5:9] = mj
    mk[:, 10:14] = mj
    mk[:, 14:22] = np.repeat(mj, 2)
    m["mk"] = mk
    h0 = np.zeros((128, 2, 10, 2, NS), f)
    if core < 2:
        st = inp["state_lru"][core]
        for a in range(2):
            h0[:, a, :, 0, 0] = st[a, 0].reshape(10, 128).T
            h0[:, a, :, 1, 3] = st[a, 1].reshape(10, 128).T
    m["h0"] = h0.reshape(128, -1)
    ckT = np.zeros((128, 8, 512), f)
    cv = np.zeros((128, 4, 1024), f)
    cosT = np.ones((128, NT), f)
    sinT = np.zeros((128, NT), f)
    abias = np.zeros((128, 24), f)
    if core < 2:
        ck = inp["cache_k"][core, 0]
        ckT[:] = ck.reshape(512, 8, 128).transpose(2, 1, 0)
        cv[:] = inp["cache_v"][core, 0].reshape(4, 128, 1024).transpose(1, 0, 2)
        T = 4 * SL
        row = (np.arange(T) // 64).astype(f)
        col = (np.arange(T) % 64).astype(f)
        inv = (np.float32(10000.0) ** (-np.arange(16, dtype=f) / np.float32(16))).astype(f)
        ang = np.concatenate([row[:, None] * inv[None, :], col[:, None] * inv[None, :]], axis=1).astype(f)
        cs, sn = np.cos(ang).astype(f), np.sin(ang).astype(f)
        for p in range(128):
            d_ = p % 64
            cosT[p, :T] = cs[:, d_ % 32]
            sinT[p, :T] = -sn[:, d_ % 32] if d_ < 32 else sn[:, d_ % 32]
    else:
        for s_ in range(4):
            for jp in range(6):
                if jp != 2 + s_:
                    abias[:, s_ * 6 + jp] = NEG
    m["ckT"], m["cv"], m["cosT"], m["sinT"], m["abias"] = ckT, cv, cosT, sinT, abias
    return m


_CACHE = {}


def get_program(n_layers):
    if n_layers not in _CACHE:
        b = Builder(n_layers)
        counts = b.build()
        _CACHE[n_layers] = (b, counts)
    return _CACHE[n_layers]


def kernel(**inp):
    inp = {k: np.asarray(v) for k, v in inp.items()}
    b, counts = get_program(N_LAYERS)
    sh = prep_shared(inp)
    in_maps = []
    for core in range(N_CORES):
        m = dict(sh)
        m.update(prep_core(inp, core))
        in_maps.append({k: m[k] for k in b.din})
    res = run_bass_kernel_spmd(b.nc, in_maps, core_ids=list(range(N_CORES)))
    outs = res.results
    B, S = inp["x_prompt"].shape[0], inp["x_prompt"].shape[1]
    y_prompt = np.zeros((B, S, D), np.float32)
    y_sample = np.zeros(inp["x_sample"].shape, np.float32)
    new_lru = np.zeros((B, 2, 2, D_RNN), np.float32)
    new_k = np.zeros((B, 1, S, 8, 2, 64), np.float32)
    new_v = np.zeros((B, 1, S, 8, 128), np.float32)
    for core in range(N_CORES):
        r = outs[core]
        y = r["yT"].transpose(2, 1, 0).reshape(NT, D)
        st = r["st"].reshape(2, 128, 10, 2, NS)
        for s, (kind, bi, q) in enumerate(core_slots(core)):
            if kind == "s":
                y_sample[bi, q * SL:(q + 1) * SL] = y[s * SL:(s + 1) * SL]
            else:
                y_prompt[bi] = y[s * SL:(s + 1) * SL]
                new_lru[bi] = st[:, :, :, :, s].transpose(0, 3, 2, 1).reshape(2, 2, D_RNN)
                if "kT" in r:
                    kk = r["kT"].transpose(2, 1, 0)
                    new_k[bi, 0] = kk[s * SL:(s + 1) * SL].reshape(SL, 8, 2, 64)
                    vv = r["vO"].transpose(1, 0, 2).reshape(NT, 1024)
                    new_v[bi, 0] = vv[s * SL:(s + 1) * SL].reshape(SL, 8, 128)
    return (y_prompt, y_sample, new_lru, new_k, new_v)
```

```python
import contextlib
import math
import numpy as np
import concourse.bass as bass
import concourse.mybir as mybir
from concourse.bass_utils import run_bass_kernel_spmd

F32 = mybir.dt.float32
BF16 = mybir.dt.bfloat16
AF = mybir.ActivationFunctionType
ALU = mybir.AluOpType

N_CORES = 8
D = 1024
NT = 1280
NS = 5
SL = 256
TT = [(0, 512), (512, 512), (1024, 256)]
D_RNN = 1280
D_B = 2048
D_FF = 2816
NI = 22
EPS = 1e-6
N_LAYERS = 4
SAME_ENGINE_SYNC = True
DEBUG_MODE = None
DBG = {}
NEG = -30000.0


class Reg:
    __slots__ = ("last_write", "readers")

    def __init__(self, inherit=()):
        self.last_write = None
        self.readers = list(inherit)


class Instr:
    __slots__ = ("eng", "fn", "deps", "is_dma", "dma_sem", "dma_val", "need_inc", "inc_val")

    def __init__(self, eng, fn, is_dma=False):
        self.eng = eng
        self.fn = fn
        self.deps = set()
        self.is_dma = is_dma
        self.dma_sem = None
        self.dma_val = 0
        self.need_inc = False
        self.inc_val = 0


ENGINES = ["pe", "act", "dve", "pool", "sp"]


def _flat(regs, out):
    for r in regs:
        if r is None:
            continue
        if isinstance(r, Reg):
            out.append(r)
        else:
            _flat(r, out)
    return out


class Prog:
    def __init__(self, nc):
        self.nc = nc
        self.instrs = []
        self.streams = {}

    def _track(self, ins, reads, writes):
        reads = _flat(reads, [])
        writes = _flat(writes, [])
        for r in reads:
            if r.last_write is not None:
                ins.deps.add(r.last_write)
        for r in writes:
            if r.last_write is not None:
                ins.deps.add(r.last_write)
            for rd in r.readers:
                ins.deps.add(rd)
        for r in reads:
            r.readers.append(ins)
        for r in writes:
            r.last_write = ins
            r.readers = []
        ins.deps.discard(ins)

    def op(self, eng, fn, reads=(), writes=()):
        ins = Instr(eng, fn)
        self._track(ins, reads, writes)
        self.instrs.append(ins)
        return ins

    def dma(self, eng, stream, fn, reads=(), writes=()):
        ins = Instr(eng, fn, is_dma=True)
        st = self.streams.setdefault(stream, [0, None])
        st[0] += 16
        ins.dma_sem = stream
        ins.dma_val = st[0]
        if st[1] is not None:
            ins.deps.add(st[1])
        st[1] = ins
        self._track(ins, reads, writes)
        self.instrs.append(ins)
        return ins

    def emit(self, final_wait_eng="sp"):
        nc = self.nc
        for ins in self.instrs:
            for d in ins.deps:
                if not d.is_dma:
                    if d.eng != ins.eng or (SAME_ENGINE_SYNC and d.eng != "pe"):
                        d.need_inc = True
        counts = {e: 0 for e in ENGINES}
        per_eng = {e: [] for e in ENGINES}
        for ins in self.instrs:
            per_eng[ins.eng].append(ins)
            if not ins.is_dma and ins.need_inc:
                counts[ins.eng] += 1
                ins.inc_val = counts[ins.eng]
        with contextlib.ExitStack() as es:
            esem = {e: es.enter_context(nc.semaphore("s_" + e)) for e in ENGINES}
            ssem = {s: es.enter_context(nc.semaphore("d_" + s)) for s in self.streams}
            block = es.enter_context(nc.Block())
            handles = {"pe": block.tensor, "act": block.scalar, "dve": block.vector,
                       "pool": block.gpsimd, "sp": block.sync}

            def make_body(e):
                def body(eng):
                    waited = {}
                    for ins in per_eng[e]:
                        need = {}
                        for d in ins.deps:
                            if d.is_dma:
                                key = ("d", d.dma_sem)
                                val = d.dma_val
                            else:
                                if d.eng == e and (e == "pe" or not SAME_ENGINE_SYNC):
                                    continue
                                key = ("e", d.eng)
                                val = d.inc_val
                            if val > need.get(key, 0):
                                need[key] = val
                        for key, val in need.items():
                            if waited.get(key, 0) >= val:
                                continue
                            waited[key] = val
                            sem = ssem[key[1]] if key[0] == "d" else esem[key[1]]
                            eng.wait_ge(sem, val)
                        bi = ins.fn(eng)
                        if ins.is_dma:
                            bi.then_inc(ssem[ins.dma_sem], 16)
                        elif ins.need_inc:
                            bi.then_inc(esem[e], 1)
                    if e == final_wait_eng:
                        for s, (cnt, _) in self.streams.items():
                            if waited.get(("d", s), 0) < cnt:
                                eng.wait_ge(ssem[s], cnt)
                return body

            for e in ENGINES:
                if per_eng[e] or e == final_wait_eng:
                    handles[e](make_body(e))
        return {e: len(per_eng[e]) for e in ENGINES}


class Buf:
    def __init__(self, t, off, nbytes, inherit):
        self.t = t
        self.off = off
        self.nbytes = nbytes
        self.inherit = inherit
        self._regs = {}
        self.closed = False

    def reg(self, *key):
        r = self._regs.get(key)
        if r is None:
            r = Reg(self.inherit)
            self._regs[key] = r
        return r

    def all_instrs(self):
        out = list(self.inherit)
        for r in self._regs.values():
            if r.last_write is not None:
                out.append(r.last_write)
            out.extend(r.readers)
        return out


_DT_SIZE = {F32: 4, BF16: 2}


class Arena:
    def __init__(self, nc, base, size, name):
        self.nc, self.base, self.size, self.name = nc, base, size, name
        self.cur = 0
        self.frames = []
        self.live = []
        self.dead = []
        self.n = 0
        self.hi = 0

    def push(self):
        self.frames.append((self.cur, len(self.live)))

    def pop(self):
        cur, nlive = self.frames.pop()
        for b in self.live[nlive:]:
            self.dead.append((b.off, b.off + b.nbytes, b.all_instrs()))
        del self.live[nlive:]
        self.cur = cur

    def alloc(self, shape, dtype):
        n = 1
        for s in shape[1:]:
            n *= s
        nbytes = (n * _DT_SIZE[dtype] + 31) // 32 * 32
        off = self.cur
        assert off + nbytes <= self.size, (self.name, off, nbytes, self.size)
        self.cur += nbytes
        self.hi = max(self.hi, self.cur)
        inherit = []
        keep = []
        for (a, b, ins) in self.dead:
            if a < off + nbytes and off < b:
                inherit.extend(ins)
                if not (off <= a and b <= off + nbytes):
                    keep.append((a, b, ins))
            else:
                keep.append((a, b, ins))
        self.dead = keep
        self.n += 1
        t = self.nc.alloc_sbuf_tensor_at("%s%d" % (self.name, self.n), list(shape), dtype,
                                         offset=self.base + off)
        buf = Buf(t, off, nbytes, list(dict.fromkeys(inherit)))
        buf.abs_off = self.base + off
        self.live.append(buf)
        return buf


def buf_view(nc, buf, shape, dtype, name):
    return nc.alloc_sbuf_tensor_at(name, list(shape), dtype, offset=buf.abs_off)


class Ring:
    def __init__(self, nc, base, size):
        self.nc, self.base, self.size = nc, base, size
        self.cur = 0
        self.bufs = []
        self.n = 0

    def reset(self):
        self.cur = 0

    def alloc(self, shape, dtype):
        n = 1
        for s in shape[1:]:
            n *= s
        nbytes = (n * _DT_SIZE[dtype] + 31) // 32 * 32
        assert nbytes <= self.size
        if self.cur + nbytes > self.size:
            self.cur = 0
        off = self.cur
        self.cur += nbytes
        inherit = []
        keep = []
        for b in self.bufs:
            if b.off < off + nbytes and off < b.off + b.nbytes:
                assert b.closed, "ring overwrite of a live weight buffer"
                inherit.extend(b.all_instrs())
            else:
                keep.append(b)
        self.bufs = keep
        self.n += 1
        t = self.nc.alloc_sbuf_tensor_at("wr%d" % self.n, list(shape), dtype, offset=self.base + off)
        buf = Buf(t, off, nbytes, list(dict.fromkeys(inherit)))
        self.bufs.append(buf)
        return buf


def slots_of(ti):
    return (ti * 2, 2) if ti < 2 else (4, 1)


def hregs(H, c, ti):
    s0, ns = slots_of(ti)
    return [H.reg(c, s0 + i) for i in range(ns)]


class Builder:
    def __init__(self, n_layers):
        self.n_layers = n_layers
        nc = bass.Bass("TRN2", target_bir_lowering=False)
        self.nc = nc
        self.P = Prog(nc)
        self.din = {}
        self.dout = {}
        self.nstream = 0
        total = 207 * 1024
        B0 = 16928
        self.pers = Arena(nc, B0, 76 * 1024, "ps")
        self.ring = Ring(nc, B0 + 76 * 1024, 32 * 1024)
        self.arena = Arena(nc, B0 + 108 * 1024, total - 108 * 1024, "ar")
        self.banks = []
        for i in range(8):
            t = nc.alloc_psum_tensor("bank%d" % i, [128, 512], F32)
            self.banks.append((t, Reg()))
        self.bank_i = 0
        self.pool_i = {}

    def inp(self, name, shape):
        ap = self.nc.dram_tensor(name, list(shape), F32, kind="ExternalInput").ap()
        self.din[name] = ap
        return ap

    def outp(self, name, shape):
        ap = self.nc.dram_tensor(name, list(shape), F32, kind="ExternalOutput").ap()
        self.dout[name] = ap
        return ap

    def ps(self):
        b = self.banks[self.bank_i]
        self.bank_i = (self.bank_i + 1) % 7
        return b

    def ps_pool(self, key, idxs):
        i = self.pool_i.get(key, 0)
        self.pool_i[key] = i + 1
        return self.banks[idxs[i % len(idxs)]]

    def mm(self, out, lhsT, rhs, start, stop, rd, wr):
        self.P.op("pe", lambda e: e.matmul(out, lhsT=lhsT, rhs=rhs, start=start, stop=stop), rd, wr)

    def act(self, out, in_, func, rd, wr, bias=None, scale=None):
        kw = {}
        if bias is not None:
            kw["bias"] = bias
        if scale is not None:
            kw["scale"] = scale
        self.P.op("act", lambda e: e.activation(out=out, in_=in_, func=func, **kw), rd, wr)

    def tt(self, eng, out, in0, in1, op, rd, wr):
        self.P.op(eng, lambda e: e.tensor_tensor(out=out, in0=in0, in1=in1, op=op), rd, wr)

    def ts(self, eng, out, in0, s1, s2, op0, op1, rd, wr):
        if op1 is None:
            self.P.op(eng, lambda e: e.tensor_scalar(out=out, in0=in0, scalar1=s1, scalar2=None, op0=op0), rd, wr)
        else:
            self.P.op(eng, lambda e: e.tensor_scalar(out=out, in0=in0, scalar1=s1, scalar2=s2, op0=op0, op1=op1), rd, wr)

    def stt(self, out, in0, scalar, in1, op0, op1, rd, wr):
        self.P.op("dve", lambda e: e.scalar_tensor_tensor(out=out, in0=in0, scalar=scalar, in1=in1,
                                                          op0=op0, op1=op1), rd, wr)

    def copy(self, eng, out, in_, rd, wr):
        self.P.op(eng, lambda e: e.tensor_copy(out=out, in_=in_), rd, wr)

    def memset(self, eng, ap, val, wr):
        self.P.op(eng, lambda e: e.memset(ap, val), (), wr)

    def scan(self, out, d0, d1, rd, wr):
        self.P.op("dve", lambda e: e.tensor_tensor_scan(out=out, data0=d0, data1=d1, initial=0.0,
                                                        op0=ALU.mult, op1=ALU.add), rd, wr)

    def load(self, out, in_, wr, eng="sp"):
        self.nstream += 1
        self.P.dma(eng, "l%d" % (self.nstream % 8), lambda e: e.dma_start(out=out, in_=in_), (), wr)

    def store(self, out, in_, rd, stream):
        self.P.dma("sp", stream, lambda e: e.dma_start(out=out, in_=in_), rd, ())

    def wload(self, src, shape):
        buf = self.ring.alloc(shape, BF16)
        t = buf.t
        if len(shape) == 3:
            o = t[:, :, :]
        else:
            o = t[:, :]
        self.P.dma("pool", "wr%d" % (self.ring.n % 8), lambda e: e.dma_start(out=o, in_=src), (), [buf.reg()])
        return buf

    def build(self):
        nc = self.nc
        nl = self.n_layers
        pers = self.pers
        xT = self.inp("xT", [128, 8, NT])
        condT = self.inp("condT", [128, 8 * NS])
        mk = self.inp("mk", [128, 32])
        cst = self.inp("cst", [128, 8])
        h0 = self.inp("h0", [128, 2 * 10 * 2 * NS])
        wmod = self.inp("wmod", [4, 12, 128, 8, 512])
        bmodF = self.inp("bmodF", [128, 4 * 240])
        ngr = self.inp("ngr", [128, 4 * 2 * 40])
        lwin = self.inp("lwin", [2, 10, 128, 8, 256])
        lcw = self.inp("lcw", [128, 2 * 10 * 4])
        lcb = self.inp("lcb", [128, 2 * 10])
        lwg = self.inp("lwg", [2, 128, 40, 128])
        lbg = self.inp("lbg", [128, 2 * 10 * 4])
        llam = self.inp("llam", [128, 2 * 2 * 10])
        lwout = self.inp("lwout", [2, 2, 128, 10, 512])
        fwu = self.inp("fwu", [4, NI, 128, 8, 256])
        fcw = self.inp("fcw", [128, 4 * 44 * 3])
        fcb = self.inp("fcb", [128, 4 * 44])
        fwd = self.inp("fwd", [4, 4, 128, NI, 256])
        self.wmod, self.lwin, self.lwg, self.lwout, self.fwu, self.fwd = wmod, lwin, lwg, lwout, fwu, fwd
        if nl >= 2:
            self.cwu = self.inp("cwu", [8, 128, 8, 256])
            self.cbu = self.inp("cbu", [128, 16])
            self.cwv = self.inp("cwv", [4, 128, 8, 512])
            self.cbv = self.inp("cbv", [1, 2048])
            self.cng = self.inp("cng", [128, 2048])
            self.cwsT = self.inp("cwsT", [128, 8, 128])
            self.cbs = self.inp("cbs", [1, 1024])
            self.cwo = self.inp("cwo", [2, 128, 16, 512])
        if nl >= 3:
            self.awqk = self.inp("awqk", [8, 128, 8, 256])
            self.awv = self.inp("awv", [2, 128, 8, 512])
            self.aqg = self.inp("aqg", [128, 2])
            self.alam = self.inp("alam", [128, 256])
            self.asg = self.inp("asg", [128, 1])
            self.awo = self.inp("awo", [2, 128, 8, 512])
            self.ckT = self.inp("ckT", [128, 8, 512])
            self.cv = self.inp("cv", [128, 4, 1024])
            self.cosT = self.inp("cosT", [128, NT])
            self.sinT = self.inp("sinT", [128, NT])
            self.rm = self.inp("rm", [128, 128])
            self.abias = self.inp("abias", [128, 24])
            self.kTo = self.outp("kT", [128, 8, NT])
            self.vOo = self.outp("vO", [128, 10, 1024])
        yT = self.outp("yT", [128, 8, NT])
        stO = self.outp("st", [2, 128, 10 * 2 * NS])
        self.stO = stO

        X = pers.alloc([128, 8, NT], F32)
        H = pers.alloc([128, 8, NT], BF16)
        self.X, self.H = X, H
        SCT = pers.alloc([128, 8 * NS], BF16)
        CND = pers.alloc([128, 8 * NS], F32)
        MK = pers.alloc([128, 32], F32)
        CST = pers.alloc([128, 8], F32)
        H0 = pers.alloc([128, 2, 10, 2, NS], F32)
        BMOD = pers.alloc([128, 4, 240], F32)
        NGR = pers.alloc([128, 4, 2, 40], F32)
        MOD = [pers.alloc([128, 48, NS], F32) for _ in range(2)]
        GS = [pers.alloc([128, 2, 40], F32) for _ in range(2)]
        LCW = pers.alloc([128, 2, 10, 4], F32)
        LCB = pers.alloc([128, 2, 10], F32)
        LBG = pers.alloc([128, 2, 10, 4], F32)
        LLAM = pers.alloc([128, 40], F32)
        NSP8 = pers.alloc([128, 2, 2, 10], F32)
        NSP16 = pers.alloc([128, 2, 2, 10], F32)
        self.NSP16 = NSP16
        LT = [pers.alloc([128, 40], F32) for _ in range(3)]
        FCW = pers.alloc([128, 4, 44, 3], F32)
        FCB = pers.alloc([128, 4, 44], F32)
        ONES = pers.alloc([128, 128], BF16)
        self.MK, self.CST, self.H0, self.MOD, self.GS = MK, CST, H0, MOD, GS
        self.LCW, self.LCB, self.LBG, self.NSP8, self.FCW, self.FCB = LCW, LCB, LBG, NSP8, FCW, FCB
        self.ONES = ONES
        self.SCT, self.BMOD, self.NGR = SCT, BMOD, NGR

        def flat2(buf):
            t = buf.t
            nd = len(t.shape)
            if nd == 2:
                return t[:, :]
            names = " ".join("a%d" % i for i in range(nd - 1))
            return t[tuple([slice(None)] * nd)].rearrange("p %s -> p (%s)" % (names, names))
        self.flat2 = flat2

        for buf, src in [(CND, condT), (MK, mk), (CST, cst), (H0, h0), (BMOD, bmodF), (NGR, ngr), (LCW, lcw),
                         (LCB, lcb), (LBG, lbg), (LLAM, llam), (FCW, fcw), (FCB, fcb)]:
            self.load(flat2(buf), src, [buf.reg()])
        for c in range(8):
            self.load(X.t[:, c, :], xT[:, c, :], [X.reg(c, 0), X.reg(c, 1), X.reg(c, 2)])
        self.memset("dve", ONES.t[:, :], 1.0, [ONES.reg()])
        self.act(SCT.t[:, :], CND.t[:, :], AF.Silu, [CND.reg()], [SCT.reg()])
        a0, a1, a2 = [b_.t[:, :] for b_ in LT]
        r0, r1, r2 = [b_.reg() for b_ in LT]
        self.act(a0, LLAM.t[:, :], AF.Abs, [LLAM.reg()], [r0])
        self.act(a1, a0, AF.Exp, [r0], [r1], scale=-1.0)
        self.act(a0, a1, AF.Ln, [r1, CST.reg()], [r0], bias=CST.t[:, 1:2])
        self.ts("dve", a2, LLAM.t[:, :], -1.0, 0.0, ALU.mult, ALU.max, [LLAM.reg()], [r2])
        self.tt("dve", a1, a0, a2, ALU.add, [r0, r2], [r1])
        self.ts("dve", flat2(NSP8), a1, -8.0, None, ALU.mult, None, [r1], [NSP8.reg()])
        self.ts("dve", flat2(NSP16), a1, -16.0, None, ALU.mult, None, [r1], [NSP16.reg()])

        self.ad = None
        if DEBUG_MODE == "attn":
            self.adaln_begin(2)
            self.adaln_flush()
            self.modulate(2, 0)
            self.attn(2, 0)
            nl = 0
        else:
            self.adaln_begin(0)
            for _ in range(6):
                self.bg()
        for l in range(nl):
            kind = l % 3
            j = l // 3
            self.modulate(l, 0)
            if kind == 0:
                self.lru(l, j)
            elif kind == 1:
                self.cmlp(l, j)
            else:
                self.attn(l, j)
            self.adaln_flush()
            self.modulate(l, 1)
            if l + 1 < nl:
                self.adaln_begin(l + 1)
            self.ffn(l)
            self.adaln_flush()

        for c in range(8):
            self.store(yT[:, c, :], X.t[:, c, :], [X.reg(c, 0), X.reg(c, 1), X.reg(c, 2)], "oy%d" % c)
        counts = self.P.emit()
        return counts

    def adaln_begin(self, l):
        assert self.ad is None
        self.ad = dict(l=l, step=0, W=self.wload(self.wmod[l, 0], [128, 8, 512]))

    def bg(self):
        ad = self.ad
        if ad is None:
            return
        l, n4 = ad["l"], ad["step"]
        MOD, GS = self.MOD[l % 2], self.GS[l % 2]
        pt, pr = self.banks[7]
        W = ad["W"]
        if n4 + 1 < 12:
            ad["W"] = self.wload(self.wmod[l, n4 + 1], [128, 8, 512])
        for nn in range(4):
            n = n4 * 4 + nn
            for kc in range(8):
                self.mm(pt[:, n * NS:(n + 1) * NS], W.t[:, kc, nn * 128:(nn + 1) * 128],
                        self.SCT.t[:, kc * NS:(kc + 1) * NS], kc == 0, kc == 7,
                        [W.reg(), self.SCT.reg()], [pr])
        W.closed = True
        ad["step"] += 1
        if ad["step"] in (6, 12):
            hf = ad["step"] // 6 - 1
            modf = self.flat2(MOD)
            cs = slice(hf * 120, (hf + 1) * 120)
            self.tt("dve", modf[:, cs], pt[:, cs], self.BMOD.t[:, l, cs], ALU.add, [pr, self.BMOD.reg()], [MOD.reg(hf)])
            m = 1 + 3 * hf
            self.stt(GS.t[:, hf, :], modf[:, m * 40:(m + 1) * 40], 1.0, self.NGR.t[:, l, hf, :],
                     ALU.add, ALU.mult, [MOD.reg(hf), self.NGR.reg()], [GS.reg(hf)])
        if ad["step"] == 12:
            self.ad = None

    def adaln_flush(self):
        while self.ad is not None:
            self.bg()

    def modulate(self, l, which):
        X, H, MOD, GS = self.X, self.H, self.MOD[l % 2], self.GS[l % 2]
        msh = 3 * which
        CST = self.CST
        ar = self.arena
        ar.push()
        RSTD = ar.alloc([128, NT], F32)
        SQ = ar.alloc([128, 4, 512], BF16)
        XN = ar.alloc([128, 2, NT], F32)
        k = 0
        for ti, (t0, tn) in enumerate(TT):
            pt, pr = self.ps()
            for c in range(8):
                sq = SQ.t[:, k % 4, 0:tn]
                sqr = SQ.reg(k % 4)
                k += 1
                if c % 3 != 2:
                    self.act(sq, X.t[:, c, t0:t0 + tn], AF.Square, [X.reg(c, ti)], [sqr])
                else:
                    self.tt("dve", sq, X.t[:, c, t0:t0 + tn], X.t[:, c, t0:t0 + tn], ALU.mult, [X.reg(c, ti)], [sqr])
                self.mm(pt[:, 0:tn], self.ONES.t[:, :], sq, c == 0, c == 7, [sqr, self.ONES.reg()], [pr])
            rs = RSTD.t[:, t0:t0 + tn]
            rr = RSTD.reg(ti)
            self.act(rs, pt[:, 0:tn], AF.Ln, [pr, CST.reg()], [rr], bias=CST.t[:, 0:1], scale=1.0 / D)
            self.act(rs, rs, AF.Exp, [rr], [rr], scale=-0.5)
        n = 0
        for c in range(8):
            xn = XN.t[:, c % 2, :]
            xnr = XN.reg(c % 2)
            self.tt("dve", xn, X.t[:, c, :], RSTD.t[:, :], ALU.mult,
                    [X.reg(c, 0), X.reg(c, 1), X.reg(c, 2), RSTD.reg(0), RSTD.reg(1), RSTD.reg(2)], [xnr])
            for s_ in range(NS):
                ti = s_ // 2 if s_ < 4 else 2
                gs = GS.t[:, which, c * NS + s_:c * NS + s_ + 1]
                sh = MOD.t[:, msh * 8 + c, s_:s_ + 1]
                o = H.t[:, c, s_ * SL:(s_ + 1) * SL]
                i_ = XN.t[:, c % 2, s_ * SL:(s_ + 1) * SL]
                if n % 5 < 3:
                    self.act(o, i_, AF.Identity, [xnr, GS.reg(which), MOD.reg(which)], [H.reg(c, s_)],
                             bias=sh, scale=gs)
                else:
                    self.ts("dve", o, i_, gs, sh, ALU.mult, ALU.add, [xnr, GS.reg(which), MOD.reg(which)],
                            [H.reg(c, s_)])
                n += 1
        ar.pop()

    def resid_add(self, pt, pr, c, ti, gate_m, MOD):
        X = self.X
        s0, ns = slots_of(ti)
        for si in range(ns):
            s = s0 + si
            xs = X.t[:, c, s * SL:(s + 1) * SL]
            self.stt(xs, pt[:, si * SL:(si + 1) * SL], MOD.t[:, gate_m * 8 + c, s:s + 1], xs,
                     ALU.mult, ALU.add, [pr, MOD.reg(gate_m // 3), X.reg(c, ti)], [X.reg(c, ti)])

    def lru(self, l, a):
        ar = self.arena
        H, MOD, MK, CST = self.H, self.MOD[l % 2], self.MK, self.CST
        ar.push()
        M = ar.alloc([128, 10, NT], BF16)
        GB = ar.alloc([128, NT], BF16)
        RECP = ar.alloc([128, NS, 259], F32)
        HD1t = buf_view(self.nc, RECP, [128, NT], F32, "hd1v%d" % l)
        XC = ar.alloc([128, NT], F32)
        XCB = ar.alloc([128, NT], BF16)
        RA = [[ar.alloc([128, NT], F32) for _ in range(2)] for _ in range(2)]
        IB = [[ar.alloc([128, NT], F32) for _ in range(2)] for _ in range(2)]
        SQb = [ar.alloc([128, NT], F32) for _ in range(2)]
        HD0 = ar.alloc([128, NT], F32)
        ST = ar.alloc([128, 10, 2, NS], F32)
        TB = [ar.alloc([128, NS], F32) for _ in range(2)]
        self.memset("dve", self.flat2(RECP), 0.0, [RECP.reg()])
        xc3 = XC.t[:, :].rearrange("p (s t) -> p s t", t=SL)
        Ws = {}

        def part_a(j):
            W = Ws[j]
            banks = [self.ps() for _ in TT]
            for kc in range(8):
                for ti, (t0, tn) in enumerate(TT):
                    pt, pr = banks[ti]
                    self.mm(pt[:, 0:tn], W.t[:, kc, 0:128], H.t[:, kc, t0:t0 + tn],
                            kc == 0, kc == 7, [W.reg(), hregs(H, kc, ti)], [pr])
            W.closed = True
            for ti, (t0, tn) in enumerate(TT):
                pt, pr = banks[ti]
                self.act(GB.t[:, t0:t0 + tn], pt[:, 0:tn], AF.Gelu_apprx_tanh, [pr], [GB.reg()])

        def part_b(j):
            W = self.wload(self.lwin[a, j], [128, 8, 256])
            Ws[j] = W
            WGj = self.wload(self.lwg[a][:, j * 4:(j + 1) * 4, :], [128, 4, 128])
            banks = [self.ps() for _ in TT]
            for kc in range(8):
                for ti, (t0, tn) in enumerate(TT):
                    pt, pr = banks[ti]
                    self.mm(pt[:, 0:tn], W.t[:, kc, 128:256], H.t[:, kc, t0:t0 + tn],
                            kc == 0, kc == 7, [W.reg(), hregs(H, kc, ti)], [pr])
            for ti, (t0, tn) in enumerate(TT):
                pt, pr = banks[ti]
                s0, ns = slots_of(ti)
                self.act(RECP.t[:, s0:s0 + ns, 2:258], pt[:, 0:tn].rearrange("p (s t) -> p s t", t=SL),
                         AF.Copy, [pr], [RECP.reg()])
            self.memset("dve", RECP.t[:, 0, 0:2], 0.0, [RECP.reg()])
            self.tt("dve", RECP.t[:, 1:5, 0:2], RECP.t[:, 0:4, 256:258],
                    MK.t[:, 14:22].rearrange("p (s t) -> p s t", t=2), ALU.mult, [RECP.reg(), MK.reg()], [RECP.reg()])
            self.tt("dve", RECP.t[:, 0:4, 258:259], RECP.t[:, 1:5, 2:3],
                    MK.t[:, 10:14].rearrange("p (s t) -> p s t", t=1), ALU.mult, [RECP.reg(), MK.reg()], [RECP.reg()])
            cw = lambda k: self.LCW.t[:, a, j, k:k + 1]
            self.act(xc3, RECP.t[:, :, 0:256], AF.Identity, [RECP.reg(), self.LCW.reg(), self.LCB.reg()], [XC.reg()],
                     bias=self.LCB.t[:, a, j:j + 1], scale=cw(0))
            if j >= 1:
                part_s(j - 1)
            for k in range(1, 4):
                self.stt(xc3, RECP.t[:, :, k:k + 256], cw(k), xc3, ALU.mult, ALU.add,
                         [RECP.reg(), self.LCW.reg(), XC.reg()], [XC.reg()])
            self.copy("dve", XCB.t[:, :], XC.t[:, :], [XC.reg()], [XCB.reg()])
            for d in range(2):
                for g in range(2):
                    dst = RA[j % 2][d] if g == 0 else IB[j % 2][d]
                    for ti, (t0, tn) in enumerate(TT):
                        pt, pr = self.ps()
                        self.mm(pt[:, 0:tn], WGj.t[:, d * 2 + g, :], XCB.t[:, t0:t0 + tn], True, True,
                                [WGj.reg(), XCB.reg()], [pr])
                        self.act(dst.t[:, t0:t0 + tn], pt[:, 0:tn], AF.Sigmoid, [pr, self.LBG.reg()], [dst.reg()],
                                 bias=self.LBG.t[:, a, j, d * 2 + g:d * 2 + g + 1])
            WGj.closed = True
            for d in range(2):
                ra = RA[j % 2][d]
                self.act(ra.t[:, :], ra.t[:, :], AF.Exp, [ra.reg(), self.NSP8.reg()], [ra.reg()],
                         scale=self.NSP8.t[:, a, d, j:j + 1])

        def part_d(j):
            for d in range(2):
                ib = IB[j % 2][d]
                self.tt("dve", ib.t[:, :], ib.t[:, :], XC.t[:, :], ALU.mult, [ib.reg(), XC.reg()], [ib.reg()])
            for d in range(2):
                ra = RA[j % 2][d]
                self.tt("dve", SQb[d].t[:, :], ra.t[:, :], ra.t[:, :], ALU.mult, [ra.reg()], [SQb[d].reg()])

        def part_s(j):
            for d in range(2):
                self.act(SQb[d].t[:, :], SQb[d].t[:, :], AF.Sqrt, [SQb[d].reg(), CST.reg()], [SQb[d].reg()],
                         bias=CST.t[:, 1:2], scale=-1.0)

        def part_x(j):
            for d in range(2):
                ib = IB[j % 2][d]
                self.tt("dve", ib.t[:, :], ib.t[:, :], SQb[d].t[:, :], ALU.mult, [ib.reg(), SQb[d].reg()], [ib.reg()])

        def part_c(j):
            for d in range(2):
                ra, ib, tb = RA[j % 2][d], IB[j % 2][d], TB[d]
                e0 = 0 if d == 0 else SL - 1
                af = ra.t[:, e0:NT:SL]
                bf = ib.t[:, e0:NT:SL]
                mcol = MK.t[:, 0:5] if d == 0 else MK.t[:, 5:10]
                self.tt("dve", tb.t[:, :], af, self.H0.t[:, a, j, d, :], ALU.mult, [ra.reg(), self.H0.reg()], [tb.reg()])
                self.tt("dve", bf, bf, tb.t[:, :], ALU.add, [ib.reg(), tb.reg()], [ib.reg()])
                self.tt("dve", af, af, mcol, ALU.mult, [ra.reg(), MK.reg()], [ra.reg()])
                if d == 0:
                    self.scan(HD0.t[:, :], ra.t[:, :], ib.t[:, :], [ra.reg(), ib.reg()], [HD0.reg()])
                    self.copy("dve", ST.t[:, j, 0, :], HD0.t[:, SL - 1:NT:SL], [HD0.reg()], [ST.reg(j)])
                else:
                    self.scan(HD1t[:, ::-1], ra.t[:, ::-1], ib.t[:, ::-1], [ra.reg(), ib.reg()], [RECP.reg()])
                    self.copy("dve", ST.t[:, j, 1, :], HD1t[:, 0:NT:SL], [RECP.reg()], [ST.reg(j)])
            self.tt("dve", HD0.t[:, :], HD0.t[:, :], HD1t[:, :], ALU.add, [HD0.reg(), RECP.reg()], [HD0.reg()])
            self.tt("dve", M.t[:, j, :], HD0.t[:, :], GB.t[:, :], ALU.mult, [HD0.reg(), GB.reg()], [M.reg(j)])

        for it in range(11):
            self.bg()
            if it >= 1:
                part_a(it - 1)
            if it < 10:
                part_b(it)
            if it == 10:
                part_s(9)
            if it >= 1:
                part_x(it - 1)
                part_c(it - 1)
            if it < 10:
                part_d(it)
        self.store(self.stO[a], self.flat2(ST), [ST.reg(j) for j in range(10)], "ost%d" % a)
        for half in range(2):
            W = self.wload(self.lwout[a, half], [128, 10, 512])
            for cc in range(4):
                c = half * 4 + cc
                for ti, (t0, tn) in enumerate(TT):
                    pt, pr = self.ps()
                    for jc in range(10):
                        self.mm(pt[:, 0:tn], W.t[:, jc, cc * 128:(cc + 1) * 128], M.t[:, jc, t0:t0 + tn],
                                jc == 0, jc == 9, [W.reg(), M.reg(jc)], [pr])
                    self.resid_add(pt, pr, c, ti, 2, MOD)
            W.closed = True
        ar.pop()

    def ffn(self, l):
        ar = self.arena
        H, MOD, MK = self.H, self.MOD[l % 2], self.MK
        ar.push()
        A = ar.alloc([128, NI, NT], BF16)
        Z = [[ar.alloc([128, NS, 258], F32) for _ in range(2)] for _ in range(2)]
        C2 = [[ar.alloc([128, NT], F32) for _ in range(2)] for _ in range(2)]
        for zz in Z:
            for z in zz:
                self.memset("dve", self.flat2(z), 0.0, [z.reg()])
        mk1 = MK.t[:, 10:14].rearrange("p (s t) -> p s t", t=1)

        def ffn_tail(i_, part):
            C_ = C2[i_ % 2]
            if part in (0, 2):
                self.act(C_[0].t[:, :], C_[0].t[:, :], AF.Silu, [C_[0].reg()], [C_[0].reg()])
            if part in (1, 2):
                self.tt("dve", A.t[:, i_, :], C_[0].t[:, :], C_[1].t[:, :], ALU.mult, [C_[0].reg(), C_[1].reg()], [A.reg(i_)])

        for i in range(NI):
            W = self.wload(self.fwu[l, i], [128, 8, 256])
            if i >= 2:
                self.bg()
            banks = [[self.ps() for _ in TT] for _ in range(2)]
            for half in range(2):
                for kc in range(8):
                    for ti, (t0, tn) in enumerate(TT):
                        pt, pr = banks[half][ti]
                        self.mm(pt[:, 0:tn], W.t[:, kc, half * 128:(half + 1) * 128], H.t[:, kc, t0:t0 + tn],
                                kc == 0, kc == 7, [W.reg(), hregs(H, kc, ti)], [pr])
            W.closed = True
            C = C2[i % 2]
            for half in range(2):
                z = Z[i % 2][half]
                for ti, (t0, tn) in enumerate(TT):
                    pt, pr = banks[half][ti]
                    s0, ns = slots_of(ti)
                    self.act(z.t[:, s0:s0 + ns, 1:257], pt[:, 0:tn].rearrange("p (s t) -> p s t", t=SL),
                             AF.Copy, [pr], [z.reg()])
            if i >= 1:
                ffn_tail(i - 1, 0)
            for half in range(2):
                z = Z[i % 2][half]
                self.tt("dve", z.t[:, 1:5, 0:1], z.t[:, 0:4, 256:257], mk1, ALU.mult, [z.reg(), MK.reg()], [z.reg()])
                self.tt("dve", z.t[:, 0:4, 257:258], z.t[:, 1:5, 1:2], mk1, ALU.mult, [z.reg(), MK.reg()], [z.reg()])
            if i >= 1:
                ffn_tail(i - 1, 1)
            for half in range(2):
                z = Z[i % 2][half]
                cb = C[half]
                n = half * NI + i
                c3 = cb.t[:, :].rearrange("p (s t) -> p s t", t=SL)
                self.act(c3, z.t[:, :, 0:256], AF.Identity, [z.reg(), self.FCW.reg(), self.FCB.reg()], [cb.reg()],
                         bias=self.FCB.t[:, l, n:n + 1], scale=self.FCW.t[:, l, n, 0:1])
            for half in range(2):
                z = Z[i % 2][half]
                cb = C[half]
                n = half * NI + i
                c3 = cb.t[:, :].rearrange("p (s t) -> p s t", t=SL)
                for k in range(1, 3):
                    self.stt(c3, z.t[:, :, k:k + 256], self.FCW.t[:, l, n, k:k + 1], c3, ALU.mult, ALU.add,
                             [z.reg(), self.FCW.reg(), cb.reg()], [cb.reg()])
        ffn_tail(NI - 1, 2)
        for q in range(4):
            W = self.wload(self.fwd[l, q], [128, NI, 256])
            for cc in range(2):
                c = q * 2 + cc
                for ti, (t0, tn) in enumerate(TT):
                    pt, pr = self.ps()
                    for ic in range(NI):
                        self.mm(pt[:, 0:tn], W.t[:, ic, cc * 128:(cc + 1) * 128], A.t[:, ic, t0:t0 + tn],
                                ic == 0, ic == NI - 1, [W.reg(), A.reg(ic)], [pr])
                    self.resid_add(pt, pr, c, ti, 5, MOD)
            W.closed = True
        ar.pop()

    def cmlp(self, l, j):
        ar = self.arena
        H, MOD, CST = self.H, self.MOD[l % 2], self.CST
        ar.push()
        U = ar.alloc([128, 16, NT], BF16)
        VGs = [ar.alloc([128, 2048], F32) for _ in range(2)]
        VN = [ar.alloc([128, 2048], BF16) for _ in range(2)]
        CNG = ar.alloc([128, 2048], F32)
        WST = ar.alloc([128, 8, 128], BF16)
        CBU = ar.alloc([128, 16], F32)
        CBVH = ar.alloc([1, 2048], BF16)
        CBVL = ar.alloc([1, 2048], BF16)
        CBSH = ar.alloc([1, 1024], BF16)
        CBSL = ar.alloc([1, 1024], BF16)
        JUNK = ar.alloc([128, 512], BF16)
        SS = ar.alloc([128, 16], F32)
        self.load(CNG.t[:, :], self.cng, [CNG.reg()])
        self.load(CBU.t[:, :], self.cbu, [CBU.reg()])
        self.P.dma("pool", "wg", lambda e: e.dma_start(out=WST.t[:, :, :], in_=self.cwsT), (), [WST.reg()])
        stg = [(VGs[0].t[0:1, 0:2048], [VGs[0].reg(c_) for c_ in range(4)], self.cbv, CBVH, CBVL),
               (VGs[1].t[0:1, 0:1024], [VGs[1].reg(c_) for c_ in range(4)], self.cbs, CBSH, CBSL)]
        for sap, sregs, src_, hi_, lo_ in stg:
            self.load(sap, src_, sregs)
            self.act(hi_.t[:, :], sap, AF.Copy, sregs, [hi_.reg()])
            self.tt("dve", lo_.t[:, :], sap, hi_.t[:, :], ALU.subtract, sregs + [hi_.reg()], [lo_.reg()])
        for i in range(8):
            W = self.wload(self.cwu[i], [128, 8, 256])
            banks = [[self.ps() for _ in TT] for _ in range(2)]
            for half in range(2):
                for kc in range(8):
                    for ti, (t0, tn) in enumerate(TT):
                        pt, pr = banks[half][ti]
                        self.mm(pt[:, 0:tn], W.t[:, kc, half * 128:(half + 1) * 128], H.t[:, kc, t0:t0 + tn],
                                kc == 0, kc == 7, [W.reg(), hregs(H, kc, ti)], [pr])
            W.closed = True
            for half in range(2):
                cc = 2 * i + half
                for ti, (t0, tn) in enumerate(TT):
                    pt, pr = banks[half][ti]
                    self.act(U.t[:, cc, t0:t0 + tn], pt[:, 0:tn], AF.Gelu_apprx_tanh, [pr, CBU.reg()],
                             [U.reg(cc, ti)], bias=CBU.t[:, cc:cc + 1])
        self.ring.reset()
        WV = [self.wload(self.cwv[ct], [128, 8, 512]) for ct in range(4)]
        vbanks = {}

        def v_mm(tb):
            ti = min(tb // 4, 2)
            tsl = slice(tb * 128, (tb + 1) * 128)
            vbanks[tb] = []
            for ct in range(4):
                pt, pr = self.ps()
                vbanks[tb].append((pt, pr))
                for kc in range(8):
                    self.mm(pt[:, 0:512], H.t[:, kc, tsl], WV[ct].t[:, kc, :], kc == 0, False,
                            [hregs(H, kc, ti), WV[ct].reg()], [pr])
                self.mm(pt[:, 0:512], self.ONES.t[0:1, :], CBVH.t[0:1, ct * 512:(ct + 1) * 512], False, False,
                        [self.ONES.reg(), CBVH.reg()], [pr])
                self.mm(pt[:, 0:512], self.ONES.t[0:1, :], CBVL.t[0:1, ct * 512:(ct + 1) * 512], False, True,
                        [self.ONES.reg(), CBVL.reg()], [pr])

        def v_act(tb):
            vg = VGs[tb % 2]
            so = 8 * (tb % 2)
            for ct in range(4):
                pt, pr = vbanks[tb][ct]
                self.act(vg.t[:, ct * 512:(ct + 1) * 512], pt[:, 0:512], AF.Gelu_apprx_tanh, [pr], [vg.reg(ct)])
                self.P.op("act", (lambda o, i_, acc: (lambda e: e.activation(out=o, in_=i_, func=AF.Square, accum_out=acc)))(
                    JUNK.t[:, :], vg.t[:, ct * 512:(ct + 1) * 512], SS.t[:, so + ct:so + ct + 1]),
                    [vg.reg(ct)], [JUNK.reg(), SS.reg(so + ct)])

        def n_part(tb):
            vg = VGs[tb % 2]
            so = 8 * (tb % 2)
            self.P.op("dve", (lambda so_: lambda e: e.tensor_reduce(out=SS.t[:, so_ + 4:so_ + 5], in_=SS.t[:, so_:so_ + 4],
                                                                    axis=mybir.AxisListType.X, op=ALU.add))(so),
                      [SS.reg(so + c_) for c_ in range(4)], [SS.reg(so + 4)])
            self.act(SS.t[:, so + 5:so + 6], SS.t[:, so + 4:so + 5], AF.Ln, [SS.reg(so + 4), CST.reg()], [SS.reg(so + 5)],
                     bias=CST.t[:, 0:1], scale=1.0 / D_B)
            self.act(SS.t[:, so + 6:so + 7], SS.t[:, so + 5:so + 6], AF.Exp, [SS.reg(so + 5)], [SS.reg(so + 6)], scale=-0.5)
            vn = VN[tb % 2]
            self.stt(vn.t[:, :], vg.t[:, :], SS.t[:, so + 6:so + 7], CNG.t[:, :], ALU.mult, ALU.mult,
                     [vg.reg(c_) for c_ in range(4)] + [SS.reg(so + 6), CNG.reg()], [vn.reg()])

        def s_part(tb):
            ti = min(tb // 4, 2)
            tsl = slice(tb * 128, (tb + 1) * 128)
            vn = VN[tb % 2]
            for cq in range(4):
                pt, pr = self.ps()
                for c4 in range(4):
                    cc = cq * 4 + c4
                    g = cc // 2
                    o = pt[:, c4 * 128:(c4 + 1) * 128]
                    self.mm(o, vn.t[:, cc * 128:(cc + 1) * 128], WST.t[:, g, :], True, False,
                            [vn.reg(), WST.reg()], [pr])
                    self.mm(o, self.ONES.t[0:1, :], CBSH.t[0:1, g * 128:(g + 1) * 128], False, False,
                            [self.ONES.reg(), CBSH.reg()], [pr])
                    self.mm(o, self.ONES.t[0:1, :], CBSL.t[0:1, g * 128:(g + 1) * 128], False, True,
                            [self.ONES.reg(), CBSL.reg()], [pr])
                uv = U.t[:, cq * 4:(cq + 1) * 4, tsl]
                self.tt("dve", uv, pt[:, 0:512].rearrange("p (c t) -> p c t", t=128), uv, ALU.mult,
                        [pr] + [U.reg(cq * 4 + c_, ti) for c_ in range(4)],
                        [U.reg(cq * 4 + c_, ti) for c_ in range(4)])

        v_mm(0)
        v_act(0)
        for tb in range(10):
            if tb + 1 < 10:
                v_mm(tb + 1)
            n_part(tb)
            if tb + 1 < 10:
                v_act(tb + 1)
            s_part(tb)
        for w in WV:
            w.closed = True
        for half in range(2):
            W = self.wload(self.cwo[half], [128, 16, 512])
            for cc4 in range(4):
                c = half * 4 + cc4
                for ti, (t0, tn) in enumerate(TT):
                    pt, pr = self.ps()
                    for kc in range(16):
                        self.mm(pt[:, 0:tn], W.t[:, kc, cc4 * 128:(cc4 + 1) * 128], U.t[:, kc, t0:t0 + tn],
                                kc == 0, kc == 15, [W.reg(), U.reg(kc, ti)], [pr])
                    self.resid_add(pt, pr, c, ti, 2, MOD)
            W.closed = True
        ar.pop()

    def attn(self, l, j):
        ar = self.arena
        H, MOD, CST = self.H, self.MOD[l % 2], self.CST
        lam_init = 0.8 - 0.6 * math.exp(-0.3 * l)
        ar.push()
        SM = ar.alloc([128, 16], F32)
        ar.push()
        ALAM = ar.alloc([128, 256], F32)
        LP = ar.alloc([128, 128], F32)
        self.load(ALAM.t[:, :], self.alam, [ALAM.reg()])
        self.load(SM.t[:, 0:1], self.asg, [SM.reg(0)])
        self.tt("dve", LP.t[:, 0:64], ALAM.t[:, 0:64], ALAM.t[:, 64:128], ALU.mult, [ALAM.reg()], [LP.reg()])
        self.tt("dve", LP.t[:, 64:128], ALAM.t[:, 128:192], ALAM.t[:, 192:256], ALU.mult, [ALAM.reg()], [LP.reg()])
        self.P.op("dve", lambda e: e.tensor_reduce(out=SM.t[:, 1:3], in_=LP.t[:, :].rearrange("p (a b) -> p a b", b=64),
                                                   axis=mybir.AxisListType.X, op=ALU.add), [LP.reg()], [SM.reg(1)])
        self.act(SM.t[:, 3:5], SM.t[:, 1:3], AF.Exp, [SM.reg(1)], [SM.reg(3)])
        self.tt("dve", SM.t[:, 5:6], SM.t[:, 4:5], SM.t[:, 3:4], ALU.subtract, [SM.reg(3)], [SM.reg(5)])
        self.ts("dve", SM.t[:, 6:7], SM.t[:, 5:6], -lam_init, None, ALU.add, None, [SM.reg(5)], [SM.reg(6)])
        self.ts("dve", SM.t[:, 7:8], SM.t[:, 0:1], 1.0 - lam_init, None, ALU.mult, None, [SM.reg(0)], [SM.reg(7)])
        if not DBG.get("nonest"):
            ar.pop()
        else:
            ar.frames.pop()
        NLAM = SM.t[:, 6:7]
        SG1 = SM.t[:, 7:8]
        QZ = [ar.alloc([128, 4, NT], BF16) for _ in range(2)]
        KT = ar.alloc([128, 4, 512 + NT], BF16)
        VB = ar.alloc([128, 14, 512], BF16)
        AO0 = ar.alloc([128, 4, NT], BF16)
        COS = ar.alloc([128, 512], F32)
        SIN = ar.alloc([128, 512], F32)
        RM = ar.alloc([128, 128], F32)
        BD = ar.alloc([128, 128], BF16)
        SQ2 = [ar.alloc([128, 512], BF16) for _ in range(2)]
        RS2 = [ar.alloc([128, 512], F32) for _ in range(2)]
        QN = [ar.alloc([128, 512], F32) for _ in range(2)]
        T1r = ar.alloc([128, 512], F32)
        T2r = ar.alloc([128, 512], F32)
        VF = ar.alloc([128, 512], F32)
        PT = [ar.alloc([128, 512], BF16) for _ in range(4)]
        RZ = ar.alloc([128, 512], F32)
        T1 = ar.alloc([128, 256], F32)
        T2 = ar.alloc([128, 256], F32)
        OO = [ar.alloc([128, 256], F32) for _ in range(2)]
        OSQ = [ar.alloc([128, 256], BF16) for _ in range(2)]
        RSO = ar.alloc([128, 256], F32)
        AQG = ar.alloc([128, 2], F32)
        ABI = ar.alloc([128, 24], F32)
        for buf, src in [(RM, self.rm), (AQG, self.aqg), (ABI, self.abias)]:
            self.load(buf.t[:, :], src, [buf.reg()])
        for qz in QZ:
            self.memset("dve", self.flat2(qz), 0.0, [qz.reg("z")])
        bde = "pool" if DBG.get("bdpool") else "dve"
        self.memset(bde, BD.t[:, :], 0.0, [BD.reg()])
        self.memset(bde, BD.t[0:64, 0:64], 1.0, [BD.reg()])
        self.memset(bde, BD.t[64:128, 64:128], 1.0, [BD.reg()])
        sc_banks = [0, 1, 2]
        acc_banks = [4, 5, 6, 7]
        for hg in range(2):
            self.ring.reset()
            Wq = [self.wload(self.awqk[2 * hg + i], [128, 8, 256]) for i in range(2)]
            Wk = [self.wload(self.awqk[4 + 2 * hg + i], [128, 8, 256]) for i in range(2)]
            WVh = self.wload(self.awv[hg], [128, 8, 512])
            self.P.dma("pool", "wg", (lambda hg_: lambda e: e.dma_start(out=KT.t[:, :, 0:512], in_=self.ckT[:, 4 * hg_:4 * hg_ + 4, :]))(hg),
                       (), [KT.reg("c")])
            self.P.dma("pool", "wg", (lambda hg_: lambda e: e.dma_start(out=VB.t[:, 0:4, :], in_=self.cv[:, :, 512 * hg_:512 * hg_ + 512]))(hg),
                       (), [VB.reg(kb) for kb in range(4)])
            items = [(ti, isk, hh) for ti in range(3) for isk in range(2) for hh in range(4)]
            st = {}

            def stage_a(n):
                ti, isk, hh = items[n]
                t0, tn = TT[ti]
                if isk == 0 and hh == 0:
                    self.load(COS.t[:, 0:tn], self.cosT[:, t0:t0 + tn], [COS.reg()])
                    self.load(SIN.t[:, 0:tn], self.sinT[:, t0:t0 + tn], [SIN.reg()])
                W = (Wk if isk else Wq)[hh // 2]
                wc = (hh % 2) * 128
                pt, pr = self.ps()
                for kc in range(8):
                    self.mm(pt[:, 0:tn], W.t[:, kc, wc:wc + 128], H.t[:, kc, t0:t0 + tn], kc == 0, kc == 7,
                            [W.reg(), hregs(H, kc, ti)], [pr])
                sq = SQ2[n % 2]
                self.act(sq.t[:, 0:tn], pt[:, 0:tn], AF.Square, [pr], [sq.reg()])
                st[n] = (pt, pr)

            def stage_b(n):
                ti, isk, hh = items[n]
                t0, tn = TT[ti]
                h = 4 * hg + hh
                pt, pr = st[n]
                sq, rs, qn = SQ2[n % 2], RS2[n % 2], QN[n % 2]
                pm, pmr = self.ps()
                self.mm(pm[:, 0:tn], BD.t[:, :], sq.t[:, 0:tn], True, True, [BD.reg(), sq.reg()], [pmr])
                self.act(rs.t[:, 0:tn], pm[:, 0:tn], AF.Ln, [pmr, CST.reg()], [rs.reg()], bias=CST.t[:, 0:1],
                         scale=1.0 / 64)
                self.act(rs.t[:, 0:tn], rs.t[:, 0:tn], AF.Exp, [rs.reg()], [rs.reg()], scale=-0.5)
                self.stt(qn.t[:, 0:tn], pt[:, 0:tn], AQG.t[:, isk:isk + 1], rs.t[:, 0:tn], ALU.mult, ALU.mult,
                         [pr, AQG.reg(), rs.reg()], [qn.reg()])
                if isk:
                    self.store(self.kTo[:, h, t0:t0 + tn], qn.t[:, 0:tn], [qn.reg()], "ok%d" % (n % 4))

            def stage_c(n):
                ti, isk, hh = items[n]
                t0, tn = TT[ti]
                qn = QN[n % 2]
                pq, pqr = self.ps()
                self.mm(pq[:, 0:tn], RM.t[:, :], qn.t[:, 0:tn], True, True, [RM.reg(), qn.reg()], [pqr])
                self.tt("pool" if DBG.get("t1pool") else "dve", T1r.t[:, 0:tn], qn.t[:, 0:tn], COS.t[:, 0:tn], ALU.mult, [qn.reg(), COS.reg()], [T1r.reg()])
                self.tt("dve", T2r.t[:, 0:tn], pq[:, 0:tn], SIN.t[:, 0:tn], ALU.mult, [pqr, SIN.reg()], [T2r.reg()])
                if isk:
                    dst, dreg = KT.t[:, hh, 512 + t0:512 + t0 + tn], KT.reg(hh, ti)
                    self.tt("dve", dst, T1r.t[:, 0:tn], T2r.t[:, 0:tn], ALU.add, [T1r.reg(), T2r.reg()], [dreg])
                else:
                    for c in range(2):
                        ps_ = slice(c * 64, (c + 1) * 64)
                        self.tt("dve", QZ[c].t[ps_, hh, t0:t0 + tn], T1r.t[ps_, 0:tn], T2r.t[ps_, 0:tn], ALU.add,
                                [T1r.reg(), T2r.reg(), QZ[c].reg("z")], [QZ[c].reg(hh, ti)])

            nit = len(items)
            for ti in range(3):
                idx = [n for n in range(nit) if items[n][0] == ti]
                lo, hi = idx[0], idx[-1] + 1
                if DBG.get("seqproj"):
                    for k in range(lo, hi):
                        stage_a(k)
                        stage_b(k)
                        stage_c(k)
                    continue
                for k in range(lo - 1, hi + 1):
                    if lo <= k + 1 < hi:
                        stage_a(k + 1)
                    if lo <= k < hi:
                        stage_b(k)
                    if lo <= k - 1 < hi:
                        stage_c(k - 1)
            for tb in range(10):
                ti = min(tb // 4, 2)
                pt, pr = self.ps()
                for kc in range(8):
                    self.mm(pt[:, 0:512], H.t[:, kc, tb * 128:(tb + 1) * 128], WVh.t[:, kc, :], kc == 0, kc == 7,
                            [hregs(H, kc, ti), WVh.reg()], [pr])
                self.act(VF.t[:, :], pt[:, 0:512], AF.Copy, [pr], [VF.reg()])
                self.copy("dve", VB.t[:, 4 + tb, :], VF.t[:, :], [VF.reg()], [VB.reg(4 + tb)])
                self.store(self.vOo[:, tb, 512 * hg:512 * hg + 512], VF.t[:, :], [VF.reg()], "ov%d" % (tb % 2))
            for w in Wq + Wk + [WVh]:
                w.closed = True
            jobs = []
            for s in range(NS):
                kbs = list(range(12)) if s < 4 else [12, 13]
                for hh in range(4):
                    for ki, kb in enumerate(kbs):
                        jobs.append((s, hh, kb, ki == 0, ki == len(kbs) - 1))
            jst = {}
            acc = {}

            def job_scores(n):
                s, hh, kb, first, last = jobs[n]
                tis = s // 2 if s < 4 else 2
                k0 = kb * 128
                if kb < 4:
                    kreg = KT.reg("c")
                else:
                    kreg = KT.reg(hh, min((kb - 4) // 4, 2))
                pc, pcr = self.ps_pool("sc", sc_banks)
                for c in range(2):
                    self.mm(pc[:, c * 256:(c + 1) * 256], KT.t[:, hh, k0:k0 + 128],
                            QZ[c].t[:, hh, s * SL:(s + 1) * SL], True, True,
                            [kreg, QZ[c].reg(hh, tis), QZ[c].reg("z")], [pcr])
                jst[n] = (pc, pcr)

            def job_rest(n):
                s, hh, kb, first, last = jobs[n]
                pc, pcr = jst.pop(n)
                ptb = PT[n % len(PT)]
                if s < 4:
                    bcol = s * 6 + kb // 2
                    self.act(ptb.t[:, :], pc[:, 0:512], AF.Exp, [pcr, ABI.reg()], [ptb.reg()],
                             bias=ABI.t[:, bcol:bcol + 1], scale=0.125)
                else:
                    self.act(ptb.t[:, :], pc[:, 0:512], AF.Exp, [pcr], [ptb.reg()], scale=0.125)
                if first:
                    acc[(s, hh)] = (self.ps_pool("acc", acc_banks), self.ps_pool("acc", acc_banks))
                (pz, pzr), (po, por) = acc[(s, hh)]
                self.mm(pz[:, 0:512], self.ONES.t[:, :], ptb.t[:, :], first, last, [self.ONES.reg(), ptb.reg()], [pzr])
                self.mm(po[:, 0:512], VB.t[:, kb, hh * 128:(hh + 1) * 128], ptb.t[:, :], first, last,
                        [VB.reg(kb), ptb.reg()], [por])

            fin_i = [0]

            def fin_part1(s, hh):
                (pz, pzr), (po, por) = acc.pop((s, hh))
                k = fin_i[0] % 2
                fin_i[0] += 1
                if DBG.get("recip_act"):
                    self.act(RZ.t[:, :], pz[:, 0:512], AF.Ln, [pzr], [RZ.reg()])
                    self.act(RZ.t[:, :], RZ.t[:, :], AF.Exp, [RZ.reg()], [RZ.reg()], scale=-1.0)
                else:
                    self.P.op("dve", lambda e: e.reciprocal(out=RZ.t[:, :], in_=pz[:, 0:512]), [pzr], [RZ.reg()])
                self.tt("dve", T1.t[:, :], po[:, 0:256], RZ.t[:, 0:256], ALU.mult, [por, RZ.reg()], [T1.reg()])
                self.stt(T2.t[:, :], po[:, 256:512], NLAM, RZ.t[:, 256:512], ALU.mult, ALU.mult,
                         [por, RZ.reg(), SM.reg(6)], [T2.reg()])
                self.tt("dve", OO[k].t[:, :], T1.t[:, :], T2.t[:, :], ALU.add, [T1.reg(), T2.reg()], [OO[k].reg()])
                self.tt("dve", OSQ[k].t[:, :], OO[k].t[:, :], OO[k].t[:, :], ALU.mult, [OO[k].reg()], [OSQ[k].reg()])
                return (s, hh, k)

            def fin_part2(f):
                s, hh, k = f
                tis = s // 2 if s < 4 else 2
                h = 4 * hg + hh
                pm, pmr = self.banks[3]
                self.mm(pm[:, 0:256], self.ONES.t[:, :], OSQ[k].t[:, :], True, True, [self.ONES.reg(), OSQ[k].reg()], [pmr])
                self.act(RSO.t[:, :], pm[:, 0:256], AF.Ln, [pmr, CST.reg()], [RSO.reg()], bias=CST.t[:, 0:1], scale=1.0 / 128)
                self.act(RSO.t[:, :], RSO.t[:, :], AF.Exp, [RSO.reg()], [RSO.reg()], scale=-0.5)
                if hg == 0:
                    dst, dreg = AO0.t[:, hh, s * SL:(s + 1) * SL], AO0.reg(hh, tis)
                else:
                    dst, dreg = H.t[:, h, s * SL:(s + 1) * SL], H.reg(h, s)
                self.stt(dst, OO[k].t[:, :], SG1, RSO.t[:, :], ALU.mult, ALU.mult, [OO[k].reg(), SM.reg(7), RSO.reg()], [dreg])

            nj = len(jobs)
            LA = DBG.get("la", 2)
            if DBG.get("noscore"):
                nj = 0
            pending = []
            for n in range(min(LA, nj)):
                job_scores(n)
            for n in range(nj):
                if n + LA < nj:
                    job_scores(n + LA)
                job_rest(n)
                while pending and pending[0][1] <= n:
                    fin_part2(pending.pop(0)[0])
                if jobs[n][4]:
                    while len(pending) >= 1:
                        fin_part2(pending.pop(0)[0])
                    pending.append((fin_part1(jobs[n][0], jobs[n][1]), n + 9))
            while pending:
                fin_part2(pending.pop(0)[0])
        for half in range(2):
            W = self.wload(self.awo[half], [128, 8, 512])
            for cc4 in range(4):
                c = half * 4 + cc4
                for ti, (t0, tn) in enumerate(TT):
                    pt, pr = self.ps()
                    for kc in range(8):
                        if kc < 4:
                            rhs, rreg = AO0.t[:, kc, t0:t0 + tn], AO0.reg(kc, ti)
                        else:
                            rhs, rreg = H.t[:, kc, t0:t0 + tn], hregs(H, kc, ti)
                        self.mm(pt[:, 0:tn], W.t[:, kc, cc4 * 128:(cc4 + 1) * 128], rhs, kc == 0, kc == 7,
                                [W.reg(), rreg], [pr])
                    self.resid_add(pt, pr, c, ti, 2, MOD)
            W.closed = True
        ar.pop()


def _fm(x_tok):
    T, F = x_tok.shape
    return np.ascontiguousarray(x_tok.reshape(T, F // 128, 128).transpose(2, 1, 0))


def core_slots(core):
    if core < 2:
        return [("s", core, q) for q in range(4)] + [("p", 30 + core, 0)]
    return [("p", 5 * (core - 2) + s, 0) for s in range(NS)]


def prep_shared(inp):
    f = np.float32
    sh = {}
    w_mod = inp["w_mod"]
    sh["wmod"] = np.ascontiguousarray(
        w_mod.reshape(4, 8, 128, 12, 512).transpose(0, 3, 2, 1, 4)).astype(f)
    bm = inp["b_mod"].reshape(4, 48, 128).transpose(2, 0, 1)
    sh["bmodF"] = np.ascontiguousarray(np.repeat(bm[:, :, :, None], NS, axis=3).reshape(128, 4 * 240)).astype(f)
    ng = inp["norm_g"].reshape(4, 2, 8, 128).transpose(3, 0, 1, 2)
    sh["ngr"] = np.ascontiguousarray(np.repeat(ng[:, :, :, :, None], NS, axis=4).reshape(128, 4 * 2 * 40)).astype(f)
    lw = inp["lru_w_in"]
    gbw = lw[:, :, :D_RNN].reshape(2, 8, 128, 10, 128)
    rcw = lw[:, :, D_RNN:].reshape(2, 8, 128, 10, 128)
    both = np.stack([gbw, rcw], axis=4)
    sh["lwin"] = np.ascontiguousarray(both.transpose(0, 3, 2, 1, 4, 5).reshape(2, 10, 128, 8, 256)).astype(f)
    sh["lcw"] = np.ascontiguousarray(inp["lru_conv_w"].reshape(2, 4, 10, 128).transpose(3, 0, 2, 1).reshape(128, 80)).astype(f)
    sh["lcb"] = np.ascontiguousarray(inp["lru_conv_b"].reshape(2, 10, 128).transpose(2, 0, 1).reshape(128, 20)).astype(f)
    sh["lwg"] = np.ascontiguousarray(inp["lru_w_gate"].transpose(0, 4, 3, 1, 2, 5).reshape(2, 128, 40, 128)).astype(f)
    sh["lbg"] = np.ascontiguousarray(inp["lru_b_gate"].transpose(4, 0, 3, 1, 2).reshape(128, 80)).astype(f)
    sh["llam"] = np.ascontiguousarray(inp["lru_lambda"].reshape(2, 2, 10, 128).transpose(3, 0, 1, 2).reshape(128, 40)).astype(f)
    sh["lwout"] = np.ascontiguousarray(inp["lru_w_out"].reshape(2, 10, 128, 2, 512).transpose(0, 3, 2, 1, 4)).astype(f)
    fu = inp["ffn_w_up"]
    g = fu[:, :, :D_FF].reshape(4, 8, 128, NI, 128)
    u = fu[:, :, D_FF:].reshape(4, 8, 128, NI, 128)
    both = np.stack([g, u], axis=4)
    sh["fwu"] = np.ascontiguousarray(both.transpose(0, 3, 2, 1, 4, 5).reshape(4, NI, 128, 8, 256)).astype(f)
    sh["fcw"] = np.ascontiguousarray(inp["ffn_conv_w"].reshape(4, 3, 44, 128).transpose(3, 0, 2, 1).reshape(128, 4 * 44 * 3)).astype(f)
    sh["fcb"] = np.ascontiguousarray(inp["ffn_conv_b"].reshape(4, 44, 128).transpose(2, 0, 1).reshape(128, 4 * 44)).astype(f)
    sh["fwd"] = np.ascontiguousarray(inp["ffn_w_down"].reshape(4, NI, 128, 4, 256).transpose(0, 3, 2, 1, 4)).astype(f)
    cw = inp["cmlp_w_in"][0]
    sh["cwu"] = np.ascontiguousarray(cw[:, :D_B].reshape(8, 128, 8, 256).transpose(2, 1, 0, 3)).astype(f)
    sh["cbu"] = np.ascontiguousarray(inp["cmlp_b_in"][0, :D_B].reshape(16, 128).T).astype(f)
    sh["cwv"] = np.ascontiguousarray(cw[:, D_B:].reshape(8, 128, 4, 512).transpose(2, 1, 0, 3)).astype(f)
    sh["cbv"] = np.ascontiguousarray(inp["cmlp_b_in"][0, D_B:].reshape(1, D_B)).astype(f)
    sh["cng"] = np.ascontiguousarray(np.broadcast_to(inp["cmlp_norm_g"][0][None, :], (128, D_B))).astype(f)
    sh["cwsT"] = np.ascontiguousarray(inp["cmlp_w_s"][0].transpose(2, 0, 1)).astype(f)
    sh["cbs"] = np.ascontiguousarray(inp["cmlp_b_s"][0].reshape(1, 1024)).astype(f)
    sh["cwo"] = np.ascontiguousarray(inp["cmlp_w_out"][0].reshape(16, 128, 2, 512).transpose(2, 1, 0, 3)).astype(f)
    aw = inp["attn_w_qkv"][0]
    sh["awqk"] = np.ascontiguousarray(aw[:, :2048].reshape(8, 128, 8, 256).transpose(2, 1, 0, 3)).astype(f)
    sh["awv"] = np.ascontiguousarray(aw[:, 2048:].reshape(8, 128, 2, 512).transpose(2, 1, 0, 3)).astype(f)
    qg = inp["attn_qk_g"][0]
    sh["aqg"] = np.ascontiguousarray(np.concatenate([qg, qg], axis=1).T).astype(f)
    sh["alam"] = np.ascontiguousarray(np.broadcast_to(inp["attn_lambda"][0].reshape(1, 256), (128, 256))).astype(f)
    sh["asg"] = np.ascontiguousarray(inp["attn_subln_g"][0].reshape(128, 1)).astype(f)
    sh["awo"] = np.ascontiguousarray(inp["attn_w_out"][0].reshape(8, 128, 2, 512).transpose(2, 1, 0, 3)).astype(f)
    rm = np.zeros((128, 128), f)
    for m_ in range(128):
        d_ = m_ % 64
        rm[m_ + 32 if d_ < 32 else m_ - 32, m_] = 1.0
    sh["rm"] = rm
    cst = np.zeros((128, 8), f)
    cst[:, 0] = EPS
    cst[:, 1] = 1.0
    sh["cst"] = cst
    return sh


def prep_core(inp, core):
    f = np.float32
    slots = core_slots(core)
    toks = []
    cond = []
    for (kind, b, q) in slots:
        if kind == "s":
            toks.append(inp["x_sample"][b, q * SL:(q + 1) * SL])
            cond.append(inp["c"][b])
        else:
            toks.append(inp["x_prompt"][b])
            cond.append(inp["c_ctx"])
    x_tok = np.concatenate(toks, axis=0)
    m = {}
    m["xT"] = _fm(x_tok).astype(f)
    cnd = np.stack(cond, axis=0)
    m["condT"] = np.ascontiguousarray(cnd.reshape(NS, 8, 128).transpose(2, 1, 0).reshape(128, 8 * NS)).astype(f)
    mj = np.array([1.0 if (slots[s][0] == "s" and slots[s + 1][0] == "s") else 0.0 for s in range(4)], f)
    mk = np.zeros((128, 32), f)
    mk[:, 1:5] = mj
    mk[:, 5:9] = mj
    mk[:, 10:14] = mj
    mk[:, 14:22] = np.repeat(mj, 2)
    m["mk"] = mk
    h0 = np.zeros((128, 2, 10, 2, NS), f)
    if core < 2:
        st = inp["state_lru"][core]
        for a in range(2):
            h0[:, a, :, 0, 0] = st[a, 0].reshape(10, 128).T
            h0[:, a, :, 1, 3] = st[a, 1].reshape(10, 128).T
    m["h0"] = h0.reshape(128, -1)
    ckT = np.zeros((128, 8, 512), f)
    cv = np.zeros((128, 4, 1024), f)
    cosT = np.ones((128, NT), f)
    sinT = np.zeros((128, NT), f)
    abias = np.zeros((128, 24), f)
    if core < 2:
        ck = inp["cache_k"][core, 0]
        ckT[:] = ck.reshape(512, 8, 128).transpose(2, 1, 0)
        cv[:] = inp["cache_v"][core, 0].reshape(4, 128, 1024).transpose(1, 0, 2)
        T = 4 * SL
        row = (np.arange(T) // 64).astype(f)
        col = (np.arange(T) % 64).astype(f)
        inv = (np.float32(10000.0) ** (-np.arange(16, dtype=f) / np.float32(16))).astype(f)
        ang = np.concatenate([row[:, None] * inv[None, :], col[:, None] * inv[None, :]], axis=1).astype(f)
        cs, sn = np.cos(ang).astype(f), np.sin(ang).astype(f)
        for p in range(128):
            d_ = p % 64
            cosT[p, :T] = cs[:, d_ % 32]
            sinT[p, :T] = -sn[:, d_ % 32] if d_ < 32 else sn[:, d_ % 32]
    else:
        for s_ in range(4):
            for jp in range(6):
                if jp != 2 + s_:
                    abias[:, s_ * 6 + jp] = NEG
    m["ckT"], m["cv"], m["cosT"], m["sinT"], m["abias"] = ckT, cv, cosT, sinT, abias
    return m


_CACHE = {}


def get_program(n_layers):
    if n_layers not in _CACHE:
        b = Builder(n_layers)
        counts = b.build()
        _CACHE[n_layers] = (b, counts)
    return _CACHE[n_layers]


def kernel(**inp):
    inp = {k: np.asarray(v) for k, v in inp.items()}
    b, counts = get_program(N_LAYERS)
    sh = prep_shared(inp)
    in_maps = []
    for core in range(N_CORES):
        m = dict(sh)
        m.update(prep_core(inp, core))
        in_maps.append({k: m[k] for k in b.din})
    res = run_bass_kernel_spmd(b.nc, in_maps, core_ids=list(range(N_CORES)))
    outs = res.results
    B, S = inp["x_prompt"].shape[0], inp["x_prompt"].shape[1]
    y_prompt = np.zeros((B, S, D), np.float32)
    y_sample = np.zeros(inp["x_sample"].shape, np.float32)
    new_lru = np.zeros((B, 2, 2, D_RNN), np.float32)
    new_k = np.zeros((B, 1, S, 8, 2, 64), np.float32)
    new_v = np.zeros((B, 1, S, 8, 128), np.float32)
    for core in range(N_CORES):
        r = outs[core]
        y = r["yT"].transpose(2, 1, 0).reshape(NT, D)
        st = r["st"].reshape(2, 128, 10, 2, NS)
        for s, (kind, bi, q) in enumerate(core_slots(core)):
            if kind == "s":
                y_sample[bi, q * SL:(q + 1) * SL] = y[s * SL:(s + 1) * SL]
            else:
                y_prompt[bi] = y[s * SL:(s + 1) * SL]
                new_lru[bi] = st[:, :, :, :, s].transpose(0, 3, 2, 1).reshape(2, 2, D_RNN)
                if "kT" in r:
                    kk = r["kT"].transpose(2, 1, 0)
                    new_k[bi, 0] = kk[s * SL:(s + 1) * SL].reshape(SL, 8, 2, 64)
                    vv = r["vO"].transpose(1, 0, 2).reshape(NT, 1024)
                    new_v[bi, 0] = vv[s * SL:(s + 1) * SL].reshape(SL, 8, 128)
    return (y_prompt, y_sample, new_lru, new_k, new_v)
```

```python
import contextlib
import math
import numpy as np
import concourse.bass as bass
import concourse.mybir as mybir
from concourse.bass_utils import run_bass_kernel_spmd

F32 = mybir.dt.float32
BF16 = mybir.dt.bfloat16
AF = mybir.ActivationFunctionType
ALU = mybir.AluOpType

N_CORES = 8
D = 1024
NT = 1280
NS = 5
SL = 256
TT = [(0, 512), (512, 512), (1024, 256)]
D_RNN = 1280
D_B = 2048
D_FF = 2816
NI = 22
EPS = 1e-6
N_LAYERS = 4
SAME_ENGINE_SYNC = True
DEBUG_MODE = None
DBG = {}
NEG = -30000.0


class Reg:
    __slots__ = ("last_write", "readers")

    def __init__(self, inherit=()):
        self.last_write = None
        self.readers = list(inherit)


class Instr:
    __slots__ = ("eng", "fn", "deps", "is_dma", "dma_sem", "dma_val", "need_inc", "inc_val")

    def __init__(self, eng, fn, is_dma=False):
        self.eng = eng
        self.fn = fn
        self.deps = set()
        self.is_dma = is_dma
        self.dma_sem = None
        self.dma_val = 0
        self.need_inc = False
        self.inc_val = 0


ENGINES = ["pe", "act", "dve", "pool", "sp"]


def _flat(regs, out):
    for r in regs:
        if r is None:
            continue
        if isinstance(r, Reg):
            out.append(r)
        else:
            _flat(r, out)
    return out


class Prog:
    def __init__(self, nc):
        self.nc = nc
        self.instrs = []
        self.streams = {}

    def _track(self, ins, reads, writes):
        reads = _flat(reads, [])
        writes = _flat(writes, [])
        for r in reads:
            if r.last_write is not None:
                ins.deps.add(r.last_write)
        for r in writes:
            if r.last_write is not None:
                ins.deps.add(r.last_write)
            for rd in r.readers:
                ins.deps.add(rd)
        for r in reads:
            r.readers.append(ins)
        for r in writes:
            r.last_write = ins
            r.readers = []
        ins.deps.discard(ins)

    def op(self, eng, fn, reads=(), writes=()):
        ins = Instr(eng, fn)
        self._track(ins, reads, writes)
        self.instrs.append(ins)
        return ins

    def dma(self, eng, stream, fn, reads=(), writes=()):
        ins = Instr(eng, fn, is_dma=True)
        st = self.streams.setdefault(stream, [0, None])
        st[0] += 16
        ins.dma_sem = stream
        ins.dma_val = st[0]
        if st[1] is not None:
            ins.deps.add(st[1])
        st[1] = ins
        self._track(ins, reads, writes)
        self.instrs.append(ins)
        return ins

    def emit(self, final_wait_eng="sp"):
        nc = self.nc
        for ins in self.instrs:
            for d in ins.deps:
                if not d.is_dma:
                    if d.eng != ins.eng or (SAME_ENGINE_SYNC and d.eng != "pe"):
                        d.need_inc = True
        counts = {e: 0 for e in ENGINES}
        per_eng = {e: [] for e in ENGINES}
        for ins in self.instrs:
            per_eng[ins.eng].append(ins)
            if not ins.is_dma and ins.need_inc:
                counts[ins.eng] += 1
                ins.inc_val = counts[ins.eng]
        with contextlib.ExitStack() as es:
            esem = {e: es.enter_context(nc.semaphore("s_" + e)) for e in ENGINES}
            ssem = {s: es.enter_context(nc.semaphore("d_" + s)) for s in self.streams}
            block = es.enter_context(nc.Block())
            handles = {"pe": block.tensor, "act": block.scalar, "dve": block.vector,
                       "pool": block.gpsimd, "sp": block.sync}

            def make_body(e):
                def body(eng):
                    waited = {}
                    for ins in per_eng[e]:
                        need = {}
                        for d in ins.deps:
                            if d.is_dma:
                                key = ("d", d.dma_sem)
                                val = d.dma_val
                            else:
                                if d.eng == e and (e == "pe" or not SAME_ENGINE_SYNC):
                                    continue
                                key = ("e", d.eng)
                                val = d.inc_val
                            if val > need.get(key, 0):
                                need[key] = val
                        for key, val in need.items():
                            if waited.get(key, 0) >= val:
                                continue
                            waited[key] = val
                            sem = ssem[key[1]] if key[0] == "d" else esem[key[1]]
                            eng.wait_ge(sem, val)
                        bi = ins.fn(eng)
                        if ins.is_dma:
                            bi.then_inc(ssem[ins.dma_sem], 16)
                        elif ins.need_inc:
                            bi.then_inc(esem[e], 1)
                    if e == final_wait_eng:
                        for s, (cnt, _) in self.streams.items():
                            if waited.get(("d", s), 0) < cnt:
                                eng.wait_ge(ssem[s], cnt)
                return body

            for e in ENGINES:
                if per_eng[e] or e == final_wait_eng:
                    handles[e](make_body(e))
        return {e: len(per_eng[e]) for e in ENGINES}


class Buf:
    def __init__(self, t, off, nbytes, inherit):
        self.t = t
        self.off = off
        self.nbytes = nbytes
        self.inherit = inherit
        self._regs = {}
        self.closed = False

    def reg(self, *key):
        r = self._regs.get(key)
        if r is None:
            r = Reg(self.inherit)
            self._regs[key] = r
        return r

    def all_instrs(self):
        out = list(self.inherit)
        for r in self._regs.values():
            if r.last_write is not None:
                out.append(r.last_write)
            out.extend(r.readers)
        return out


_DT_SIZE = {F32: 4, BF16: 2}


class Arena:
    def __init__(self, nc, base, size, name):
        self.nc, self.base, self.size, self.name = nc, base, size, name
        self.cur = 0
        self.frames = []
        self.live = []
        self.dead = []
        self.n = 0
        self.hi = 0

    def push(self):
        self.frames.append((self.cur, len(self.live)))

    def pop(self):
        cur, nlive = self.frames.pop()
        for b in self.live[nlive:]:
            self.dead.append((b.off, b.off + b.nbytes, b.all_instrs()))
        del self.live[nlive:]
        self.cur = cur

    def alloc(self, shape, dtype):
        n = 1
        for s in shape[1:]:
            n *= s
        nbytes = (n * _DT_SIZE[dtype] + 31) // 32 * 32
        off = self.cur
        assert off + nbytes <= self.size, (self.name, off, nbytes, self.size)
        self.cur += nbytes
        self.hi = max(self.hi, self.cur)
        inherit = []
        keep = []
        for (a, b, ins) in self.dead:
            if a < off + nbytes and off < b:
                inherit.extend(ins)
                if not (off <= a and b <= off + nbytes):
                    keep.append((a, b, ins))
            else:
                keep.append((a, b, ins))
        self.dead = keep
        self.n += 1
        t = self.nc.alloc_sbuf_tensor_at("%s%d" % (self.name, self.n), list(shape), dtype,
                                         offset=self.base + off)
        buf = Buf(t, off, nbytes, list(dict.fromkeys(inherit)))
        buf.abs_off = self.base + off
        self.live.append(buf)
        return buf


def buf_view(nc, buf, shape, dtype, name):
    return nc.alloc_sbuf_tensor_at(name, list(shape), dtype, offset=buf.abs_off)


class Ring:
    def __init__(self, nc, base, size):
        self.nc, self.base, self.size = nc, base, size
        self.cur = 0
        self.bufs = []
        self.n = 0

    def reset(self):
        self.cur = 0

    def alloc(self, shape, dtype):
        n = 1
        for s in shape[1:]:
            n *= s
        nbytes = (n * _DT_SIZE[dtype] + 31) // 32 * 32
        assert nbytes <= self.size
        if self.cur + nbytes > self.size:
            self.cur = 0
        off = self.cur
        self.cur += nbytes
        inherit = []
        keep = []
        for b in self.bufs:
            if b.off < off + nbytes and off < b.off + b.nbytes:
                assert b.closed, "ring overwrite of a live weight buffer"
                inherit.extend(b.all_instrs())
            else:
                keep.append(b)
        self.bufs = keep
        self.n += 1
        t = self.nc.alloc_sbuf_tensor_at("wr%d" % self.n, list(shape), dtype, offset=self.base + off)
        buf = Buf(t, off, nbytes, list(dict.fromkeys(inherit)))
        self.bufs.append(buf)
        return buf


def slots_of(ti):
    return (ti * 2, 2) if ti < 2 else (4, 1)


def hregs(H, c, ti):
    s0, ns = slots_of(ti)
    return [H.reg(c, s0 + i) for i in range(ns)]


class Builder:
    def __init__(self, n_layers):
        self.n_layers = n_layers
        nc = bass.Bass("TRN2", target_bir_lowering=False)
        self.nc = nc
        self.P = Prog(nc)
        self.din = {}
        self.dout = {}
        self.nstream = 0
        total = 207 * 1024
        B0 = 16928
        self.pers = Arena(nc, B0, 76 * 1024, "ps")
        self.ring = Ring(nc, B0 + 76 * 1024, 32 * 1024)
        self.arena = Arena(nc, B0 + 108 * 1024, total - 108 * 1024, "ar")
        self.banks = []
        for i in range(8):
            t = nc.alloc_psum_tensor("bank%d" % i, [128, 512], F32)
            self.banks.append((t, Reg()))
        self.bank_i = 0
        self.pool_i = {}

    def inp(self, name, shape):
        ap = self.nc.dram_tensor(name, list(shape), F32, kind="ExternalInput").ap()
        self.din[name] = ap
        return ap

    def outp(self, name, shape):
        ap = self.nc.dram_tensor(name, list(shape), F32, kind="ExternalOutput").ap()
        self.dout[name] = ap
        return ap

    def ps(self):
        b = self.banks[self.bank_i]
        self.bank_i = (self.bank_i + 1) % 7
        return b

    def ps_pool(self, key, idxs):
        i = self.pool_i.get(key, 0)
        self.pool_i[key] = i + 1
        return self.banks[idxs[i % len(idxs)]]

    def mm(self, out, lhsT, rhs, start, stop, rd, wr):
        self.P.op("pe", lambda e: e.matmul(out, lhsT=lhsT, rhs=rhs, start=start, stop=stop), rd, wr)

    def act(self, out, in_, func, rd, wr, bias=None, scale=None):
        kw = {}
        if bias is not None:
            kw["bias"] = bias
        if scale is not None:
            kw["scale"] = scale
        self.P.op("act", lambda e: e.activation(out=out, in_=in_, func=func, **kw), rd, wr)

    def tt(self, eng, out, in0, in1, op, rd, wr):
        self.P.op(eng, lambda e: e.tensor_tensor(out=out, in0=in0, in1=in1, op=op), rd, wr)

    def ts(self, eng, out, in0, s1, s2, op0, op1, rd, wr):
        if op1 is None:
            self.P.op(eng, lambda e: e.tensor_scalar(out=out, in0=in0, scalar1=s1, scalar2=None, op0=op0), rd, wr)
        else:
            self.P.op(eng, lambda e: e.tensor_scalar(out=out, in0=in0, scalar1=s1, scalar2=s2, op0=op0, op1=op1), rd, wr)

    def stt(self, out, in0, scalar, in1, op0, op1, rd, wr):
        self.P.op("dve", lambda e: e.scalar_tensor_tensor(out=out, in0=in0, scalar=scalar, in1=in1,
                                                          op0=op0, op1=op1), rd, wr)

    def copy(self, eng, out, in_, rd, wr):
        self.P.op(eng, lambda e: e.tensor_copy(out=out, in_=in_), rd, wr)

    def memset(self, eng, ap, val, wr):
        self.P.op(eng, lambda e: e.memset(ap, val), (), wr)

    def scan(self, out, d0, d1, rd, wr):
        self.P.op("dve", lambda e: e.tensor_tensor_scan(out=out, data0=d0, data1=d1, initial=0.0,
                                                        op0=ALU.mult, op1=ALU.add), rd, wr)

    def load(self, out, in_, wr, eng="sp"):
        self.nstream += 1
        self.P.dma(eng, "l%d" % (self.nstream % 8), lambda e: e.dma_start(out=out, in_=in_), (), wr)

    def store(self, out, in_, rd, stream):
        self.P.dma("sp", stream, lambda e: e.dma_start(out=out, in_=in_), rd, ())

    def wload(self, src, shape):
        buf = self.ring.alloc(shape, BF16)
        t = buf.t
        if len(shape) == 3:
            o = t[:, :, :]
        else:
            o = t[:, :]
        self.P.dma("pool", "wr%d" % (self.ring.n % 8), lambda e: e.dma_start(out=o, in_=src), (), [buf.reg()])
        return buf

    def build(self):
        nc = self.nc
        nl = self.n_layers
        pers = self.pers
        xT = self.inp("xT", [128, 8, NT])
        condT = self.inp("condT", [128, 8 * NS])
        mk = self.inp("mk", [128, 32])
        cst = self.inp("cst", [128, 8])
        h0 = self.inp("h0", [128, 2 * 10 * 2 * NS])
        wmod = self.inp("wmod", [4, 12, 128, 8, 512])
        bmodF = self.inp("bmodF", [128, 4 * 240])
        ngr = self.inp("ngr", [128, 4 * 2 * 40])
        lwin = self.inp("lwin", [2, 10, 128, 8, 256])
        lcw = self.inp("lcw", [128, 2 * 10 * 4])
        lcb = self.inp("lcb", [128, 2 * 10])
        lwg = self.inp("lwg", [2, 128, 40, 128])
        lbg = self.inp("lbg", [128, 2 * 10 * 4])
        llam = self.inp("llam", [128, 2 * 2 * 10])
        lwout = self.inp("lwout", [2, 2, 128, 10, 512])
        fwu = self.inp("fwu", [4, NI, 128, 8, 256])
        fcw = self.inp("fcw", [128, 4 * 44 * 3])
        fcb = self.inp("fcb", [128, 4 * 44])
        fwd = self.inp("fwd", [4, 4, 128, NI, 256])
        self.wmod, self.lwin, self.lwg, self.lwout, self.fwu, self.fwd = wmod, lwin, lwg, lwout, fwu, fwd
        if nl >= 2:
            self.cwu = self.inp("cwu", [8, 128, 8, 256])
            self.cbu = self.inp("cbu", [128, 16])
            self.cwv = self.inp("cwv", [4, 128, 8, 512])
            self.cbv = self.inp("cbv", [1, 2048])
            self.cng = self.inp("cng", [128, 2048])
            self.cwsT = self.inp("cwsT", [128, 8, 128])
            self.cbs = self.inp("cbs", [1, 1024])
            self.cwo = self.inp("cwo", [2, 128, 16, 512])
        if nl >= 3:
            self.awqk = self.inp("awqk", [8, 128, 8, 256])
            self.awv = self.inp("awv", [2, 128, 8, 512])
            self.aqg = self.inp("aqg", [128, 2])
            self.alam = self.inp("alam", [128, 256])
            self.asg = self.inp("asg", [128, 1])
            self.awo = self.inp("awo", [2, 128, 8, 512])
            self.ckT = self.inp("ckT", [128, 8, 512])
            self.cv = self.inp("cv", [128, 4, 1024])
            self.cosT = self.inp("cosT", [128, NT])
            self.sinT = self.inp("sinT", [128, NT])
            self.rm = self.inp("rm", [128, 128])
            self.abias = self.inp("abias", [128, 24])
            self.kTo = self.outp("kT", [128, 8, NT])
            self.vOo = self.outp("vO", [128, 10, 1024])
        yT = self.outp("yT", [128, 8, NT])
        stO = self.outp("st", [2, 128, 10 * 2 * NS])
        self.stO = stO

        X = pers.alloc([128, 8, NT], F32)
        H = pers.alloc([128, 8, NT], BF16)
        self.X, self.H = X, H
        SCT = pers.alloc([128, 8 * NS], BF16)
        CND = pers.alloc([128, 8 * NS], F32)
        MK = pers.alloc([128, 32], F32)
        CST = pers.alloc([128, 8], F32)
        H0 = pers.alloc([128, 2, 10, 2, NS], F32)
        BMOD = pers.alloc([128, 4, 240], F32)
        NGR = pers.alloc([128, 4, 2, 40], F32)
        MOD = [pers.alloc([128, 48, NS], F32) for _ in range(2)]
        GS = [pers.alloc([128, 2, 40], F32) for _ in range(2)]
        LCW = pers.alloc([128, 2, 10, 4], F32)
        LCB = pers.alloc([128, 2, 10], F32)
        LBG = pers.alloc([128, 2, 10, 4], F32)
        LLAM = pers.alloc([128, 40], F32)
        NSP8 = pers.alloc([128, 2, 2, 10], F32)
        NSP16 = pers.alloc([128, 2, 2, 10], F32)
        self.NSP16 = NSP16
        LT = [pers.alloc([128, 40], F32) for _ in range(3)]
        FCW = pers.alloc([128, 4, 44, 3], F32)
        FCB = pers.alloc([128, 4, 44], F32)
        ONES = pers.alloc([128, 128], BF16)
        self.MK, self.CST, self.H0, self.MOD, self.GS = MK, CST, H0, MOD, GS
        self.LCW, self.LCB, self.LBG, self.NSP8, self.FCW, self.FCB = LCW, LCB, LBG, NSP8, FCW, FCB
        self.ONES = ONES
        self.SCT, self.BMOD, self.NGR = SCT, BMOD, NGR

        def flat2(buf):
            t = buf.t
            nd = len(t.shape)
            if nd == 2:
                return t[:, :]
            names = " ".join("a%d" % i for i in range(nd - 1))
            return t[tuple([slice(None)] * nd)].rearrange("p %s -> p (%s)" % (names, names))
        self.flat2 = flat2

        for buf, src in [(CND, condT), (MK, mk), (CST, cst), (H0, h0), (BMOD, bmodF), (NGR, ngr), (LCW, lcw),
                         (LCB, lcb), (LBG, lbg), (LLAM, llam), (FCW, fcw), (FCB, fcb)]:
            self.load(flat2(buf), src, [buf.reg()])
        for c in range(8):
            self.load(X.t[:, c, :], xT[:, c, :], [X.reg(c, 0), X.reg(c, 1), X.reg(c, 2)])
        self.memset("dve", ONES.t[:, :], 1.0, [ONES.reg()])
        self.act(SCT.t[:, :], CND.t[:, :], AF.Silu, [CND.reg()], [SCT.reg()])
        a0, a1, a2 = [b_.t[:, :] for b_ in LT]
        r0, r1, r2 = [b_.reg() for b_ in LT]
        self.act(a0, LLAM.t[:, :], AF.Abs, [LLAM.reg()], [r0])
        self.act(a1, a0, AF.Exp, [r0], [r1], scale=-1.0)
        self.act(a0, a1, AF.Ln, [r1, CST.reg()], [r0], bias=CST.t[:, 1:2])
        self.ts("dve", a2, LLAM.t[:, :], -1.0, 0.0, ALU.mult, ALU.max, [LLAM.reg()], [r2])
        self.tt("dve", a1, a0, a2, ALU.add, [r0, r2], [r1])
        self.ts("dve", flat2(NSP8), a1, -8.0, None, ALU.mult, None, [r1], [NSP8.reg()])
        self.ts("dve", flat2(NSP16), a1, -16.0, None, ALU.mult, None, [r1], [NSP16.reg()])

        self.ad = None
        if DEBUG_MODE == "attn":
            self.adaln_begin(2)
            self.adaln_flush()
            self.modulate(2, 0)
            self.attn(2, 0)
            nl = 0
        else:
            self.adaln_begin(0)
            for _ in range(6):
                self.bg()
        for l in range(nl):
            kind = l % 3
            j = l // 3
            self.modulate(l, 0)
            if kind == 0:
                self.lru(l, j)
            elif kind == 1:
                self.cmlp(l, j)
            else:
                self.attn(l, j)
            self.adaln_flush()
            self.modulate(l, 1)
            if l + 1 < nl:
                self.adaln_begin(l + 1)
            self.ffn(l)
            self.adaln_flush()

        for c in range(8):
            self.store(yT[:, c, :], X.t[:, c, :], [X.reg(c, 0), X.reg(c, 1), X.reg(c, 2)], "oy%d" % c)
        counts = self.P.emit()
        return counts

    def adaln_begin(self, l):
        assert self.ad is None
        self.ad = dict(l=l, step=0, W=self.wload(self.wmod[l, 0], [128, 8, 512]))

    def bg(self):
        ad = self.ad
        if ad is None:
            return
        l, n4 = ad["l"], ad["step"]
        MOD, GS = self.MOD[l % 2], self.GS[l % 2]
        pt, pr = self.banks[7]
        W = ad["W"]
        if n4 + 1 < 12:
            ad["W"] = self.wload(self.wmod[l, n4 + 1], [128, 8, 512])
        for nn in range(4):
            n = n4 * 4 + nn
            for kc in range(8):
                self.mm(pt[:, n * NS:(n + 1) * NS], W.t[:, kc, nn * 128:(nn + 1) * 128],
                        self.SCT.t[:, kc * NS:(kc + 1) * NS], kc == 0, kc == 7,
                        [W.reg(), self.SCT.reg()], [pr])
        W.closed = True
        ad["step"] += 1
        if ad["step"] in (6, 12):
            hf = ad["step"] // 6 - 1
            modf = self.flat2(MOD)
            cs = slice(hf * 120, (hf + 1) * 120)
            self.tt("dve", modf[:, cs], pt[:, cs], self.BMOD.t[:, l, cs], ALU.add, [pr, self.BMOD.reg()], [MOD.reg(hf)])
            m = 1 + 3 * hf
            self.stt(GS.t[:, hf, :], modf[:, m * 40:(m + 1) * 40], 1.0, self.NGR.t[:, l, hf, :],
                     ALU.add, ALU.mult, [MOD.reg(hf), self.NGR.reg()], [GS.reg(hf)])
        if ad["step"] == 12:
            self.ad = None

    def adaln_flush(self):
        while self.ad is not None:
            self.bg()

    def modulate(self, l, which):
        X, H, MOD, GS = self.X, self.H, self.MOD[l % 2], self.GS[l % 2]
        msh = 3 * which
        CST = self.CST
        ar = self.arena
        ar.push()
        RSTD = ar.alloc([128, NT], F32)
        SQ = ar.alloc([128, 4, 512], BF16)
        XN = ar.alloc([128, 2, NT], F32)
        k = 0
        for ti, (t0, tn) in enumerate(TT):
            pt, pr = self.ps()
            for c in range(8):
                sq = SQ.t[:, k % 4, 0:tn]
                sqr = SQ.reg(k % 4)
                k += 1
                if c % 3 != 2:
                    self.act(sq, X.t[:, c, t0:t0 + tn], AF.Square, [X.reg(c, ti)], [sqr])
                else:
                    self.tt("dve", sq, X.t[:, c, t0:t0 + tn], X.t[:, c, t0:t0 + tn], ALU.mult, [X.reg(c, ti)], [sqr])
                self.mm(pt[:, 0:tn], self.ONES.t[:, :], sq, c == 0, c == 7, [sqr, self.ONES.reg()], [pr])
            rs = RSTD.t[:, t0:t0 + tn]
            rr = RSTD.reg(ti)
            self.act(rs, pt[:, 0:tn], AF.Ln, [pr, CST.reg()], [rr], bias=CST.t[:, 0:1], scale=1.0 / D)
            self.act(rs, rs, AF.Exp, [rr], [rr], scale=-0.5)
        n = 0
        for c in range(8):
            xn = XN.t[:, c % 2, :]
            xnr = XN.reg(c % 2)
            self.tt("dve", xn, X.t[:, c, :], RSTD.t[:, :], ALU.mult,
                    [X.reg(c, 0), X.reg(c, 1), X.reg(c, 2), RSTD.reg(0), RSTD.reg(1), RSTD.reg(2)], [xnr])
            for s_ in range(NS):
                ti = s_ // 2 if s_ < 4 else 2
                gs = GS.t[:, which, c * NS + s_:c * NS + s_ + 1]
                sh = MOD.t[:, msh * 8 + c, s_:s_ + 1]
                o = H.t[:, c, s_ * SL:(s_ + 1) * SL]
                i_ = XN.t[:, c % 2, s_ * SL:(s_ + 1) * SL]
                if n % 5 < 3:
                    self.act(o, i_, AF.Identity, [xnr, GS.reg(which), MOD.reg(which)], [H.reg(c, s_)],
                             bias=sh, scale=gs)
                else:
                    self.ts("dve", o, i_, gs, sh, ALU.mult, ALU.add, [xnr, GS.reg(which), MOD.reg(which)],
                            [H.reg(c, s_)])
                n += 1
        ar.pop()

    def resid_add(self, pt, pr, c, ti, gate_m, MOD):
        X = self.X
        s0, ns = slots_of(ti)
        for si in range(ns):
            s = s0 + si
            xs = X.t[:, c, s * SL:(s + 1) * SL]
            self.stt(xs, pt[:, si * SL:(si + 1) * SL], MOD.t[:, gate_m * 8 + c, s:s + 1], xs,
                     ALU.mult, ALU.add, [pr, MOD.reg(gate_m // 3), X.reg(c, ti)], [X.reg(c, ti)])

    def lru(self, l, a):
        ar = self.arena
        H, MOD, MK, CST = self.H, self.MOD[l % 2], self.MK, self.CST
        ar.push()
        M = ar.alloc([128, 10, NT], BF16)
        GB = ar.alloc([128, NT], BF16)
        RECP = ar.alloc([128, NS, 259], F32)
        HD1t = buf_view(self.nc, RECP, [128, NT], F32, "hd1v%d" % l)
        XC = ar.alloc([128, NT], F32)
        XCB = ar.alloc([128, NT], BF16)
        RA = [[ar.alloc([128, NT], F32) for _ in range(2)] for _ in range(2)]
        IB = [[ar.alloc([128, NT], F32) for _ in range(2)] for _ in range(2)]
        SQb = [ar.alloc([128, NT], F32) for _ in range(2)]
        HD0 = ar.alloc([128, NT], F32)
        ST = ar.alloc([128, 10, 2, NS], F32)
        TB = [ar.alloc([128, NS], F32) for _ in range(2)]
        self.memset("dve", self.flat2(RECP), 0.0, [RECP.reg()])
        xc3 = XC.t[:, :].rearrange("p (s t) -> p s t", t=SL)
        Ws = {}

        def part_a(j):
            W = Ws[j]
            banks = [self.ps() for _ in TT]
            for kc in range(8):
                for ti, (t0, tn) in enumerate(TT):
                    pt, pr = banks[ti]
                    self.mm(pt[:, 0:tn], W.t[:, kc, 0:128], H.t[:, kc, t0:t0 + tn],
                            kc == 0, kc == 7, [W.reg(), hregs(H, kc, ti)], [pr])
            W.closed = True
            for ti, (t0, tn) in enumerate(TT):
                pt, pr = banks[ti]
                self.act(GB.t[:, t0:t0 + tn], pt[:, 0:tn], AF.Gelu_apprx_tanh, [pr], [GB.reg()])

        def part_b(j):
            W = self.wload(self.lwin[a, j], [128, 8, 256])
            Ws[j] = W
            WGj = self.wload(self.lwg[a][:, j * 4:(j + 1) * 4, :], [128, 4, 128])
            banks = [self.ps() for _ in TT]
            for kc in range(8):
                for ti, (t0, tn) in enumerate(TT):
                    pt, pr = banks[ti]
                    self.mm(pt[:, 0:tn], W.t[:, kc, 128:256], H.t[:, kc, t0:t0 + tn],
                            kc == 0, kc == 7, [W.reg(), hregs(H, kc, ti)], [pr])
            for ti, (t0, tn) in enumerate(TT):
                pt, pr = banks[ti]
                s0, ns = slots_of(ti)
                self.act(RECP.t[:, s0:s0 + ns, 2:258], pt[:, 0:tn].rearrange("p (s t) -> p s t", t=SL),
                         AF.Copy, [pr], [RECP.reg()])
            self.memset("dve", RECP.t[:, 0, 0:2], 0.0, [RECP.reg()])
            self.tt("dve", RECP.t[:, 1:5, 0:2], RECP.t[:, 0:4, 256:258],
                    MK.t[:, 14:22].rearrange("p (s t) -> p s t", t=2), ALU.mult, [RECP.reg(), MK.reg()], [RECP.reg()])
            self.tt("dve", RECP.t[:, 0:4, 258:259], RECP.t[:, 1:5, 2:3],
                    MK.t[:, 10:14].rearrange("p (s t) -> p s t", t=1), ALU.mult, [RECP.reg(), MK.reg()], [RECP.reg()])
            cw = lambda k: self.LCW.t[:, a, j, k:k + 1]
            self.act(xc3, RECP.t[:, :, 0:256], AF.Identity, [RECP.reg(), self.LCW.reg(), self.LCB.reg()], [XC.reg()],
                     bias=self.LCB.t[:, a, j:j + 1], scale=cw(0))
            if j >= 1:
                part_s(j - 1)
            for k in range(1, 4):
                self.stt(xc3, RECP.t[:, :, k:k + 256], cw(k), xc3, ALU.mult, ALU.add,
                         [RECP.reg(), self.LCW.reg(), XC.reg()], [XC.reg()])
            self.copy("dve", XCB.t[:, :], XC.t[:, :], [XC.reg()], [XCB.reg()])
            for d in range(2):
                for g in range(2):
                    dst = RA[j % 2][d] if g == 0 else IB[j % 2][d]
                    for ti, (t0, tn) in enumerate(TT):
                        pt, pr = self.ps()
                        self.mm(pt[:, 0:tn], WGj.t[:, d * 2 + g, :], XCB.t[:, t0:t0 + tn], True, True,
                                [WGj.reg(), XCB.reg()], [pr])
                        self.act(dst.t[:, t0:t0 + tn], pt[:, 0:tn], AF.Sigmoid, [pr, self.LBG.reg()], [dst.reg()],
                                 bias=self.LBG.t[:, a, j, d * 2 + g:d * 2 + g + 1])
            WGj.closed = True
            for d in range(2):
                ra = RA[j % 2][d]
                self.act(ra.t[:, :], ra.t[:, :], AF.Exp, [ra.reg(), self.NSP8.reg()], [ra.reg()],
                         scale=self.NSP8.t[:, a, d, j:j + 1])

        def part_d(j):
            for d in range(2):
                ib = IB[j % 2][d]
                self.tt("dve", ib.t[:, :], ib.t[:, :], XC.t[:, :], ALU.mult, [ib.reg(), XC.reg()], [ib.reg()])
            for d in range(2):
                ra = RA[j % 2][d]
                self.tt("dve", SQb[d].t[:, :], ra.t[:, :], ra.t[:, :], ALU.mult, [ra.reg()], [SQb[d].reg()])

        def part_s(j):
            for d in range(2):
                self.act(SQb[d].t[:, :], SQb[d].t[:, :], AF.Sqrt, [SQb[d].reg(), CST.reg()], [SQb[d].reg()],
                         bias=CST.t[:, 1:2], scale=-1.0)

        def part_x(j):
            for d in range(2):
                ib = IB[j % 2][d]
                self.tt("dve", ib.t[:, :], ib.t[:, :], SQb[d].t[:, :], ALU.mult, [ib.reg(), SQb[d].reg()], [ib.reg()])

        def part_c(j):
            for d in range(2):
                ra, ib, tb = RA[j % 2][d], IB[j % 2][d], TB[d]
                e0 = 0 if d == 0 else SL - 1
                af = ra.t[:, e0:NT:SL]
                bf = ib.t[:, e0:NT:SL]
                mcol = MK.t[:, 0:5] if d == 0 else MK.t[:, 5:10]
                self.tt("dve", tb.t[:, :], af, self.H0.t[:, a, j, d, :], ALU.mult, [ra.reg(), self.H0.reg()], [tb.reg()])
                self.tt("dve", bf, bf, tb.t[:, :], ALU.add, [ib.reg(), tb.reg()], [ib.reg()])
                self.tt("dve", af, af, mcol, ALU.mult, [ra.reg(), MK.reg()], [ra.reg()])
                if d == 0:
                    self.scan(HD0.t[:, :], ra.t[:, :], ib.t[:, :], [ra.reg(), ib.reg()], [HD0.reg()])
                    self.copy("dve", ST.t[:, j, 0, :], HD0.t[:, SL - 1:NT:SL], [HD0.reg()], [ST.reg(j)])
                else:
                    self.scan(HD1t[:, ::-1], ra.t[:, ::-1], ib.t[:, ::-1], [ra.reg(), ib.reg()], [RECP.reg()])
                    self.copy("dve", ST.t[:, j, 1, :], HD1t[:, 0:NT:SL], [RECP.reg()], [ST.reg(j)])
            self.tt("dve", HD0.t[:, :], HD0.t[:, :], HD1t[:, :], ALU.add, [HD0.reg(), RECP.reg()], [HD0.reg()])
            self.tt("dve", M.t[:, j, :], HD0.t[:, :], GB.t[:, :], ALU.mult, [HD0.reg(), GB.reg()], [M.reg(j)])

        for it in range(11):
            self.bg()
            if it >= 1:
                part_a(it - 1)
            if it < 10:
                part_b(it)
            if it == 10:
                part_s(9)
            if it >= 1:
                part_x(it - 1)
                part_c(it - 1)
            if it < 10:
                part_d(it)
        self.store(self.stO[a], self.flat2(ST), [ST.reg(j) for j in range(10)], "ost%d" % a)
        for half in range(2):
            W = self.wload(self.lwout[a, half], [128, 10, 512])
            for cc in range(4):
                c = half * 4 + cc
                bk = [self.ps() for _ in TT]
                for ti, (t0, tn) in enumerate(TT):
                    pt, pr = bk[ti]
                    for jc in range(9):
                        self.mm(pt[:, 0:tn], W.t[:, jc, cc * 128:(cc + 1) * 128], M.t[:, jc, t0:t0 + tn],
                                jc == 0, False, [W.reg(), M.reg(jc)], [pr])
                for ti, (t0, tn) in enumerate(TT):
                    pt, pr = bk[ti]
                    self.mm(pt[:, 0:tn], W.t[:, 9, cc * 128:(cc + 1) * 128], M.t[:, 9, t0:t0 + tn],
                            False, True, [W.reg(), M.reg(9)], [pr])
                    self.resid_add(pt, pr, c, ti, 2, MOD)
            W.closed = True
        ar.pop()

    def ffn(self, l):
        ar = self.arena
        H, MOD, MK = self.H, self.MOD[l % 2], self.MK
        ar.push()
        A = ar.alloc([128, NI, NT], BF16)
        Z = [[ar.alloc([128, NS, 258], F32) for _ in range(2)] for _ in range(2)]
        C2 = [[ar.alloc([128, NT], F32) for _ in range(2)] for _ in range(2)]
        for zz in Z:
            for z in zz:
                self.memset("dve", self.flat2(z), 0.0, [z.reg()])
        mk1 = MK.t[:, 10:14].rearrange("p (s t) -> p s t", t=1)

        def ffn_tail(i_, part):
            C_ = C2[i_ % 2]
            if part in (0, 2):
                self.act(C_[0].t[:, :], C_[0].t[:, :], AF.Silu, [C_[0].reg()], [C_[0].reg()])
            if part in (1, 2):
                self.tt("dve", A.t[:, i_, :], C_[0].t[:, :], C_[1].t[:, :], ALU.mult, [C_[0].reg(), C_[1].reg()], [A.reg(i_)])

        for i in range(NI):
            W = self.wload(self.fwu[l, i], [128, 8, 256])
            if i >= 2:
                self.bg()
            banks = [[self.ps() for _ in TT] for _ in range(2)]
            for half in range(2):
                for kc in range(8):
                    for ti, (t0, tn) in enumerate(TT):
                        pt, pr = banks[half][ti]
                        self.mm(pt[:, 0:tn], W.t[:, kc, half * 128:(half + 1) * 128], H.t[:, kc, t0:t0 + tn],
                                kc == 0, kc == 7, [W.reg(), hregs(H, kc, ti)], [pr])
            W.closed = True
            C = C2[i % 2]
            for half in range(2):
                z = Z[i % 2][half]
                for ti, (t0, tn) in enumerate(TT):
                    pt, pr = banks[half][ti]
                    s0, ns = slots_of(ti)
                    self.act(z.t[:, s0:s0 + ns, 1:257], pt[:, 0:tn].rearrange("p (s t) -> p s t", t=SL),
                             AF.Copy, [pr], [z.reg()])
            if i >= 1:
                ffn_tail(i - 1, 0)
            for half in range(2):
                z = Z[i % 2][half]
                self.tt("dve", z.t[:, 1:5, 0:1], z.t[:, 0:4, 256:257], mk1, ALU.mult, [z.reg(), MK.reg()], [z.reg()])
                self.tt("dve", z.t[:, 0:4, 257:258], z.t[:, 1:5, 1:2], mk1, ALU.mult, [z.reg(), MK.reg()], [z.reg()])
            if i >= 1:
                ffn_tail(i - 1, 1)
            for half in range(2):
                z = Z[i % 2][half]
                cb = C[half]
                n = half * NI + i
                c3 = cb.t[:, :].rearrange("p (s t) -> p s t", t=SL)
                self.act(c3, z.t[:, :, 0:256], AF.Identity, [z.reg(), self.FCW.reg(), self.FCB.reg()], [cb.reg()],
                         bias=self.FCB.t[:, l, n:n + 1], scale=self.FCW.t[:, l, n, 0:1])
            for half in range(2):
                z = Z[i % 2][half]
                cb = C[half]
                n = half * NI + i
                c3 = cb.t[:, :].rearrange("p (s t) -> p s t", t=SL)
                for k in range(1, 3):
                    self.stt(c3, z.t[:, :, k:k + 256], self.FCW.t[:, l, n, k:k + 1], c3, ALU.mult, ALU.add,
                             [z.reg(), self.FCW.reg(), cb.reg()], [cb.reg()])
        ffn_tail(NI - 1, 2)
        for q in range(4):
            W = self.wload(self.fwd[l, q], [128, NI, 256])
            for cc in range(2):
                c = q * 2 + cc
                bk = [self.ps() for _ in TT]
                for ti, (t0, tn) in enumerate(TT):
                    pt, pr = bk[ti]
                    for ic in range(NI - 1):
                        self.mm(pt[:, 0:tn], W.t[:, ic, cc * 128:(cc + 1) * 128], A.t[:, ic, t0:t0 + tn],
                                ic == 0, False, [W.reg(), A.reg(ic)], [pr])
                for ti, (t0, tn) in enumerate(TT):
                    pt, pr = bk[ti]
                    ic = NI - 1
                    self.mm(pt[:, 0:tn], W.t[:, ic, cc * 128:(cc + 1) * 128], A.t[:, ic, t0:t0 + tn],
                            False, True, [W.reg(), A.reg(ic)], [pr])
                    self.resid_add(pt, pr, c, ti, 5, MOD)
            W.closed = True
        ar.pop()

    def cmlp(self, l, j):
        ar = self.arena
        H, MOD, CST = self.H, self.MOD[l % 2], self.CST
        ar.push()
        U = ar.alloc([128, 16, NT], BF16)
        VGs = [ar.alloc([128, 2048], F32) for _ in range(2)]
        VN = [ar.alloc([128, 2048], BF16) for _ in range(2)]
        CNG = ar.alloc([128, 2048], F32)
        WST = ar.alloc([128, 8, 128], BF16)
        CBU = ar.alloc([128, 16], F32)
        CBVH = ar.alloc([1, 2048], BF16)
        CBVL = ar.alloc([1, 2048], BF16)
        CBSH = ar.alloc([1, 1024], BF16)
        CBSL = ar.alloc([1, 1024], BF16)
        JUNK = ar.alloc([128, 512], BF16)
        SS = ar.alloc([128, 16], F32)
        self.load(CNG.t[:, :], self.cng, [CNG.reg()])
        self.load(CBU.t[:, :], self.cbu, [CBU.reg()])
        self.P.dma("pool", "wg", lambda e: e.dma_start(out=WST.t[:, :, :], in_=self.cwsT), (), [WST.reg()])
        stg = [(VGs[0].t[0:1, 0:2048], [VGs[0].reg(c_) for c_ in range(4)], self.cbv, CBVH, CBVL),
               (VGs[1].t[0:1, 0:1024], [VGs[1].reg(c_) for c_ in range(4)], self.cbs, CBSH, CBSL)]
        for sap, sregs, src_, hi_, lo_ in stg:
            self.load(sap, src_, sregs)
            self.act(hi_.t[:, :], sap, AF.Copy, sregs, [hi_.reg()])
            self.tt("dve", lo_.t[:, :], sap, hi_.t[:, :], ALU.subtract, sregs + [hi_.reg()], [lo_.reg()])
        for i in range(8):
            W = self.wload(self.cwu[i], [128, 8, 256])
            banks = [[self.ps() for _ in TT] for _ in range(2)]
            for half in range(2):
                for kc in range(8):
                    for ti, (t0, tn) in enumerate(TT):
                        pt, pr = banks[half][ti]
                        self.mm(pt[:, 0:tn], W.t[:, kc, half * 128:(half + 1) * 128], H.t[:, kc, t0:t0 + tn],
                                kc == 0, kc == 7, [W.reg(), hregs(H, kc, ti)], [pr])
            W.closed = True
            for half in range(2):
                cc = 2 * i + half
                for ti, (t0, tn) in enumerate(TT):
                    pt, pr = banks[half][ti]
                    self.act(U.t[:, cc, t0:t0 + tn], pt[:, 0:tn], AF.Gelu_apprx_tanh, [pr, CBU.reg()],
                             [U.reg(cc, ti)], bias=CBU.t[:, cc:cc + 1])
        self.ring.reset()
        WV = [self.wload(self.cwv[ct], [128, 8, 512]) for ct in range(4)]
        vbanks = {}

        def v_mm(tb):
            ti = min(tb // 4, 2)
            tsl = slice(tb * 128, (tb + 1) * 128)
            vbanks[tb] = []
            for ct in range(4):
                pt, pr = self.ps()
                vbanks[tb].append((pt, pr))
                for kc in range(8):
                    self.mm(pt[:, 0:512], H.t[:, kc, tsl], WV[ct].t[:, kc, :], kc == 0, False,
                            [hregs(H, kc, ti), WV[ct].reg()], [pr])
                self.mm(pt[:, 0:512], self.ONES.t[0:1, :], CBVH.t[0:1, ct * 512:(ct + 1) * 512], False, False,
                        [self.ONES.reg(), CBVH.reg()], [pr])
                self.mm(pt[:, 0:512], self.ONES.t[0:1, :], CBVL.t[0:1, ct * 512:(ct + 1) * 512], False, True,
                        [self.ONES.reg(), CBVL.reg()], [pr])

        def v_act(tb):
            vg = VGs[tb % 2]
            so = 8 * (tb % 2)
            for ct in range(4):
                pt, pr = vbanks[tb][ct]
                self.act(vg.t[:, ct * 512:(ct + 1) * 512], pt[:, 0:512], AF.Gelu_apprx_tanh, [pr], [vg.reg(ct)])
                self.P.op("act", (lambda o, i_, acc: (lambda e: e.activation(out=o, in_=i_, func=AF.Square, accum_out=acc)))(
                    JUNK.t[:, :], vg.t[:, ct * 512:(ct + 1) * 512], SS.t[:, so + ct:so + ct + 1]),
                    [vg.reg(ct)], [JUNK.reg(), SS.reg(so + ct)])

        def n_part(tb):
            vg = VGs[tb % 2]
            so = 8 * (tb % 2)
            self.P.op("dve", (lambda so_: lambda e: e.tensor_reduce(out=SS.t[:, so_ + 4:so_ + 5], in_=SS.t[:, so_:so_ + 4],
                                                                    axis=mybir.AxisListType.X, op=ALU.add))(so),
                      [SS.reg(so + c_) for c_ in range(4)], [SS.reg(so + 4)])
            self.act(SS.t[:, so + 5:so + 6], SS.t[:, so + 4:so + 5], AF.Ln, [SS.reg(so + 4), CST.reg()], [SS.reg(so + 5)],
                     bias=CST.t[:, 0:1], scale=1.0 / D_B)
            self.act(SS.t[:, so + 6:so + 7], SS.t[:, so + 5:so + 6], AF.Exp, [SS.reg(so + 5)], [SS.reg(so + 6)], scale=-0.5)
            vn = VN[tb % 2]
            self.stt(vn.t[:, :], vg.t[:, :], SS.t[:, so + 6:so + 7], CNG.t[:, :], ALU.mult, ALU.mult,
                     [vg.reg(c_) for c_ in range(4)] + [SS.reg(so + 6), CNG.reg()], [vn.reg()])

        def s_part(tb):
            ti = min(tb // 4, 2)
            tsl = slice(tb * 128, (tb + 1) * 128)
            vn = VN[tb % 2]
            for cq in range(4):
                pt, pr = self.ps()
                for c4 in range(4):
                    cc = cq * 4 + c4
                    g = cc // 2
                    o = pt[:, c4 * 128:(c4 + 1) * 128]
                    self.mm(o, vn.t[:, cc * 128:(cc + 1) * 128], WST.t[:, g, :], True, False,
                            [vn.reg(), WST.reg()], [pr])
                    self.mm(o, self.ONES.t[0:1, :], CBSH.t[0:1, g * 128:(g + 1) * 128], False, False,
                            [self.ONES.reg(), CBSH.reg()], [pr])
                    self.mm(o, self.ONES.t[0:1, :], CBSL.t[0:1, g * 128:(g + 1) * 128], False, True,
                            [self.ONES.reg(), CBSL.reg()], [pr])
                uv = U.t[:, cq * 4:(cq + 1) * 4, tsl]
                self.tt("dve", uv, pt[:, 0:512].rearrange("p (c t) -> p c t", t=128), uv, ALU.mult,
                        [pr] + [U.reg(cq * 4 + c_, ti) for c_ in range(4)],
                        [U.reg(cq * 4 + c_, ti) for c_ in range(4)])

        v_mm(0)
        v_act(0)
        for tb in range(10):
            if tb + 1 < 10:
                v_mm(tb + 1)
            n_part(tb)
            if tb + 1 < 10:
                v_act(tb + 1)
            s_part(tb)
        for w in WV:
            w.closed = True
        for half in range(2):
            W = self.wload(self.cwo[half], [128, 16, 512])
            for cc4 in range(4):
                c = half * 4 + cc4
                for ti, (t0, tn) in enumerate(TT):
                    pt, pr = self.ps()
                    for kc in range(16):
                        self.mm(pt[:, 0:tn], W.t[:, kc, cc4 * 128:(cc4 + 1) * 128], U.t[:, kc, t0:t0 + tn],
                                kc == 0, kc == 15, [W.reg(), U.reg(kc, ti)], [pr])
                    self.resid_add(pt, pr, c, ti, 2, MOD)
            W.closed = True
        ar.pop()

    def attn(self, l, j):
        ar = self.arena
        H, MOD, CST = self.H, self.MOD[l % 2], self.CST
        lam_init = 0.8 - 0.6 * math.exp(-0.3 * l)
        ar.push()
        SM = ar.alloc([128, 16], F32)
        ar.push()
        ALAM = ar.alloc([128, 256], F32)
        LP = ar.alloc([128, 128], F32)
        self.load(ALAM.t[:, :], self.alam, [ALAM.reg()])
        self.load(SM.t[:, 0:1], self.asg, [SM.reg(0)])
        self.tt("dve", LP.t[:, 0:64], ALAM.t[:, 0:64], ALAM.t[:, 64:128], ALU.mult, [ALAM.reg()], [LP.reg()])
        self.tt("dve", LP.t[:, 64:128], ALAM.t[:, 128:192], ALAM.t[:, 192:256], ALU.mult, [ALAM.reg()], [LP.reg()])
        self.P.op("dve", lambda e: e.tensor_reduce(out=SM.t[:, 1:3], in_=LP.t[:, :].rearrange("p (a b) -> p a b", b=64),
                                                   axis=mybir.AxisListType.X, op=ALU.add), [LP.reg()], [SM.reg(1)])
        self.act(SM.t[:, 3:5], SM.t[:, 1:3], AF.Exp, [SM.reg(1)], [SM.reg(3)])
        self.tt("dve", SM.t[:, 5:6], SM.t[:, 4:5], SM.t[:, 3:4], ALU.subtract, [SM.reg(3)], [SM.reg(5)])
        self.ts("dve", SM.t[:, 6:7], SM.t[:, 5:6], -lam_init, None, ALU.add, None, [SM.reg(5)], [SM.reg(6)])
        self.ts("dve", SM.t[:, 7:8], SM.t[:, 0:1], 1.0 - lam_init, None, ALU.mult, None, [SM.reg(0)], [SM.reg(7)])
        if not DBG.get("nonest"):
            ar.pop()
        else:
            ar.frames.pop()
        NLAM = SM.t[:, 6:7]
        SG1 = SM.t[:, 7:8]
        QZ = [ar.alloc([128, 4, NT], BF16) for _ in range(2)]
        KT = ar.alloc([128, 4, 512 + NT], BF16)
        VB = ar.alloc([128, 14, 512], BF16)
        AO0 = ar.alloc([128, 4, NT], BF16)
        COS = ar.alloc([128, 512], F32)
        SIN = ar.alloc([128, 512], F32)
        RM = ar.alloc([128, 128], F32)
        BD = ar.alloc([128, 128], BF16)
        SQ2 = [ar.alloc([128, 512], BF16) for _ in range(2)]
        RS2 = [ar.alloc([128, 512], F32) for _ in range(2)]
        QN = [ar.alloc([128, 512], F32) for _ in range(2)]
        T1r = ar.alloc([128, 512], F32)
        T2r = ar.alloc([128, 512], F32)
        VF = ar.alloc([128, 512], F32)
        PT = [ar.alloc([128, 512], BF16) for _ in range(4)]
        RZ = ar.alloc([128, 512], F32)
        T1 = ar.alloc([128, 256], F32)
        T2 = ar.alloc([128, 256], F32)
        OO = [ar.alloc([128, 256], F32) for _ in range(2)]
        OSQ = [ar.alloc([128, 256], BF16) for _ in range(2)]
        RSO = ar.alloc([128, 256], F32)
        AQG = ar.alloc([128, 2], F32)
        ABI = ar.alloc([128, 24], F32)
        for buf, src in [(RM, self.rm), (AQG, self.aqg), (ABI, self.abias)]:
            self.load(buf.t[:, :], src, [buf.reg()])
        for qz in QZ:
            self.memset("dve", self.flat2(qz), 0.0, [qz.reg("z")])
        bde = "pool" if DBG.get("bdpool") else "dve"
        self.memset(bde, BD.t[:, :], 0.0, [BD.reg()])
        self.memset(bde, BD.t[0:64, 0:64], 1.0, [BD.reg()])
        self.memset(bde, BD.t[64:128, 64:128], 1.0, [BD.reg()])
        sc_banks = [0, 1, 2]
        acc_banks = [4, 5, 6, 7]
        for hg in range(2):
            self.ring.reset()
            Wq = [self.wload(self.awqk[2 * hg + i], [128, 8, 256]) for i in range(2)]
            Wk = [self.wload(self.awqk[4 + 2 * hg + i], [128, 8, 256]) for i in range(2)]
            WVh = self.wload(self.awv[hg], [128, 8, 512])
            self.P.dma("pool", "wg", (lambda hg_: lambda e: e.dma_start(out=KT.t[:, :, 0:512], in_=self.ckT[:, 4 * hg_:4 * hg_ + 4, :]))(hg),
                       (), [KT.reg("c")])
            self.P.dma("pool", "wg", (lambda hg_: lambda e: e.dma_start(out=VB.t[:, 0:4, :], in_=self.cv[:, :, 512 * hg_:512 * hg_ + 512]))(hg),
                       (), [VB.reg(kb) for kb in range(4)])
            items = [(ti, isk, hh) for ti in range(3) for isk in range(2) for hh in range(4)]
            st = {}

            def stage_a(n):
                ti, isk, hh = items[n]
                t0, tn = TT[ti]
                if isk == 0 and hh == 0:
                    self.load(COS.t[:, 0:tn], self.cosT[:, t0:t0 + tn], [COS.reg()])
                    self.load(SIN.t[:, 0:tn], self.sinT[:, t0:t0 + tn], [SIN.reg()])
                W = (Wk if isk else Wq)[hh // 2]
                wc = (hh % 2) * 128
                pt, pr = self.ps()
                for kc in range(8):
                    self.mm(pt[:, 0:tn], W.t[:, kc, wc:wc + 128], H.t[:, kc, t0:t0 + tn], kc == 0, kc == 7,
                            [W.reg(), hregs(H, kc, ti)], [pr])
                sq = SQ2[n % 2]
                self.act(sq.t[:, 0:tn], pt[:, 0:tn], AF.Square, [pr], [sq.reg()])
                st[n] = (pt, pr)

            def stage_b(n):
                ti, isk, hh = items[n]
                t0, tn = TT[ti]
                h = 4 * hg + hh
                pt, pr = st[n]
                sq, rs, qn = SQ2[n % 2], RS2[n % 2], QN[n % 2]
                pm, pmr = self.ps()
                self.mm(pm[:, 0:tn], BD.t[:, :], sq.t[:, 0:tn], True, True, [BD.reg(), sq.reg()], [pmr])
                self.act(rs.t[:, 0:tn], pm[:, 0:tn], AF.Ln, [pmr, CST.reg()], [rs.reg()], bias=CST.t[:, 0:1],
                         scale=1.0 / 64)
                self.act(rs.t[:, 0:tn], rs.t[:, 0:tn], AF.Exp, [rs.reg()], [rs.reg()], scale=-0.5)
                self.stt(qn.t[:, 0:tn], pt[:, 0:tn], AQG.t[:, isk:isk + 1], rs.t[:, 0:tn], ALU.mult, ALU.mult,
                         [pr, AQG.reg(), rs.reg()], [qn.reg()])
                if isk:
                    self.store(self.kTo[:, h, t0:t0 + tn], qn.t[:, 0:tn], [qn.reg()], "ok%d" % (n % 4))

            def stage_c(n):
                ti, isk, hh = items[n]
                t0, tn = TT[ti]
                qn = QN[n % 2]
                pq, pqr = self.ps()
                self.mm(pq[:, 0:tn], RM.t[:, :], qn.t[:, 0:tn], True, True, [RM.reg(), qn.reg()], [pqr])
                self.tt("pool" if DBG.get("t1pool") else "dve", T1r.t[:, 0:tn], qn.t[:, 0:tn], COS.t[:, 0:tn], ALU.mult, [qn.reg(), COS.reg()], [T1r.reg()])
                self.tt("dve", T2r.t[:, 0:tn], pq[:, 0:tn], SIN.t[:, 0:tn], ALU.mult, [pqr, SIN.reg()], [T2r.reg()])
                if isk:
                    dst, dreg = KT.t[:, hh, 512 + t0:512 + t0 + tn], KT.reg(hh, ti)
                    self.tt("dve", dst, T1r.t[:, 0:tn], T2r.t[:, 0:tn], ALU.add, [T1r.reg(), T2r.reg()], [dreg])
                else:
                    for c in range(2):
                        ps_ = slice(c * 64, (c + 1) * 64)
                        self.tt("dve", QZ[c].t[ps_, hh, t0:t0 + tn], T1r.t[ps_, 0:tn], T2r.t[ps_, 0:tn], ALU.add,
                                [T1r.reg(), T2r.reg(), QZ[c].reg("z")], [QZ[c].reg(hh, ti)])

            nit = len(items)
            for ti in range(3):
                idx = [n for n in range(nit) if items[n][0] == ti]
                lo, hi = idx[0], idx[-1] + 1
                if DBG.get("seqproj"):
                    for k in range(lo, hi):
                        stage_a(k)
                        stage_b(k)
                        stage_c(k)
                    continue
                for k in range(lo - 1, hi + 1):
                    if lo <= k + 1 < hi:
                        stage_a(k + 1)
                    if lo <= k < hi:
                        stage_b(k)
                    if lo <= k - 1 < hi:
                        stage_c(k - 1)
            for tb in range(10):
                ti = min(tb // 4, 2)
                pt, pr = self.ps()
                for kc in range(8):
                    self.mm(pt[:, 0:512], H.t[:, kc, tb * 128:(tb + 1) * 128], WVh.t[:, kc, :], kc == 0, kc == 7,
                            [hregs(H, kc, ti), WVh.reg()], [pr])
                self.act(VF.t[:, :], pt[:, 0:512], AF.Copy, [pr], [VF.reg()])
                self.copy("dve", VB.t[:, 4 + tb, :], VF.t[:, :], [VF.reg()], [VB.reg(4 + tb)])
                self.store(self.vOo[:, tb, 512 * hg:512 * hg + 512], VF.t[:, :], [VF.reg()], "ov%d" % (tb % 2))
            for w in Wq + Wk + [WVh]:
                w.closed = True
            jobs = []
            for s in range(NS):
                kbs = list(range(12)) if s < 4 else [12, 13]
                for hh in range(4):
                    for ki, kb in enumerate(kbs):
                        jobs.append((s, hh, kb, ki == 0, ki == len(kbs) - 1))
            jst = {}
            acc = {}

            def job_scores(n):
                s, hh, kb, first, last = jobs[n]
                tis = s // 2 if s < 4 else 2
                k0 = kb * 128
                if kb < 4:
                    kreg = KT.reg("c")
                else:
                    kreg = KT.reg(hh, min((kb - 4) // 4, 2))
                pc, pcr = self.ps_pool("sc", sc_banks)
                for c in range(2):
                    self.mm(pc[:, c * 256:(c + 1) * 256], KT.t[:, hh, k0:k0 + 128],
                            QZ[c].t[:, hh, s * SL:(s + 1) * SL], True, True,
                            [kreg, QZ[c].reg(hh, tis), QZ[c].reg("z")], [pcr])
                jst[n] = (pc, pcr)

            def job_rest(n):
                s, hh, kb, first, last = jobs[n]
                pc, pcr = jst.pop(n)
                ptb = PT[n % len(PT)]
                if s < 4:
                    bcol = s * 6 + kb // 2
                    self.act(ptb.t[:, :], pc[:, 0:512], AF.Exp, [pcr, ABI.reg()], [ptb.reg()],
                             bias=ABI.t[:, bcol:bcol + 1], scale=0.125)
                else:
                    self.act(ptb.t[:, :], pc[:, 0:512], AF.Exp, [pcr], [ptb.reg()], scale=0.125)
                if first:
                    acc[(s, hh)] = (self.ps_pool("acc", acc_banks), self.ps_pool("acc", acc_banks))
                (pz, pzr), (po, por) = acc[(s, hh)]
                self.mm(pz[:, 0:512], self.ONES.t[:, :], ptb.t[:, :], first, last, [self.ONES.reg(), ptb.reg()], [pzr])
                self.mm(po[:, 0:512], VB.t[:, kb, hh * 128:(hh + 1) * 128], ptb.t[:, :], first, last,
                        [VB.reg(kb), ptb.reg()], [por])

            fin_i = [0]

            def fin_part1(s, hh):
                (pz, pzr), (po, por) = acc.pop((s, hh))
                k = fin_i[0] % 2
                fin_i[0] += 1
                if DBG.get("recip_act"):
                    self.act(RZ.t[:, :], pz[:, 0:512], AF.Ln, [pzr], [RZ.reg()])
                    self.act(RZ.t[:, :], RZ.t[:, :], AF.Exp, [RZ.reg()], [RZ.reg()], scale=-1.0)
                else:
                    self.P.op("dve", lambda e: e.reciprocal(out=RZ.t[:, :], in_=pz[:, 0:512]), [pzr], [RZ.reg()])
                self.tt("dve", T1.t[:, :], po[:, 0:256], RZ.t[:, 0:256], ALU.mult, [por, RZ.reg()], [T1.reg()])
                self.stt(T2.t[:, :], po[:, 256:512], NLAM, RZ.t[:, 256:512], ALU.mult, ALU.mult,
                         [por, RZ.reg(), SM.reg(6)], [T2.reg()])
                self.tt("dve", OO[k].t[:, :], T1.t[:, :], T2.t[:, :], ALU.add, [T1.reg(), T2.reg()], [OO[k].reg()])
                self.tt("dve", OSQ[k].t[:, :], OO[k].t[:, :], OO[k].t[:, :], ALU.mult, [OO[k].reg()], [OSQ[k].reg()])
                return (s, hh, k)

            def fin_part2(f):
                s, hh, k = f
                tis = s // 2 if s < 4 else 2
                h = 4 * hg + hh
                pm, pmr = self.banks[3]
                self.mm(pm[:, 0:256], self.ONES.t[:, :], OSQ[k].t[:, :], True, True, [self.ONES.reg(), OSQ[k].reg()], [pmr])
                self.act(RSO.t[:, :], pm[:, 0:256], AF.Ln, [pmr, CST.reg()], [RSO.reg()], bias=CST.t[:, 0:1], scale=1.0 / 128)
                self.act(RSO.t[:, :], RSO.t[:, :], AF.Exp, [RSO.reg()], [RSO.reg()], scale=-0.5)
                if hg == 0:
                    dst, dreg = AO0.t[:, hh, s * SL:(s + 1) * SL], AO0.reg(hh, tis)
                else:
                    dst, dreg = H.t[:, h, s * SL:(s + 1) * SL], H.reg(h, s)
                self.stt(dst, OO[k].t[:, :], SG1, RSO.t[:, :], ALU.mult, ALU.mult, [OO[k].reg(), SM.reg(7), RSO.reg()], [dreg])

            nj = len(jobs)
            LA = DBG.get("la", 2)
            if DBG.get("noscore"):
                nj = 0
            pending = []
            for n in range(min(LA, nj)):
                job_scores(n)
            for n in range(nj):
                if n + LA < nj:
                    job_scores(n + LA)
                job_rest(n)
                while pending and pending[0][1] <= n:
                    fin_part2(pending.pop(0)[0])
                if jobs[n][4]:
                    while len(pending) >= 1:
                        fin_part2(pending.pop(0)[0])
                    pending.append((fin_part1(jobs[n][0], jobs[n][1]), n + 9))
            while pending:
                fin_part2(pending.pop(0)[0])
        for half in range(2):
            W = self.wload(self.awo[half], [128, 8, 512])
            for cc4 in range(4):
                c = half * 4 + cc4
                for ti, (t0, tn) in enumerate(TT):
                    pt, pr = self.ps()
                    for kc in range(8):
                        if kc < 4:
                            rhs, rreg = AO0.t[:, kc, t0:t0 + tn], AO0.reg(kc, ti)
                        else:
                            rhs, rreg = H.t[:, kc, t0:t0 + tn], hregs(H, kc, ti)
                        self.mm(pt[:, 0:tn], W.t[:, kc, cc4 * 128:(cc4 + 1) * 128], rhs, kc == 0, kc == 7,
                                [W.reg(), rreg], [pr])
                    self.resid_add(pt, pr, c, ti, 2, MOD)
            W.closed = True
        ar.pop()


def _fm(x_tok):
    T, F = x_tok.shape
    return np.ascontiguousarray(x_tok.reshape(T, F // 128, 128).transpose(2, 1, 0))


def core_slots(core):
    if core < 2:
        return [("s", core, q) for q in range(4)] + [("p", 30 + core, 0)]
    return [("p", 5 * (core - 2) + s, 0) for s in range(NS)]


def prep_shared(inp):
    f = np.float32
    sh = {}
    w_mod = inp["w_mod"]
    sh["wmod"] = np.ascontiguousarray(
        w_mod.reshape(4, 8, 128, 12, 512).transpose(0, 3, 2, 1, 4)).astype(f)
    bm = inp["b_mod"].reshape(4, 48, 128).transpose(2, 0, 1)
    sh["bmodF"] = np.ascontiguousarray(np.repeat(bm[:, :, :, None], NS, axis=3).reshape(128, 4 * 240)).astype(f)
    ng = inp["norm_g"].reshape(4, 2, 8, 128).transpose(3, 0, 1, 2)
    sh["ngr"] = np.ascontiguousarray(np.repeat(ng[:, :, :, :, None], NS, axis=4).reshape(128, 4 * 2 * 40)).astype(f)
    lw = inp["lru_w_in"]
    gbw = lw[:, :, :D_RNN].reshape(2, 8, 128, 10, 128)
    rcw = lw[:, :, D_RNN:].reshape(2, 8, 128, 10, 128)
    both = np.stack([gbw, rcw], axis=4)
    sh["lwin"] = np.ascontiguousarray(both.transpose(0, 3, 2, 1, 4, 5).reshape(2, 10, 128, 8, 256)).astype(f)
    sh["lcw"] = np.ascontiguousarray(inp["lru_conv_w"].reshape(2, 4, 10, 128).transpose(3, 0, 2, 1).reshape(128, 80)).astype(f)
    sh["lcb"] = np.ascontiguousarray(inp["lru_conv_b"].reshape(2, 10, 128).transpose(2, 0, 1).reshape(128, 20)).astype(f)
    sh["lwg"] = np.ascontiguousarray(inp["lru_w_gate"].transpose(0, 4, 3, 1, 2, 5).reshape(2, 128, 40, 128)).astype(f)
    sh["lbg"] = np.ascontiguousarray(inp["lru_b_gate"].transpose(4, 0, 3, 1, 2).reshape(128, 80)).astype(f)
    sh["llam"] = np.ascontiguousarray(inp["lru_lambda"].reshape(2, 2, 10, 128).transpose(3, 0, 1, 2).reshape(128, 40)).astype(f)
    sh["lwout"] = np.ascontiguousarray(inp["lru_w_out"].reshape(2, 10, 128, 2, 512).transpose(0, 3, 2, 1, 4)).astype(f)
    fu = inp["ffn_w_up"]
    g = fu[:, :, :D_FF].reshape(4, 8, 128, NI, 128)
    u = fu[:, :, D_FF:].reshape(4, 8, 128, NI, 128)
    both = np.stack([g, u], axis=4)
    sh["fwu"] = np.ascontiguousarray(both.transpose(0, 3, 2, 1, 4, 5).reshape(4, NI, 128, 8, 256)).astype(f)
    sh["fcw"] = np.ascontiguousarray(inp["ffn_conv_w"].reshape(4, 3, 44, 128).transpose(3, 0, 2, 1).reshape(128, 4 * 44 * 3)).astype(f)
    sh["fcb"] = np.ascontiguousarray(inp["ffn_conv_b"].reshape(4, 44, 128).transpose(2, 0, 1).reshape(128, 4 * 44)).astype(f)
    sh["fwd"] = np.ascontiguousarray(inp["ffn_w_down"].reshape(4, NI, 128, 4, 256).transpose(0, 3, 2, 1, 4)).astype(f)
    cw = inp["cmlp_w_in"][0]
    sh["cwu"] = np.ascontiguousarray(cw[:, :D_B].reshape(8, 128, 8, 256).transpose(2, 1, 0, 3)).astype(f)
    sh["cbu"] = np.ascontiguousarray(inp["cmlp_b_in"][0, :D_B].reshape(16, 128).T).astype(f)
    sh["cwv"] = np.ascontiguousarray(cw[:, D_B:].reshape(8, 128, 4, 512).transpose(2, 1, 0, 3)).astype(f)
    sh["cbv"] = np.ascontiguousarray(inp["cmlp_b_in"][0, D_B:].reshape(1, D_B)).astype(f)
    sh["cng"] = np.ascontiguousarray(np.broadcast_to(inp["cmlp_norm_g"][0][None, :], (128, D_B))).astype(f)
    sh["cwsT"] = np.ascontiguousarray(inp["cmlp_w_s"][0].transpose(2, 0, 1)).astype(f)
    sh["cbs"] = np.ascontiguousarray(inp["cmlp_b_s"][0].reshape(1, 1024)).astype(f)
    sh["cwo"] = np.ascontiguousarray(inp["cmlp_w_out"][0].reshape(16, 128, 2, 512).transpose(2, 1, 0, 3)).astype(f)
    aw = inp["attn_w_qkv"][0]
    sh["awqk"] = np.ascontiguousarray(aw[:, :2048].reshape(8, 128, 8, 256).transpose(2, 1, 0, 3)).astype(f)
    sh["awv"] = np.ascontiguousarray(aw[:, 2048:].reshape(8, 128, 2, 512).transpose(2, 1, 0, 3)).astype(f)
    qg = inp["attn_qk_g"][0]
    sh["aqg"] = np.ascontiguousarray(np.concatenate([qg, qg], axis=1).T).astype(f)
    sh["alam"] = np.ascontiguousarray(np.broadcast_to(inp["attn_lambda"][0].reshape(1, 256), (128, 256))).astype(f)
    sh["asg"] = np.ascontiguousarray(inp["attn_subln_g"][0].reshape(128, 1)).astype(f)
    sh["awo"] = np.ascontiguousarray(inp["attn_w_out"][0].reshape(8, 128, 2, 512).transpose(2, 1, 0, 3)).astype(f)
    rm = np.zeros((128, 128), f)
    for m_ in range(128):
        d_ = m_ % 64
        rm[m_ + 32 if d_ < 32 else m_ - 32, m_] = 1.0
    sh["rm"] = rm
    cst = np.zeros((128, 8), f)
    cst[:, 0] = EPS
    cst[:, 1] = 1.0
    sh["cst"] = cst
    return sh


def prep_core(inp, core):
    f = np.float32
    slots = core_slots(core)
    toks = []
    cond = []
    for (kind, b, q) in slots:
        if kind == "s":
            toks.append(inp["x_sample"][b, q * SL:(q + 1) * SL])
            cond.append(inp["c"][b])
        else:
            toks.append(inp["x_prompt"][b])
            cond.append(inp["c_ctx"])
    x_tok = np.concatenate(toks, axis=0)
    m = {}
    m["xT"] = _fm(x_tok).astype(f)
    cnd = np.stack(cond, axis=0)
    m["condT"] = np.ascontiguousarray(cnd.reshape(NS, 8, 128).transpose(2, 1, 0).reshape(128, 8 * NS)).astype(f)
    mj = np.array([1.0 if (slots[s][0] == "s" and slots[s + 1][0] == "s") else 0.0 for s in range(4)], f)
    mk = np.zeros((128, 32), f)
    mk[:, 1:5] = mj
    mk[:, 5:9] = mj
    mk[:, 10:14] = mj
    mk[:, 14:22] = np.repeat(mj, 2)
    m["mk"] = mk
    h0 = np.zeros((128, 2, 10, 2, NS), f)
    if core < 2:
        st = inp["state_lru"][core]
        for a in range(2):
            h0[:, a, :, 0, 0] = st[a, 0].reshape(10, 128).T
            h0[:, a, :, 1, 3] = st[a, 1].reshape(10, 128).T
    m["h0"] = h0.reshape(128, -1)
    ckT = np.zeros((128, 8, 512), f)
    cv = np.zeros((128, 4, 1024), f)
    cosT = np.ones((128, NT), f)
    sinT = np.zeros((128, NT), f)
    abias = np.zeros((128, 24), f)
    if core < 2:
        ck = inp["cache_k"][core, 0]
        ckT[:] = ck.reshape(512, 8, 128).transpose(2, 1, 0)
        cv[:] = inp["cache_v"][core, 0].reshape(4, 128, 1024).transpose(1, 0, 2)
        T = 4 * SL
        row = (np.arange(T) // 64).astype(f)
        col = (np.arange(T) % 64).astype(f)
        inv = (np.float32(10000.0) ** (-np.arange(16, dtype=f) / np.float32(16))).astype(f)
        ang = np.concatenate([row[:, None] * inv[None, :], col[:, None] * inv[None, :]], axis=1).astype(f)
        cs, sn = np.cos(ang).astype(f), np.sin(ang).astype(f)
        for p in range(128):
            d_ = p % 64
            cosT[p, :T] = cs[:, d_ % 32]
            sinT[p, :T] = -sn[:, d_ % 32] if d_ < 32 else sn[:, d_ % 32]
    else:
        for s_ in range(4):
            for jp in range(6):
                if jp != 2 + s_:
                    abias[:, s_ * 6 + jp] = NEG
    m["ckT"], m["cv"], m["cosT"], m["sinT"], m["abias"] = ckT, cv, cosT, sinT, abias
    return m


_CACHE = {}


def get_program(n_layers):
    if n_layers not in _CACHE:
        b = Builder(n_layers)
        counts = b.build()
        _CACHE[n_layers] = (b, counts)
    return _CACHE[n_layers]


def kernel(**inp):
    inp = {k: np.asarray(v) for k, v in inp.items()}
    b, counts = get_program(N_LAYERS)
    sh = prep_shared(inp)
    in_maps = []
    for core in range(N_CORES):
        m = dict(sh)
        m.update(prep_core(inp, core))
        in_maps.append({k: m[k] for k in b.din})
    res = run_bass_kernel_spmd(b.nc, in_maps, core_ids=list(range(N_CORES)))
    outs = res.results
    B, S = inp["x_prompt"].shape[0], inp["x_prompt"].shape[1]
    y_prompt = np.zeros((B, S, D), np.float32)
    y_sample = np.zeros(inp["x_sample"].shape, np.float32)
    new_lru = np.zeros((B, 2, 2, D_RNN), np.float32)
    new_k = np.zeros((B, 1, S, 8, 2, 64), np.float32)
    new_v = np.zeros((B, 1, S, 8, 128), np.float32)
    for core in range(N_CORES):
        r = outs[core]
        y = r["yT"].transpose(2, 1, 0).reshape(NT, D)
        st = r["st"].reshape(2, 128, 10, 2, NS)
        for s, (kind, bi, q) in enumerate(core_slots(core)):
            if kind == "s":
                y_sample[bi, q * SL:(q + 1) * SL] = y[s * SL:(s + 1) * SL]
            else:
                y_prompt[bi] = y[s * SL:(s + 1) * SL]
                new_lru[bi] = st[:, :, :, :, s].transpose(0, 3, 2, 1).reshape(2, 2, D_RNN)
                if "kT" in r:
                    kk = r["kT"].transpose(2, 1, 0)
                    new_k[bi, 0] = kk[s * SL:(s + 1) * SL].reshape(SL, 8, 2, 64)
                    vv = r["vO"].transpose(1, 0, 2).reshape(NT, 1024)
                    new_v[bi, 0] = vv[s * SL:(s + 1) * SL].reshape(SL, 8, 128)
    return (y_prompt, y_sample, new_lru, new_k, new_v)
```
